# Optimizing a Trainium2 kernel written in Bass

```python
import math
import jax, jax.numpy as jnp
from jax import lax
import numpy as np

D_MODEL = 1024
BATCH = 4
SEQ = 4096
DEPTH = 2

CHUNK = 64
QBLOCK = 128
HEAD_DIM = 64
HEADS_A = 8
HEADS_B = 8
WIDTH_A = HEADS_A * HEAD_DIM
WIDTH_B = HEADS_B * HEAD_DIM
D_MIX = WIDTH_A + WIDTH_B
IDX_HEADS = 8
IDX_DIM = 64
TOPK_MAX = 256
N_BUCKETS = 32
MAX_DISTANCE = 128
ALPHA = (2.0 * DEPTH) ** 0.25
DN_BETA = (8.0 * DEPTH) ** -0.25
LN_EPS = 1e-5

SPLIT_SIZES = (WIDTH_A, WIDTH_A, WIDTH_A, WIDTH_A,
               WIDTH_B, WIDTH_B, WIDTH_B, WIDTH_B,
               IDX_HEADS * IDX_DIM, IDX_DIM, IDX_HEADS)
N_IN = int(sum(SPLIT_SIZES))
SPLIT_POINTS = tuple(int(v) for v in np.cumsum(SPLIT_SIZES)[:-1])
V_COLS = ((2 * WIDTH_A, 3 * WIDTH_A), (4 * WIDTH_A + 2 * WIDTH_B, 4 * WIDTH_A + 3 * WIDTH_B))

kernel_name = "hybrid_stickbreak_dsa_deepnorm"


def layer_norm(h, g, b):
    hf = h.astype(jnp.float32)
    mu = jnp.mean(hf, axis=-1, keepdims=True)
    var = jnp.mean(jnp.square(hf - mu), axis=-1, keepdims=True)
    out = (hf - mu) * lax.rsqrt(var + LN_EPS) * g.astype(jnp.float32) + b.astype(jnp.float32)
    return out.astype(h.dtype)


def t5_bucket(rel):
    half = N_BUCKETS // 2
    max_exact = half // 2
    ret = jnp.where(rel > 0, half, 0)
    n = jnp.abs(rel)
    nf = jnp.maximum(n, 1).astype(jnp.float32)
    large = max_exact + (jnp.log(nf / max_exact) / math.log(MAX_DISTANCE / max_exact)
                         * (half - max_exact)).astype(jnp.int32)
    large = jnp.minimum(large, half - 1)
    return ret + jnp.where(n < max_exact, n, large)


def to_blocks(a):
    b, s = a.shape[:2]
    a = a.reshape((b, s // QBLOCK, QBLOCK) + a.shape[2:])
    return jnp.moveaxis(a, 1, 0)


def from_blocks(a):
    nblk, b = a.shape[:2]
    a = jnp.moveaxis(a, 0, 1)
    return a.reshape(b, nblk * QBLOCK, -1)


def stick_breaking_attention(q, k, v):
    b, s_len, h, d = q.shape
    scale = 1.0 / math.sqrt(d)
    key_pos = jnp.arange(s_len, dtype=jnp.int32)

    def block(args):
        qb, blk = args
        t = blk * QBLOCK + jnp.arange(QBLOCK, dtype=jnp.int32)
        z = jnp.einsum('bqhd,bshd->bhqs', qb, k).astype(jnp.float32) * scale
        strict = key_pos[None, :] < t[:, None]
        log_rest = jnp.where(strict, jax.nn.log_sigmoid(-z), 0.0)
        tail = lax.cumsum(log_rest, axis=3, reverse=True) - log_rest
        a = jnp.where(strict, jnp.exp(jax.nn.log_sigmoid(z) + tail), 0.0)
        return jnp.einsum('bhqs,bshd->bqhd', a.astype(v.dtype), v)

    nblk = s_len // QBLOCK
    out = lax.map(block, (to_blocks(q), jnp.arange(nblk, dtype=jnp.int32)))
    return from_blocks(out)


def dsa_sparse_attention(q, k, v, q_idx, k_idx, w_idx, rel_bias, topk):
    b, s_len, h, d = q.shape
    scale = 1.0 / math.sqrt(d)
    key_pos = jnp.arange(s_len, dtype=jnp.int32)
    gather = jax.vmap(lambda arr, idx: arr[idx])

    def block(args):
        qb, qib, wib, blk = args
        t = blk * QBLOCK + jnp.arange(QBLOCK, dtype=jnp.int32)
        rel = jax.nn.relu(jnp.einsum('bqhd,bsd->bqhs', qib, k_idx).astype(jnp.float32))
        score = jnp.einsum('bqhs,bqh->bqs', rel, wib.astype(jnp.float32))
        chunk_end = (t // CHUNK + 1) * CHUNK
        admissible = key_pos[None, :] < chunk_end[:, None]
        score = jnp.where(admissible[None], score, -jnp.inf)
        vals, idx = lax.top_k(score, topk)
        valid = vals > -jnp.inf
        kg = gather(k, idx)
        vg = gather(v, idx)
        logits = jnp.einsum('bqhd,bqkhd->bqhk', qb, kg).astype(jnp.float32) * scale
        bias = rel_bias[t5_bucket(idx - t[None, :, None])]
        logits = logits + jnp.swapaxes(bias, -1, -2).astype(jnp.float32)
        logits = jnp.where(valid[:, :, None, :], logits, -jnp.inf)
        p = jax.nn.softmax(logits, axis=-1)
        return jnp.einsum('bqhk,bqkhd->bqhd', p.astype(vg.dtype), vg)

    nblk = s_len // QBLOCK
    out = lax.map(block, (to_blocks(q), to_blocks(q_idx), to_blocks(w_idx),
                          jnp.arange(nblk, dtype=jnp.int32)))
    return from_blocks(out)


def hybrid_layer(x, w_in, w_out, ln_g, ln_b, rel_bias, topk):
    b, s_len, _ = x.shape
    proj = jnp.einsum('bsd,dn->bsn', x, w_in)
    qA, kA, vA, gA, qB, kB, vB, gB, qI, kI, wI = jnp.split(proj, SPLIT_POINTS, axis=-1)
    heads = lambda a, hh: a.reshape(b, s_len, hh, HEAD_DIM)
    oA = stick_breaking_attention(heads(qA, HEADS_A), heads(kA, HEADS_A), heads(vA, HEADS_A))
    oB = dsa_sparse_attention(heads(qB, HEADS_B), heads(kB, HEADS_B), heads(vB, HEADS_B),
                              qI.reshape(b, s_len, IDX_HEADS, IDX_DIM), kI, wI,
                              rel_bias, topk)
    mix = jnp.concatenate([oA * jax.nn.silu(gA), oB * jax.nn.silu(gB)], axis=-1)
    y = jnp.einsum('bsm,md->bsd', mix, w_out)
    return layer_norm(ALPHA * x + y, ln_g, ln_b)


def setup_inputs(seed: int = 0) -> dict:
    key = jax.random.key(seed)
    k_x, k_in, k_out, k_g, k_b, k_rb = jax.random.split(key, 6)
    x = jax.random.normal(k_x, (BATCH, SEQ, D_MODEL), jnp.float32)
    col_scale = np.ones((N_IN,), np.float32)
    for lo, hi in V_COLS:
        col_scale[lo:hi] = DN_BETA
    w_in = (jax.random.normal(k_in, (DEPTH, D_MODEL, N_IN), jnp.float32)
            * (D_MODEL ** -0.5) * jnp.asarray(col_scale))
    w_out = jax.random.normal(k_out, (DEPTH, D_MIX, D_MODEL), jnp.float32) * (D_MIX ** -0.5) * DN_BETA
    ln_g = 1.0 + 0.02 * jax.random.normal(k_g, (DEPTH, D_MODEL), jnp.float32)
    ln_b = 0.02 * jax.random.normal(k_b, (DEPTH, D_MODEL), jnp.float32)
    rel_bias = 0.5 * jax.random.normal(k_rb, (N_BUCKETS, HEADS_B), jnp.float32)
    return {"x": x, "w_in": w_in, "w_out": w_out, "ln_g": ln_g, "ln_b": ln_b,
            "rel_bias": rel_bias}


def reference(x, w_in, w_out, ln_g, ln_b, rel_bias):
    seq_len = x.shape[1]
    topk = min(TOPK_MAX, seq_len // 4)
    h = x
    for layer in range(DEPTH):
        h = hybrid_layer(h, w_in[layer], w_out[layer], ln_g[layer], ln_b[layer], rel_bias, topk)
    return h
```

```python
import math
import numpy as np
import ml_dtypes
import concourse.bass as bass
import concourse.mybir as mybir
from concourse.bass_utils import run_bass_kernel_spmd

F32 = mybir.dt.float32
BF16 = mybir.dt.bfloat16
U8 = mybir.dt.uint8
AF = mybir.ActivationFunctionType
ALU = mybir.AluOpType
AX = mybir.AxisListType

S = 4096
D = 1024
NIN = 4680
NL = 2
QA, KA, VA, GA, QB, KB, VB, GB, QI, KI, WI = 0, 512, 1024, 1536, 2048, 2560, 3072, 3584, 4096, 4608, 4672
ALPHA = (2.0 * NL) ** 0.25
LN_EPS = 1e-5
TOPK = 256
NIT = 20
F_QA, F_KA, F_SGA, F_QB, F_KB, F_SGB, F_QI, F_KI = 0, 4, 8, 12, 16, 20, 24, 28
NFCH = 29
FCH = ([(QA + 128 * i, "q") for i in range(4)] + [(KA + 128 * i, "c") for i in range(4)]
       + [(GA + 128 * i, "g") for i in range(4)] + [(QB + 128 * i, "q") for i in range(4)]
       + [(KB + 128 * i, "c") for i in range(4)] + [(GB + 128 * i, "g") for i in range(4)]
       + [(QI + 128 * i, "c") for i in range(4)] + [(-1, "c")])
C_ID4, C_NEGCC, C_POW2, C_ONES, C_RB15 = 0, 512, 640, 672, 800
NC32 = 808
B_NUT, B_NEG1, B_ONES, B_IDB, B_MW = 0, 128, 256, 384, 512
NCBF = 512 + 896


class Sched:
    def __init__(self):
        self.ops = []
        self.last_w = {}
        self.readers = {}
        self.bar_idx = None

    def add(self, eng, fn, reads=(), writes=(), dma=False):
        idx = len(self.ops)
        deps = set()
        if self.bar_idx is not None:
            deps.add(self.bar_idx)
        for k in reads:
            if k in self.last_w:
                deps.add(self.last_w[k])
        for k in writes:
            if k in self.last_w:
                deps.add(self.last_w[k])
            rd = self.readers.get(k)
            if rd:
                deps.update(rd[0].values())
                deps.update(rd[1])
        self.ops.append(dict(eng=eng, fn=fn, deps=deps, dma=dma, sig=dma))
        for k in reads:
            rd = self.readers.setdefault(k, ({}, []))
            if dma:
                rd[1].append(idx)
            else:
                rd[0][eng] = idx
        for k in writes:
            self.last_w[k] = idx
            self.readers[k] = ({}, [])
        return idx

    def barrier(self):
        keys = list(set(self.last_w.keys()) | set(self.readers.keys()))
        bt = self.bar_tile
        self.bar_idx = self.add("dve", lambda e: e.memset(bt, 0.0), reads=(), writes=keys + ["__bar__"])
        self.last_w = {}
        self.readers = {}

    def emit(self, nc, ndma=8):
        ops = self.ops
        NDMA = {"sp": ndma, "pool": 4, "act": 4}
        for i, o in enumerate(ops):
            nd = set()
            for d in o["deps"]:
                if ops[d]["eng"] == "pe" and o["eng"] == "pe" and not ops[d]["dma"] and not o["dma"]:
                    continue
                nd.add(d)
            o["deps"] = nd
        slot_last = {}
        dcount = {"sp": 0, "pool": 0, "act": 0}
        for i, o in enumerate(ops):
            if o["dma"]:
                q = o["eng"]
                slot = dcount[q] % NDMA[q]
                dcount[q] += 1
                o["slot"] = slot
                prev = slot_last.get((q, slot))
                if prev is not None:
                    o["deps"].add(prev)
                slot_last[(q, slot)] = i
        for o in ops:
            for d in o["deps"]:
                ops[d]["sig"] = True
        csem = {e: nc.alloc_semaphore("sem_" + e) for e in ("pe", "act", "dve", "pool")}
        dsem = {q: [nc.alloc_semaphore("dsem_%s%d" % (q, i)) for i in range(NDMA[q])] for q in NDMA}
        ccount = {e: 0 for e in csem}
        duse = {(q, s): 0 for q in NDMA for s in range(NDMA[q])}
        for o in ops:
            if o["dma"]:
                key = (o["eng"], o["slot"])
                duse[key] += 1
                o["tick"] = (dsem[o["eng"]][o["slot"]], 16 * duse[key])
            elif o["sig"]:
                ccount[o["eng"]] += 1
                o["tick"] = (csem[o["eng"]], ccount[o["eng"]])
        streams = {e: [] for e in ("pe", "act", "dve", "pool", "sp")}
        for i, o in enumerate(ops):
            streams[o["eng"]].append(i)

        def run(engname, eng):
            waited = {}
            for i in streams[engname]:
                o = ops[i]
                for d in sorted(o["deps"]):
                    sem, val = ops[d]["tick"]
                    if waited.get(sem.num, 0) < val:
                        eng.wait_ge(sem, val)
                        waited[sem.num] = val
                ins = o["fn"](eng)
                if o["sig"]:
                    sem, val = o["tick"]
                    ins.then_inc(sem, 16 if o["dma"] else 1)
            return waited

        with nc.Block() as block:
            @block.tensor
            def _(e):
                run("pe", e)

            @block.scalar
            def _(e):
                run("act", e)

            @block.vector
            def _(e):
                run("dve", e)

            @block.gpsimd
            def _(e):
                run("pool", e)

            @block.sync
            def _(e):
                run("sp", e)
                for q in NDMA:
                    for s in range(NDMA[q]):
                        if duse[(q, s)]:
                            e.wait_ge(dsem[q][s], 16 * duse[(q, s)])


class Arena:
    def __init__(self, nc, nbytes):
        self.t = nc.alloc_sbuf_tensor("arena", [128, nbytes], U8)
        self.n = nbytes
        self.off = 0

    def reset(self):
        self.off = 0

    def tile(self, shape, dtype):
        esz = 4 if dtype == F32 else 2
        free = 1
        for s in shape[1:]:
            free *= s
        nb = (free * esz + 63) // 64 * 64
        assert self.off + nb <= self.n, ("arena overflow", self.off, nb)
        v = self.t[:, self.off:self.off + free * esz].bitcast(dtype)
        self.off += nb
        if len(shape) == 3:
            v = v.rearrange("p (a b) -> p a b", b=shape[2])
        return v


def build(nlayers=NL, phases=("proj", "sb", "dsa", "out"), dbg=False, sb_win=None, nit=NIT, qbs=range(8)):
    nc = bass.Bass("TRN2", target_bir_lowering=False)

    def dram(name, shape, dtype, kind="Internal"):
        return nc.dram_tensor(name, shape, dtype, kind=kind).ap()

    x_d = dram("x", [S, D], F32, "ExternalInput")
    win_d = dram("w_in", [NL, D, NIN], F32, "ExternalInput")
    wout_d = dram("w_out", [NL, D, D], F32, "ExternalInput")
    lng_d = dram("ln_g", [NL, D], F32, "ExternalInput")
    lnb_d = dram("ln_b", [NL, D], F32, "ExternalInput")
    tz_d = dram("tz", [8, 5, 128, 512], F32, "ExternalInput")
    c32_d = dram("c32", [128, NC32], F32, "ExternalInput")
    cbf_d = dram("cbf", [128, NCBF], BF16, "ExternalInput")
    out_d = dram("out", [S, D], F32, "ExternalOutput")
    sk = "ExternalOutput" if dbg else "Internal"
    featT_d = dram("featT", [NFCH * 128, S], BF16, sk)
    V_d = dram("Vs", [S, 1024], BF16, sk)
    wI_d = dram("wIs", [S, 8], F32, sk)
    mixT_d = dram("mixT", [1024, S], BF16, sk)
    h1_d = dram("h1", [S, D], F32, "Internal")

    sc = Sched()
    ar = Arena(nc, 156 * 1024)
    sc.bar_tile = nc.alloc_sbuf_tensor("bart", [128, 8], F32)[:, 0:1]
    c32 = nc.alloc_sbuf_tensor("c32s", [128, NC32], F32)
    cbf = nc.alloc_sbuf_tensor("cbfs", [128, NCBF], BF16)
    gam = nc.alloc_sbuf_tensor("gam", [128, D], F32)
    bet = nc.alloc_sbuf_tensor("bet", [128, D], F32)
    PD = [nc.alloc_psum_tensor("pd%d" % i, [128, 2, 512], F32) for i in range(4)]
    banks = [PD[i][:, j, :] for i in range(4) for j in range(2)]
    bkey = [("ps", i) for i in range(8)]

    ident = c32[:, C_ID4:C_ID4 + 128]
    ident4 = c32[:, C_ID4:C_ID4 + 512]
    negcc = c32[:, C_NEGCC:C_NEGCC + 128]
    pow2 = c32[:, C_POW2:C_POW2 + 32]
    ones32 = c32[:, C_ONES:C_ONES + 128]
    rb15 = c32[:, C_RB15:C_RB15 + 8]
    nut = cbf[:, B_NUT:B_NUT + 128]
    neg1 = cbf[:, B_NEG1:B_NEG1 + 128]
    onesb = cbf[:, B_ONES:B_ONES + 128]
    identb = cbf[:, B_IDB:B_IDB + 128]
    mw = cbf[:, B_MW:B_MW + 896]

    def dma(q, out, in_, reads, writes):
        sc.add(q, lambda e, out=out, in_=in_: e.dma_start(out=out, in_=in_), reads=reads, writes=writes, dma=True)

    dma("sp", c32[:, :], c32_d[:, :], [], ["c32"])
    dma("sp", cbf[:, :], cbf_d[:, :], [], ["cbf"])
    CST = ["c32", "cbf"]

    def act(out, in_, func, reads, writes, bias=None, scale=None, accum_out=None):
        kw = {}
        if bias is not None:
            kw["bias"] = bias
        if scale is not None:
            kw["scale"] = scale
        if accum_out is not None:
            kw["accum_out"] = accum_out
        sc.add("act", lambda e: e.activation(out=out, in_=in_, func=func, **kw), reads=reads, writes=writes)

    def mm_group(out, pairs, reads, writes):
        def fn(e):
            ins = None
            n = len(pairs)
            for i, (l, r) in enumerate(pairs):
                ins = e.matmul(out, l, r, start=(i == 0), stop=(i == n - 1))
            return ins
        sc.add("pe", fn, reads=reads, writes=writes)

    def phase_proj(L, src_d, srckey):
        ar.reset()
        wbf = ar.tile([128, 8, NIN], BF16)
        wki2 = ar.tile([128, 8, 128], BF16)
        wst = [ar.tile([128, 2340], F32) for _ in range(2)]
        xin = [ar.tile([128, 4, 1024], F32) for _ in range(2)]
        xT = [ar.tile([128, 8, 512], BF16) for _ in range(2)]
        ost = [ar.tile([128, 512], BF16) for _ in range(4)]
        vst = [ar.tile([128, 1024], BF16) for _ in range(2)]
        wist = [ar.tile([128, 8], F32) for _ in range(2)]
        h = 2340
        for kc in range(8):
            for b in range(2):
                dma("sp", wst[b][:, :], win_d[L, kc * 128:(kc + 1) * 128, b * h:(b + 1) * h], [], [("wst", b)])
                if b == 0:
                    sc.add("dve", lambda e, kc=kc: e.tensor_copy(out=wbf[:, kc, 0:h], in_=wst[0][:, :]),
                           reads=[("wst", 0)], writes=[("wbf", kc, 0)])
                else:
                    sc.add("pool", lambda e, kc=kc: e.tensor_copy(out=wbf[:, kc, h:NIN], in_=wst[1][:, :]),
                           reads=[("wst", 1)], writes=[("wbf", kc, 1)])
                    sc.add("pool", lambda e, kc=kc: e.tensor_copy(out=wki2[:, kc, 0:64], in_=wst[1][:, KI - h:KI - h + 64]),
                           reads=[("wst", 1)], writes=[("wki2", kc, 0)])
                    sc.add("pool", lambda e, kc=kc: e.tensor_copy(out=wki2[:, kc, 64:128], in_=wst[1][:, KI - h:KI - h + 64]),
                           reads=[("wst", 1)], writes=[("wki2", kc, 1)])
        WK = [("wbf", kc, i) for kc in range(8) for i in range(2)] + [("wki2", kc, i) for kc in range(8) for i in range(2)]
        bi = 0
        oi = 0
        for tb in range(8):
            xb = tb % 2
            dma("sp", xin[xb][:, :, :], src_d[tb * 512:(tb + 1) * 512, :].rearrange("(j p) d -> p j d", p=128),
                [srckey], [("xin", xb)])
            for j in range(4):
                for g in range(2):
                    bk = bi % 8
                    bi += 1

                    def fn(e, xb=xb, j=j, g=g, bk=bk):
                        ins = None
                        for i in range(4):
                            kc = g * 4 + i
                            ins = e.transpose(banks[bk][:, i * 128:(i + 1) * 128],
                                              xin[xb][:, j, kc * 128:(kc + 1) * 128], ident)
                        return ins
                    sc.add("pe", fn, reads=[("xin", xb)] + CST, writes=[bkey[bk]])
                    o = xT[xb][:, g * 4:g * 4 + 4, j * 128:(j + 1) * 128]
                    i_ = banks[bk].rearrange("p (a b) -> p a b", b=128)
                    if (j + g) % 2 == 0:
                        sc.add("dve", lambda e, o=o, i_=i_: e.tensor_copy(out=o, in_=i_),
                               reads=[bkey[bk]], writes=[("xT", xb, j, g)])
                    else:
                        act(o, i_, AF.Copy, [bkey[bk]], [("xT", xb, j, g)])
            XT = [("xT", xb, j, g) for j in range(4) for g in range(2)]
            for c in range(NFCH):
                col, kind = FCH[c]
                bk = bi % 8
                bi += 1
                if col >= 0:
                    pairs = [(wbf[:, kc, col:col + 128], xT[xb][:, kc, :]) for kc in range(8)]
                else:
                    pairs = [(wki2[:, kc, :], xT[xb][:, kc, :]) for kc in range(8)]
                mm_group(banks[bk], pairs, XT + WK, [bkey[bk]])
                os_ = oi % 4
                oi += 1
                if kind == "g":
                    act(ost[os_][:, :], banks[bk], AF.Silu, [bkey[bk]], [("ost", os_)])
                elif kind == "q":
                    sc.add("dve", lambda e, os_=os_, bk=bk: e.tensor_scalar(
                        out=ost[os_][:, :], in0=banks[bk], scalar1=0.125, scalar2=None, op0=ALU.mult),
                        reads=[bkey[bk]], writes=[("ost", os_)])
                else:
                    if c % 2 == 0:
                        sc.add("dve", lambda e, os_=os_, bk=bk: e.tensor_copy(out=ost[os_][:, :], in_=banks[bk]),
                               reads=[bkey[bk]], writes=[("ost", os_)])
                    else:
                        act(ost[os_][:, :], banks[bk], AF.Copy, [bkey[bk]], [("ost", os_)])
                dma("pool", featT_d[c * 128:(c + 1) * 128, tb * 512:(tb + 1) * 512], ost[os_][:, :],
                    [("ost", os_)], [("featT", c, tb)])
            for j in range(4):
                vs = (tb * 4 + j) % 2
                for hf in range(2):
                    bk = bi % 8
                    bi += 1
                    c0 = VA if hf == 0 else VB
                    pairs = [(xT[xb][:, kc, j * 128:(j + 1) * 128], wbf[:, kc, c0:c0 + 512]) for kc in range(8)]
                    mm_group(banks[bk], pairs, XT + WK, [bkey[bk]])
                    o = vst[vs][:, hf * 512:(hf + 1) * 512]
                    if hf == 0:
                        sc.add("dve", lambda e, o=o, bk=bk: e.tensor_copy(out=o, in_=banks[bk]),
                               reads=[bkey[bk]], writes=[("vst", vs, hf)])
                    else:
                        act(o, banks[bk], AF.Copy, [bkey[bk]], [("vst", vs, hf)])
                r0 = tb * 512 + j * 128
                dma("pool", V_d[r0:r0 + 128, :], vst[vs][:, :], [("vst", vs, 0), ("vst", vs, 1)], [("V", tb)])
                bk = bi % 8
                bi += 1
                pairs = [(xT[xb][:, kc, j * 128:(j + 1) * 128], wbf[:, kc, WI:WI + 8]) for kc in range(8)]
                mm_group(banks[bk][:, 0:8], pairs, XT + WK, [bkey[bk]])
                sc.add("dve", lambda e, vs=vs, bk=bk: e.tensor_copy(out=wist[vs][:, :], in_=banks[bk][:, 0:8]),
                       reads=[bkey[bk]], writes=[("wist", vs)])
                dma("pool", wI_d[r0:r0 + 128, :], wist[vs][:, :], [("wist", vs)], [("wI", tb)])

    def featkeys(c0, nch, tb_hi):
        return [("featT", c0 + i, t) for i in range(nch) for t in range(tb_hi + 1)]

    def phase_sb(L):
        ar.reset()
        qT = [ar.tile([128, 512], BF16) for _ in range(2)]
        sg = [ar.tile([128, 512], BF16) for _ in range(2)]
        kT = [ar.tile([128, S], BF16) for _ in range(2)]
        vP = [ar.tile([128, 32, 128], BF16) for _ in range(2)]
        ee = [ar.tile([128, 2, 512], F32) for _ in range(2)]
        sp = [ar.tile([128, 2, 512], BF16) for _ in range(2)]
        Sb = ar.tile([128, 2, 512], BF16)
        Aa = [ar.tile([128, 2, 512], BF16) for _ in range(2)]
        mst = [ar.tile([128, 512], BF16) for _ in range(2)]
        it = 0
        tcount = 0
        for q in qbs:
            for p in range(4):
                s = it % 2
                it += 1
                nk = 512 * (q + 1)
                kb_hi = 4 * q + 3
                kb_lo = 0 if sb_win is None else max(0, 4 * q - sb_win)
                k0 = kb_lo * 128
                dma("sp", qT[s][:, :], featT_d[(F_QA + p) * 128:(F_QA + p + 1) * 128, q * 512:(q + 1) * 512],
                    [("featT", F_QA + p, q)], [("qT", s)])
                dma("sp", sg[s][:, :], featT_d[(F_SGA + p) * 128:(F_SGA + p + 1) * 128, q * 512:(q + 1) * 512],
                    [("featT", F_SGA + p, q)], [("sg", s)])
                dma("sp", kT[s][:, k0:nk], featT_d[(F_KA + p) * 128:(F_KA + p + 1) * 128, k0:nk],
                    featkeys(F_KA + p, 1, q), [("kT", s)])
                dma("sp", vP[s][:, kb_lo:kb_hi + 1, :],
                    V_d[k0:nk, p * 128:(p + 1) * 128].rearrange("(kb p) c -> p kb c", p=128),
                    [("V", t) for t in range(q + 1)], [("vP", s)])
                OB = PD[3]
                first = True
                for kb in range(kb_hi, kb_lo - 1, -1):
                    zi = tcount % 2
                    tcount += 1
                    Z = PD[zi]
                    T = PD[2]
                    diag = kb >= 4 * q
                    kk = kb - 4 * q
                    last = kb == kb_lo

                    def fz(e, Z=Z, s=s, kb=kb):
                        ins = None
                        for hh in range(2):
                            pr = slice(64 * hh, 64 * hh + 64)
                            ins = e.matmul(Z[:, hh, :], kT[s][pr, kb * 128:(kb + 1) * 128], qT[s][pr, :],
                                           start=True, stop=True)
                        return ins
                    sc.add("pe", fz, reads=[("kT", s), ("qT", s)], writes=[("pd", zi)])
                    act(ee[zi][:, :, :], Z[:, :, :], AF.Exp, [("pd", zi)], [("ee", zi)])
                    act(sp[zi][:, :, :], ee[zi][:, :, :], AF.Ln, [("ee", zi)], [("sp", zi)], bias=1.0)
                    if diag:
                        msk = mw[:, 384 - 128 * kk:384 - 128 * kk + 512]
                        for hh in range(2):
                            sc.add("dve", lambda e, zi=zi, hh=hh, msk=msk: e.tensor_tensor(
                                out=sp[zi][:, hh, :], in0=sp[zi][:, hh, :], in1=msk, op=ALU.mult),
                                reads=[("sp", zi)] + CST, writes=[("sp", zi)])

                    def ft(e, T=T, s=s, kb=kb, zi=zi, first=first):
                        ins = None
                        for hh in range(2):
                            pr = slice(64 * hh, 64 * hh + 64)
                            e.matmul(T[:, hh, :], nut, sp[zi][:, hh, :], start=True, stop=False)
                            if not first:
                                e.matmul(T[:, hh, :], neg1, Sb[:, hh, :], start=False, stop=False)
                            ins = e.matmul(T[:, hh, :], kT[s][pr, kb * 128:(kb + 1) * 128], qT[s][pr, :],
                                           start=False, stop=True)
                        return ins
                    sc.add("pe", ft, reads=[("sp", zi), ("Sb",), ("kT", s), ("qT", s)] + CST, writes=[("pd", 2)])
                    act(Aa[zi][:, :, :], T[:, :, :], AF.Exp, [("pd", 2)], [("Aa", zi)])
                    if diag:
                        for hh in range(2):
                            sc.add("dve", lambda e, zi=zi, hh=hh, msk=msk: e.tensor_tensor(
                                out=Aa[zi][:, hh, :], in0=Aa[zi][:, hh, :], in1=msk, op=ALU.mult),
                                reads=[("Aa", zi)] + CST, writes=[("Aa", zi)])

                    def fo(e, s=s, kb=kb, zi=zi, first=first, last=last):
                        ins = None
                        for hh in range(2):
                            ins = e.matmul(OB[:, hh, :], vP[s][:, kb, :], Aa[zi][:, hh, :], start=first, stop=last)
                        return ins
                    sc.add("pe", fo, reads=[("vP", s), ("Aa", zi)], writes=[("pd", 3)])
                    if not last:
                        if first:
                            sc.add("dve", lambda e, zi=zi: e.tensor_copy(out=Sb[:, :, :], in_=sp[zi][:, :, :]),
                                   reads=[("sp", zi)], writes=[("Sb",)])
                        else:
                            sc.add("dve", lambda e, zi=zi: e.tensor_tensor(
                                out=Sb[:, :, :], in0=Sb[:, :, :], in1=sp[zi][:, :, :], op=ALU.add),
                                reads=[("sp", zi), ("Sb",)], writes=[("Sb",)])
                    first = False
                for hh in range(2):
                    pr = slice(64 * hh, 64 * hh + 64)
                    sc.add("dve", lambda e, s=s, hh=hh, pr=pr: e.tensor_tensor(
                        out=mst[s][pr, :], in0=OB[pr, hh, :], in1=sg[s][pr, :], op=ALU.mult),
                        reads=[("pd", 3), ("sg", s)], writes=[("mst", s, hh)])
                dma("pool", mixT_d[p * 128:(p + 1) * 128, q * 512:(q + 1) * 512], mst[s][:, :],
                    [("mst", s, 0), ("mst", s, 1)], [("mixT", p, q)])

    def phase_dsa(L):
        ar.reset()
        qIT = ar.tile([128, 4, 512], BF16)
        kI2 = ar.tile([128, S], BF16)
        wIt = ar.tile([128, 4, 8], F32)
        dg = ar.tile([128, 8, 128], BF16)
        Rr = [ar.tile([128, 2, 512], BF16) for _ in range(4)]
        score = ar.tile([128, S], F32)
        junk = ar.tile([128, S], BF16)
        maskT = ar.tile([128, 32, 512], BF16)
        small = ar.tile([128, 64], F32)
        wt = ar.tile([128, 32], F32)
        dthr = ar.tile([128, 512], F32)
        thrB = ar.tile([128, 512], F32)
        qT = [ar.tile([128, 512], BF16) for _ in range(2)]
        sg = [ar.tile([128, 512], BF16) for _ in range(2)]
        kT = [ar.tile([128, S], BF16) for _ in range(2)]
        vP = [ar.tile([128, 32, 128], BF16) for _ in range(2)]
        tzb = [ar.tile([128, 512], F32) for _ in range(4)]
        tmp = [ar.tile([128, 512], F32) for _ in range(2)]
        Ee = [ar.tile([128, 2, 512], BF16) for _ in range(2)]
        Pm = [ar.tile([128, 2, 512], BF16) for _ in range(2)]
        rd = ar.tile([128, 512], F32)
        mo = ar.tile([128, 512], F32)
        mst = [ar.tile([128, 512], BF16) for _ in range(2)]
        mx, mn, w0, lo0, mid, cnt, tt, thr = [small[:, i:i + 1] for i in range(8)]
        it = 0
        tcount = 0
        tzc = 0
        for q in qbs:
            nkq = 512 * (q + 1)
            dma("sp", qIT[:, :, :],
                featT_d[F_QI * 128:(F_QI + 4) * 128, q * 512:(q + 1) * 512].rearrange("(c p) t -> p c t", p=128),
                [("featT", F_QI + i, q) for i in range(4)], [("qIT",)])
            dma("sp", kI2[:, 0:nkq], featT_d[F_KI * 128:(F_KI + 1) * 128, 0:nkq], featkeys(F_KI, 1, q), [("kI2",)])
            dma("sp", wIt[:, :, :], wI_d[q * 512:(q + 1) * 512, :].rearrange("(j p) h -> p j h", p=128),
                [("wI", q)], [("wIt",)])
            for j in range(3):
                sc.add("pool", lambda e, j=j, q=q: e.memset(maskT[:, 4 * q + j + 1:4 * q + 4, j * 128:(j + 1) * 128], 0.0),
                       reads=[], writes=[("maskT", j)])
            for j in range(4):
                n = 512 * q + 128 * (j + 1)
                for h in range(8):
                    sc.add("dve", lambda e, j=j, h=h: e.tensor_scalar(
                        out=dg[:, h, :], in0=identb, scalar1=wIt[:, j, h:h + 1], scalar2=None, op0=ALU.mult),
                        reads=[("wIt",)] + CST, writes=[("dg", h)])
                nch = (n + 511) // 512
                for c in range(nch):
                    wc = min(512, n - 512 * c)
                    ACC = PD[2][:, 0, :]
                    for pp in range(4):
                        zi = tcount % 2
                        tcount += 1
                        X = PD[zi]

                        def fx(e, X=X, pp=pp, j=j, c=c, wc=wc):
                            ins = None
                            for hh in range(2):
                                pr = slice(64 * hh, 64 * hh + 64)
                                ins = e.matmul(X[:, hh, 0:wc], qIT[pr, pp, j * 128:(j + 1) * 128],
                                               kI2[pr, c * 512:c * 512 + wc], start=True, stop=True)
                            return ins
                        sc.add("pe", fx, reads=[("qIT",), ("kI2",)], writes=[("pd", zi)])
                        ri = tcount % 4
                        if pp % 2 == 0:
                            act(Rr[ri][:, :, 0:wc], X[:, :, 0:wc], AF.Relu, [("pd", zi)], [("Rr", ri)])
                        else:
                            sc.add("dve", lambda e, ri=ri, X=X, wc=wc: e.tensor_scalar(
                                out=Rr[ri][:, :, 0:wc], in0=X[:, :, 0:wc], scalar1=0.0, scalar2=None, op0=ALU.max),
                                reads=[("pd", zi)], writes=[("Rr", ri)])

                        def fa(e, pp=pp, ri=ri, wc=wc, ACC=ACC):
                            ins = None
                            for hh in range(2):
                                ins = e.matmul(ACC[:, 0:wc], dg[:, 2 * pp + hh, :], Rr[ri][:, hh, 0:wc],
                                               start=(pp == 0 and hh == 0), stop=(pp == 3 and hh == 1))
                            return ins
                        sc.add("pe", fa, reads=[("Rr", ri), ("dg", 2 * pp), ("dg", 2 * pp + 1)], writes=[("pd", 2)])
                    act(score[:, c * 512:c * 512 + wc], ACC[:, 0:wc], AF.Copy, [("pd", 2)], [("score", c)])
                SCK = [("score", c) for c in range(nch)]
                sc.add("dve", lambda e, n=n: e.tensor_tensor(out=score[:, n - 128:n], in0=score[:, n - 128:n],
                                                             in1=negcc, op=ALU.add),
                       reads=SCK + CST, writes=[("score", nch - 1)])
                if n <= 256:
                    sc.add("dve", lambda e: e.memset(thr, -1.0e29), reads=[], writes=[("thr",)])
                else:
                    sc.add("dve", lambda e, n=n: e.tensor_reduce(out=mx, in_=score[:, 0:n], axis=AX.X, op=ALU.max),
                           reads=SCK, writes=[("mx",)])
                    sc.add("dve", lambda e, n=n: e.tensor_reduce(out=mn, in_=score[:, 0:n - 64], axis=AX.X, op=ALU.min),
                           reads=SCK, writes=[("mn",)])
                    sc.add("dve", lambda e: e.scalar_tensor_tensor(out=w0, in0=mx, scalar=1.0, in1=mn,
                                                                   op0=ALU.add, op1=ALU.subtract),
                           reads=[("mx",), ("mn",)], writes=[("w0",)])
                    sc.add("dve", lambda e: e.tensor_scalar(out=wt[:, 0:nit + 1], in0=pow2[:, 0:nit + 1],
                                                            scalar1=w0, scalar2=None, op0=ALU.mult),
                           reads=[("w0",)] + CST, writes=[("wt",)])
                    sc.add("dve", lambda e: e.scalar_tensor_tensor(out=mid, in0=mn, scalar=-1.0, in1=wt[:, 0:1],
                                                                   op0=ALU.add, op1=ALU.add),
                           reads=[("mn",), ("wt",)], writes=[("mid",)])
                    for k in range(nit):
                        sc.add("dve", lambda e, n=n: e.tensor_scalar(
                            out=junk[:, 0:n], in0=score[:, 0:n], scalar1=mid, scalar2=None,
                            op0=ALU.is_gt, op1=ALU.add, accum_out=cnt),
                            reads=SCK + [("mid",)], writes=[("cnt",), ("junk",)])
                        sc.add("dve", lambda e: e.tensor_scalar(out=tt, in0=cnt, scalar1=float(TOPK) - 0.5, scalar2=0.5,
                                                                op0=ALU.is_ge, op1=ALU.subtract),
                               reads=[("cnt",)], writes=[("tt",)])
                        if k < nit - 1:
                            sc.add("dve", lambda e, k=k: e.scalar_tensor_tensor(
                                out=mid, in0=tt, scalar=wt[:, k:k + 1], in1=mid, op0=ALU.mult, op1=ALU.add),
                                reads=[("tt",), ("wt",), ("mid",)], writes=[("mid",)])
                        else:
                            sc.add("dve", lambda e: e.tensor_scalar(out=tt, in0=tt, scalar1=-0.5, scalar2=None, op0=ALU.add),
                                   reads=[("tt",)], writes=[("tt",)])
                            sc.add("dve", lambda e, k=k: e.scalar_tensor_tensor(
                                out=thr, in0=tt, scalar=wt[:, k:k + 1], in1=mid, op0=ALU.mult, op1=ALU.add),
                                reads=[("tt",), ("wt",), ("mid",)], writes=[("thr",)])
                sc.add("dve", lambda e: e.tensor_scalar(out=dthr[:, :], in0=ident4, scalar1=thr, scalar2=None, op0=ALU.mult),
                       reads=[("thr",)] + CST, writes=[("dthr",)])
                mm_group(PD[3][:, 0, :], [(ones32, dthr[:, :])], [("dthr",)] + CST, [("pd", 3)])
                act(thrB[:, :], PD[3][:, 0, :], AF.Copy, [("pd", 3)], [("thrB",)])
                nkb = n // 128
                for g0 in range(0, nkb, 4):
                    gn = min(4, nkb - g0)
                    TR = PD[3][:, 1, :]

                    def ftr(e, g0=g0, gn=gn, TR=TR):
                        ins = None
                        for i in range(gn):
                            ins = e.transpose(TR[:, i * 128:(i + 1) * 128], score[:, (g0 + i) * 128:(g0 + i + 1) * 128], ident)
                        return ins
                    sc.add("pe", ftr, reads=SCK + CST, writes=[("pd", 3, 1)])
                    sc.add("dve", lambda e, g0=g0, gn=gn, TR=TR, j=j: e.tensor_tensor(
                        out=maskT[:, g0:g0 + gn, j * 128:(j + 1) * 128],
                        in0=TR[:, 0:gn * 128].rearrange("p (a b) -> p a b", b=128),
                        in1=thrB[:, 0:gn * 128].rearrange("p (a b) -> p a b", b=128), op=ALU.is_gt),
                        reads=[("pd", 3, 1), ("thrB",)], writes=[("maskT", j)])
            MK = [("maskT", j) for j in range(4)]
            for p in range(4):
                s = it % 2
                it += 1
                kb_hi = 4 * q + 3
                dma("sp", qT[s][:, :], featT_d[(F_QB + p) * 128:(F_QB + p + 1) * 128, q * 512:(q + 1) * 512],
                    [("featT", F_QB + p, q)], [("qT", s)])
                dma("sp", sg[s][:, :], featT_d[(F_SGB + p) * 128:(F_SGB + p + 1) * 128, q * 512:(q + 1) * 512],
                    [("featT", F_SGB + p, q)], [("sg", s)])
                dma("sp", kT[s][:, 0:nkq], featT_d[(F_KB + p) * 128:(F_KB + p + 1) * 128, 0:nkq],
                    featkeys(F_KB + p, 1, q), [("kT", s)])
                dma("sp", vP[s][:, 0:kb_hi + 1, :],
                    V_d[0:nkq, 512 + p * 128:512 + (p + 1) * 128].rearrange("(kb p) c -> p kb c", p=128),
                    [("V", t) for t in range(q + 1)], [("vP", s)])
                OB = PD[2]
                DN = PD[3]
                for kb in range(0, kb_hi + 1):
                    zi = tcount % 2
                    tcount += 1
                    Z = PD[zi]
                    first = kb == 0
                    last = kb == kb_hi
                    near = kb >= 4 * q - 1
                    di = kb - (4 * q - 1)

                    def fz(e, Z=Z, s=s, kb=kb):
                        ins = None
                        for hh in range(2):
                            pr = slice(64 * hh, 64 * hh + 64)
                            ins = e.matmul(Z[:, hh, :], kT[s][pr, kb * 128:(kb + 1) * 128], qT[s][pr, :],
                                           start=True, stop=True)
                        return ins
                    sc.add("pe", fz, reads=[("kT", s), ("qT", s)], writes=[("pd", zi)])
                    for hh in range(2):
                        h = 2 * p + hh
                        if near:
                            tzi = tzc % 4
                            tzc += 1
                            dma("sp", tzb[tzi][:, :], tz_d[h, di, :, :], [], [("tzb", tzi)])
                            sc.add("dve", lambda e, Z=Z, hh=hh, tzi=tzi: e.tensor_tensor(
                                out=tmp[hh][:, :], in0=Z[:, hh, :], in1=tzb[tzi][:, :], op=ALU.add),
                                reads=[("pd", zi), ("tzb", tzi)], writes=[("tmp", hh)])
                            act(Ee[zi][:, hh, :], tmp[hh][:, :], AF.Exp, [("tmp", hh)], [("Ee", zi, hh)])
                        else:
                            act(Ee[zi][:, hh, :], Z[:, hh, :], AF.Exp, [("pd", zi)] + CST, [("Ee", zi, hh)],
                                bias=rb15[:, h:h + 1])
                        sc.add("dve", lambda e, zi=zi, hh=hh, kb=kb: e.tensor_tensor(
                            out=Pm[zi][:, hh, :], in0=Ee[zi][:, hh, :], in1=maskT[:, kb, :], op=ALU.mult),
                            reads=[("Ee", zi, hh)] + MK, writes=[("Pm", zi, hh)])

                    def fo(e, s=s, kb=kb, zi=zi, first=first, last=last):
                        ins = None
                        for hh in range(2):
                            e.matmul(OB[:, hh, :], vP[s][:, kb, :], Pm[zi][:, hh, :], start=first, stop=last)
                            ins = e.matmul(DN[:, hh, :], onesb, Pm[zi][:, hh, :], start=first, stop=last)
                        return ins
                    sc.add("pe", fo, reads=[("vP", s), ("Pm", zi, 0), ("Pm", zi, 1)] + CST, writes=[("pd", 2), ("pd", 3)])
                for hh in range(2):
                    pr = slice(64 * hh, 64 * hh + 64)
                    sc.add("dve", lambda e, hh=hh, pr=pr: e.reciprocal(out=rd[pr, :], in_=DN[pr, hh, :]),
                           reads=[("pd", 3)], writes=[("rd", hh)])
                    sc.add("dve", lambda e, hh=hh, pr=pr: e.tensor_tensor(out=mo[pr, :], in0=OB[pr, hh, :], in1=rd[pr, :],
                                                                          op=ALU.mult),
                           reads=[("pd", 2), ("rd", hh)], writes=[("mo", hh)])
                    sc.add("dve", lambda e, hh=hh, pr=pr, s=s: e.tensor_tensor(out=mst[s][pr, :], in0=mo[pr, :],
                                                                               in1=sg[s][pr, :], op=ALU.mult),
                           reads=[("mo", hh), ("sg", s)], writes=[("mst", s, hh)])
                dma("pool", mixT_d[512 + p * 128:512 + (p + 1) * 128, q * 512:(q + 1) * 512], mst[s][:, :],
                    [("mst", s, 0), ("mst", s, 1)], [("mixT", 4 + p, q)])

    def phase_out(L, src_d, srckey, dst_d, dstkey):
        ar.reset()
        wo = ar.tile([128, 8, 1024], BF16)
        wst = [ar.tile([128, 1024], F32) for _ in range(2)]
        mxg = [ar.tile([128, 8, 512], BF16) for _ in range(2)]
        xr = [ar.tile([128, 4, 1024], F32) for _ in range(2)]
        rr = [ar.tile([128, 1024], F32) for _ in range(2)]
        jk = ar.tile([128, 1024], F32)
        jk2 = ar.tile([128, 1024], F32)
        xn = [ar.tile([128, 1024], F32) for _ in range(2)]
        oo = [ar.tile([128, 1024], F32) for _ in range(2)]
        st = [ar.tile([128, 16], F32) for _ in range(2)]
        dma("sp", gam[:, :], lng_d[L:L + 1, :].to_broadcast([128, D]), [], [("gam",)])
        dma("sp", bet[:, :], lnb_d[L:L + 1, :].to_broadcast([128, D]), [], [("bet",)])
        for kc in range(8):
            b = kc % 2
            dma("sp", wst[b][:, :], wout_d[L, kc * 128:(kc + 1) * 128, :], [], [("wst", b)])
            sc.add("dve", lambda e, b=b, kc=kc: e.tensor_copy(out=wo[:, kc, :], in_=wst[b][:, :]),
                   reads=[("wst", b)], writes=[("wo", kc)])
        WK = [("wo", kc) for kc in range(8)]
        bi = 0
        ti = 0
        for tb in qbs:
            g = tb % 2
            dma("sp", mxg[g][:, :, :], mixT_d[:, tb * 512:(tb + 1) * 512].rearrange("(c p) t -> p c t", p=128),
                [("mixT", c, tb) for c in range(8)], [("mxg", g)])
            dma("sp", xr[g][:, :, :], src_d[tb * 512:(tb + 1) * 512, :].rearrange("(j p) d -> p j d", p=128),
                [srckey], [("xr", g)])
            for j in range(4):
                r = ti % 2
                ti += 1
                for nh in range(2):
                    bk = bi % 8
                    bi += 1
                    pairs = [(mxg[g][:, mc, j * 128:(j + 1) * 128], wo[:, mc, nh * 512:(nh + 1) * 512]) for mc in range(8)]
                    mm_group(banks[bk], pairs, [("mxg", g)] + WK, [bkey[bk]])
                    sc.add("dve", lambda e, r=r, g=g, j=j, nh=nh, bk=bk: e.scalar_tensor_tensor(
                        out=rr[r][:, nh * 512:(nh + 1) * 512], in0=xr[g][:, j, nh * 512:(nh + 1) * 512],
                        scalar=float(ALPHA), in1=banks[bk], op0=ALU.mult, op1=ALU.add),
                        reads=[("xr", g), bkey[bk]], writes=[("rr", r, nh)])
                RK = [("rr", r, 0), ("rr", r, 1)]
                sm, ssq, mean, msq, var, sd, rstd, nmr = [st[r][:, i:i + 1] for i in range(8)]
                sc.add("dve", lambda e, r=r, sm=sm: e.tensor_scalar(out=jk[:, :], in0=rr[r][:, :], scalar1=1.0, scalar2=None,
                                                                    op0=ALU.mult, op1=ALU.add, accum_out=sm),
                       reads=RK, writes=[("st", r, 0), ("jk",)])
                act(jk2[:, :], rr[r][:, :], AF.Square, RK, [("st", r, 1), ("jk2",)], accum_out=ssq)
                sc.add("dve", lambda e, sm=sm, mean=mean: e.tensor_scalar(out=mean, in0=sm, scalar1=1.0 / D, scalar2=None,
                                                                          op0=ALU.mult),
                       reads=[("st", r, 0)], writes=[("st", r, 2)])
                sc.add("dve", lambda e, mean=mean, msq=msq: e.tensor_tensor(out=msq, in0=mean, in1=mean, op=ALU.mult),
                       reads=[("st", r, 2)], writes=[("st", r, 3)])
                sc.add("dve", lambda e, ssq=ssq, msq=msq, var=var: e.scalar_tensor_tensor(
                    out=var, in0=ssq, scalar=1.0 / D, in1=msq, op0=ALU.mult, op1=ALU.subtract),
                    reads=[("st", r, 1), ("st", r, 3)], writes=[("st", r, 4)])
                sc.add("dve", lambda e, var=var: e.tensor_scalar(out=var, in0=var, scalar1=LN_EPS, scalar2=None, op0=ALU.add),
                       reads=[("st", r, 4)], writes=[("st", r, 4)])
                act(sd, var, AF.Sqrt, [("st", r, 4)], [("st", r, 5)])
                sc.add("dve", lambda e, sd=sd, rstd=rstd: e.reciprocal(out=rstd, in_=sd),
                       reads=[("st", r, 5)], writes=[("st", r, 6)])
                sc.add("dve", lambda e, mean=mean, rstd=rstd, nmr=nmr: e.scalar_tensor_tensor(
                    out=nmr, in0=mean, scalar=-1.0, in1=rstd, op0=ALU.mult, op1=ALU.mult),
                    reads=[("st", r, 2), ("st", r, 6)], writes=[("st", r, 7)])
                act(xn[r][:, :], rr[r][:, :], AF.Identity, RK + [("st", r, 6), ("st", r, 7)], [("xn", r)],
                    bias=nmr, scale=rstd)
                sc.add("pool", lambda e, r=r: e.tensor_tensor(out=oo[r][:, :], in0=xn[r][:, :], in1=gam[:, :], op=ALU.mult),
                       reads=[("xn", r), ("gam",)], writes=[("oo", r)])
                sc.add("dve", lambda e, r=r: e.tensor_tensor(out=oo[r][:, :], in0=oo[r][:, :], in1=bet[:, :], op=ALU.add),
                       reads=[("oo", r), ("bet",)], writes=[("oo", r)])
                r0 = tb * 512 + j * 128
                dma("pool", dst_d[r0:r0 + 128, :], oo[r][:, :], [("oo", r)], [dstkey])

    for L in range(nlayers):
        src_d, srckey = (x_d, ("xsrc",)) if L == 0 else (h1_d, ("h1",))
        dst_d, dstkey = (out_d, ("outd",)) if L == nlayers - 1 else (h1_d, ("h1",))
        if "proj" in phases:
            phase_proj(L, src_d, srckey)
            sc.barrier()
        if "sb" in phases:
            phase_sb(L)
            sc.barrier()
        if "dsa" in phases:
            phase_dsa(L)
            sc.barrier()
        if "out" in phases:
            phase_out(L, src_d, srckey, dst_d, dstkey)
            sc.barrier()
    sc.emit(nc)
    return nc


def t5_bucket_np(rel):
    rel = np.asarray(rel, np.int64)
    half, me = 16, 8
    ret = np.where(rel > 0, half, 0)
    n = np.abs(rel)
    nf = np.maximum(n, 1).astype(np.float32)
    large = me + (np.log(nf / np.float32(me)) / np.float32(math.log(128 / 8)) * np.float32(half - me)).astype(np.int32)
    large = np.minimum(large, half - 1)
    return ret + np.where(n < me, n, large)


def make_consts(rel_bias):
    c32 = np.zeros((128, NC32), np.float32)
    eye = np.eye(128, dtype=np.float32)
    for i in range(4):
        c32[:, C_ID4 + 128 * i:C_ID4 + 128 * (i + 1)] = eye
    ncc = np.zeros((128, 128), np.float32)
    ncc[:64, 64:] = -1.0e30
    c32[:, C_NEGCC:C_NEGCC + 128] = ncc
    c32[:, C_POW2:C_POW2 + 32] = (0.5 ** np.arange(1, 33, dtype=np.float64)).astype(np.float32)[None, :]
    c32[:, C_ONES:C_ONES + 128] = 1.0
    c32[:, C_RB15:C_RB15 + 8] = rel_bias[15, :][None, :]
    cb = np.zeros((128, NCBF), np.float32)
    jj = np.arange(128)[:, None]
    ss = np.arange(128)[None, :]
    cb[:, B_NUT:B_NUT + 128] = np.where(jj >= ss, -1.0, 0.0)
    cb[:, B_NEG1:B_NEG1 + 128] = -1.0
    cb[:, B_ONES:B_ONES + 128] = 1.0
    cb[:, B_IDB:B_IDB + 128] = eye
    cc = np.arange(896)[None, :]
    cb[:, B_MW:B_MW + 896] = np.where((cc - 384) > jj, 1.0, 0.0)
    cbf = cb.astype(ml_dtypes.bfloat16)
    sl = np.arange(128)[:, None]
    tl = np.arange(512)[None, :]
    tz = np.zeros((8, 5, 128, 512), np.float32)
    for di in range(5):
        bidx = t5_bucket_np((di - 1) * 128 + sl - tl)
        tz[:, di] = np.transpose(rel_bias[bidx], (2, 0, 1))
    return c32, cbf, tz


_CACHE = {}


def kernel(x, w_in, w_out, ln_g, ln_b, rel_bias):
    x = np.asarray(x, np.float32)
    w_in = np.ascontiguousarray(np.asarray(w_in, np.float32))
    w_out = np.ascontiguousarray(np.asarray(w_out, np.float32))
    ln_g = np.ascontiguousarray(np.asarray(ln_g, np.float32))
    ln_b = np.ascontiguousarray(np.asarray(ln_b, np.float32))
    rel_bias = np.asarray(rel_bias, np.float32)
    c32, cbf, tz = make_consts(rel_bias)
    if "nc" not in _CACHE:
        _CACHE["nc"] = build()
    nc = _CACHE["nc"]
    in_maps = []
    for c in range(8):
        b = c % 4
        in_maps.append({"x": np.ascontiguousarray(x[b]), "w_in": w_in, "w_out": w_out, "ln_g": ln_g, "ln_b": ln_b,
                        "tz": tz, "c32": c32, "cbf": cbf})
    res = run_bass_kernel_spmd(nc, in_maps, core_ids=list(range(8)))
    out = np.stack([np.asarray(res.results[b]["out"], np.float32) for b in range(4)], axis=0)
    return out
```

```python
import math
import numpy as np
import ml_dtypes
import concourse.bass as bass
import concourse.mybir as mybir
from concourse.bass_utils import run_bass_kernel_spmd

F32 = mybir.dt.float32
BF16 = mybir.dt.bfloat16
U8 = mybir.dt.uint8
AF = mybir.ActivationFunctionType
ALU = mybir.AluOpType
AX = mybir.AxisListType

S = 4096
D = 1024
NIN = 4680
NL = 2
QA, KA, VA, GA, QB, KB, VB, GB, QI, KI, WI = 0, 512, 1024, 1536, 2048, 2560, 3072, 3584, 4096, 4608, 4672
ALPHA = (2.0 * NL) ** 0.25
LN_EPS = 1e-5
TOPK = 256
NIT = 16
F_QA, F_KA, F_SGA, F_QB, F_KB, F_SGB, F_QI, F_KI = 0, 4, 8, 12, 16, 20, 24, 28
NFCH = 29
FCH = ([(QA + 128 * i, "q") for i in range(4)] + [(KA + 128 * i, "c") for i in range(4)]
       + [(GA + 128 * i, "g") for i in range(4)] + [(QB + 128 * i, "q") for i in range(4)]
       + [(KB + 128 * i, "c") for i in range(4)] + [(GB + 128 * i, "g") for i in range(4)]
       + [(QI + 128 * i, "c") for i in range(4)] + [(-1, "c")])
C_ID4, C_NEGCC, C_POW2, C_ONES, C_RB15 = 0, 512, 640, 672, 800
NC32 = 808
B_NUT, B_NEG1, B_ONES, B_IDB, B_MW = 0, 128, 256, 384, 512
NCBF = 512 + 896


class Sched:
    def __init__(self):
        self.ops = []
        self.last_w = {}
        self.readers = {}
        self.bar_idx = None
        self.defer = None

    def begin_defer(self):
        self.defer = []

    def end_defer(self):
        d = self.defer
        self.defer = None
        return d

    def flush(self, lst):
        for a in lst:
            self.add(*a)

    def add(self, eng, fn, reads=(), writes=(), dma=False):
        if self.defer is not None:
            self.defer.append((eng, fn, tuple(reads), tuple(writes), dma))
            return None
        idx = len(self.ops)
        deps = set()
        if self.bar_idx is not None:
            deps.add(self.bar_idx)
        for k in reads:
            if k in self.last_w:
                deps.add(self.last_w[k])
        for k in writes:
            if k in self.last_w:
                deps.add(self.last_w[k])
            rd = self.readers.get(k)
            if rd:
                deps.update(rd[0].values())
                deps.update(rd[1])
        self.ops.append(dict(eng=eng, fn=fn, deps=deps, dma=dma, sig=dma))
        for k in reads:
            rd = self.readers.setdefault(k, ({}, []))
            if dma:
                rd[1].append(idx)
            else:
                rd[0][eng] = idx
        for k in writes:
            self.last_w[k] = idx
            self.readers[k] = ({}, [])
        return idx

    def barrier(self):
        keys = list(set(self.last_w.keys()) | set(self.readers.keys()))
        bt = self.bar_tile
        self.bar_idx = self.add("dve", lambda e: e.memset(bt, 0.0), reads=(), writes=keys + ["__bar__"])
        self.last_w = {}
        self.readers = {}

    def emit(self, nc, ndma=8):
        ops = self.ops
        NDMA = {"sp": ndma, "pool": 4, "act": 4}
        for i, o in enumerate(ops):
            nd = set()
            for d in o["deps"]:
                if ops[d]["eng"] == "pe" and o["eng"] == "pe" and not ops[d]["dma"] and not o["dma"]:
                    continue
                nd.add(d)
            o["deps"] = nd
        slot_last = {}
        dcount = {"sp": 0, "pool": 0, "act": 0}
        for i, o in enumerate(ops):
            if o["dma"]:
                q = o["eng"]
                slot = dcount[q] % NDMA[q]
                dcount[q] += 1
                o["slot"] = slot
                prev = slot_last.get((q, slot))
                if prev is not None:
                    o["deps"].add(prev)
                slot_last[(q, slot)] = i
        for o in ops:
            for d in o["deps"]:
                ops[d]["sig"] = True
        csem = {e: nc.alloc_semaphore("sem_" + e) for e in ("pe", "act", "dve", "pool")}
        dsem = {q: [nc.alloc_semaphore("dsem_%s%d" % (q, i)) for i in range(NDMA[q])] for q in NDMA}
        ccount = {e: 0 for e in csem}
        duse = {(q, s): 0 for q in NDMA for s in range(NDMA[q])}
        for o in ops:
            if o["dma"]:
                key = (o["eng"], o["slot"])
                duse[key] += 1
                o["tick"] = (dsem[o["eng"]][o["slot"]], 16 * duse[key])
            elif o["sig"]:
                ccount[o["eng"]] += 1
                o["tick"] = (csem[o["eng"]], ccount[o["eng"]])
        streams = {e: [] for e in ("pe", "act", "dve", "pool", "sp")}
        for i, o in enumerate(ops):
            streams[o["eng"]].append(i)

        def run(engname, eng):
            waited = {}
            for i in streams[engname]:
                o = ops[i]
                for d in sorted(o["deps"]):
                    sem, val = ops[d]["tick"]
                    if waited.get(sem.num, 0) < val:
                        eng.wait_ge(sem, val)
                        waited[sem.num] = val
                ins = o["fn"](eng)
                if o["sig"]:
                    sem, val = o["tick"]
                    ins.then_inc(sem, 16 if o["dma"] else 1)
            return waited

        with nc.Block() as block:
            @block.tensor
            def _(e):
                run("pe", e)

            @block.scalar
            def _(e):
                run("act", e)

            @block.vector
            def _(e):
                run("dve", e)

            @block.gpsimd
            def _(e):
                run("pool", e)

            @block.sync
            def _(e):
                run("sp", e)
                for q in NDMA:
                    for s in range(NDMA[q]):
                        if duse[(q, s)]:
                            e.wait_ge(dsem[q][s], 16 * duse[(q, s)])


def emit_pipelined(sc, stages, skew=1):
    n = len(stages)
    ns = len(stages[0]) if n else 0
    for step in range(n + (ns - 1) * skew):
        for k in range(ns):
            i = step - k * skew
            if 0 <= i < n:
                sc.flush(stages[i][k])


class Arena:
    def __init__(self, nc, nbytes):
        self.t = nc.alloc_sbuf_tensor("arena", [128, nbytes], U8)
        self.n = nbytes
        self.off = 0

    def reset(self):
        self.off = 0

    def tile(self, shape, dtype):
        esz = 4 if dtype == F32 else (1 if dtype == U8 else 2)
        free = 1
        for s in shape[1:]:
            free *= s
        nb = (free * esz + 63) // 64 * 64
        assert self.off + nb <= self.n, ("arena overflow", self.off, nb)
        v = self.t[:, self.off:self.off + free * esz]
        if dtype != U8:
            v = v.bitcast(dtype)
        self.off += nb
        if len(shape) == 3:
            v = v.rearrange("p (a b) -> p a b", b=shape[2])
        return v


def build(nlayers=NL, phases=("proj", "sb", "dsa", "out"), dbg=False, sb_win=None, nit=NIT, qbs=range(8)):
    nc = bass.Bass("TRN2", target_bir_lowering=False)

    def dram(name, shape, dtype, kind="Internal"):
        return nc.dram_tensor(name, shape, dtype, kind=kind).ap()

    x_d = dram("x", [S, D], F32, "ExternalInput")
    win_d = dram("w_in", [NL, D, NIN], F32, "ExternalInput")
    wout_d = dram("w_out", [NL, D, D], F32, "ExternalInput")
    lng_d = dram("ln_g", [NL, D], F32, "ExternalInput")
    lnb_d = dram("ln_b", [NL, D], F32, "ExternalInput")
    tz_d = dram("tz", [8, 5, 128, 512], F32, "ExternalInput")
    c32_d = dram("c32", [128, NC32], F32, "ExternalInput")
    cbf_d = dram("cbf", [128, NCBF], BF16, "ExternalInput")
    out_d = dram("out", [S, D], F32, "ExternalOutput")
    sk = "ExternalOutput" if dbg else "Internal"
    featT_d = dram("featT", [NFCH * 128, S], BF16, sk)
    V_d = dram("Vs", [S, 1024], BF16, sk)
    wI_d = dram("wIs", [S, 8], F32, sk)
    mixT_d = dram("mixT", [1024, S], BF16, sk)
    h1_d = dram("h1", [S, D], F32, "Internal")

    sc = Sched()
    ar = Arena(nc, 160 * 1024)
    sc.bar_tile = nc.alloc_sbuf_tensor("bart", [128, 8], F32)[:, 0:1]
    c32 = nc.alloc_sbuf_tensor("c32s", [128, NC32], F32)
    cbf = nc.alloc_sbuf_tensor("cbfs", [128, NCBF], BF16)
    gam = nc.alloc_sbuf_tensor("gam", [128, D], F32)
    bet = nc.alloc_sbuf_tensor("bet", [128, D], F32)
    PD = [nc.alloc_psum_tensor("pd%d" % i, [128, 2, 512], F32) for i in range(4)]
    banks = [PD[i][:, j, :] for i in range(4) for j in range(2)]
    bkey = [("ps", i) for i in range(8)]

    ident = c32[:, C_ID4:C_ID4 + 128]
    ident4 = c32[:, C_ID4:C_ID4 + 512]
    negcc = c32[:, C_NEGCC:C_NEGCC + 128]
    pow2 = c32[:, C_POW2:C_POW2 + 32]
    ones32 = c32[:, C_ONES:C_ONES + 128]
    rb15 = c32[:, C_RB15:C_RB15 + 8]
    nut = cbf[:, B_NUT:B_NUT + 128]
    neg1 = cbf[:, B_NEG1:B_NEG1 + 128]
    onesb = cbf[:, B_ONES:B_ONES + 128]
    identb = cbf[:, B_IDB:B_IDB + 128]
    mw = cbf[:, B_MW:B_MW + 896]

    def dma(q, out, in_, reads, writes):
        sc.add(q, lambda e, out=out, in_=in_: e.dma_start(out=out, in_=in_), reads=reads, writes=writes, dma=True)

    dma("sp", c32[:, :], c32_d[:, :], [], ["c32"])
    dma("sp", cbf[:, :], cbf_d[:, :], [], ["cbf"])
    CST = ["c32", "cbf"]

    def act(out, in_, func, reads, writes, bias=None, scale=None, accum_out=None):
        kw = {}
        if bias is not None:
            kw["bias"] = bias
        if scale is not None:
            kw["scale"] = scale
        if accum_out is not None:
            kw["accum_out"] = accum_out
        sc.add("act", lambda e: e.activation(out=out, in_=in_, func=func, **kw), reads=reads, writes=writes)

    def mm_group(out, pairs, reads, writes):
        def fn(e):
            ins = None
            n = len(pairs)
            for i, (l, r) in enumerate(pairs):
                ins = e.matmul(out, l, r, start=(i == 0), stop=(i == n - 1))
            return ins
        sc.add("pe", fn, reads=reads, writes=writes)

    def phase_proj(L, src_d, srckey):
        ar.reset()
        wbf = ar.tile([128, 8, NIN], BF16)
        wki2 = ar.tile([128, 8, 128], BF16)
        wst = [ar.tile([128, 2340], F32) for _ in range(2)]
        xin = [ar.tile([128, 4, 1024], F32) for _ in range(2)]
        xT = [ar.tile([128, 8, 512], BF16) for _ in range(2)]
        ost = [ar.tile([128, 512], BF16) for _ in range(4)]
        vst = [ar.tile([128, 1024], BF16) for _ in range(2)]
        wist = [ar.tile([128, 8], F32) for _ in range(2)]
        h = 2340
        for kc in range(8):
            for b in range(2):
                dma("sp", wst[b][:, :], win_d[L, kc * 128:(kc + 1) * 128, b * h:(b + 1) * h], [], [("wst", b)])
                if b == 0:
                    sc.add("dve", lambda e, kc=kc: e.tensor_copy(out=wbf[:, kc, 0:h], in_=wst[0][:, :]),
                           reads=[("wst", 0)], writes=[("wbf", kc, 0)])
                else:
                    sc.add("pool", lambda e, kc=kc: e.tensor_copy(out=wbf[:, kc, h:NIN], in_=wst[1][:, :]),
                           reads=[("wst", 1)], writes=[("wbf", kc, 1)])
                    sc.add("pool", lambda e, kc=kc: e.tensor_copy(out=wki2[:, kc, 0:64], in_=wst[1][:, KI - h:KI - h + 64]),
                           reads=[("wst", 1)], writes=[("wki2", kc, 0)])
                    sc.add("pool", lambda e, kc=kc: e.tensor_copy(out=wki2[:, kc, 64:128], in_=wst[1][:, KI - h:KI - h + 64]),
                           reads=[("wst", 1)], writes=[("wki2", kc, 1)])
        WK = [("wbf", kc, i) for kc in range(8) for i in range(2)] + [("wki2", kc, i) for kc in range(8) for i in range(2)]
        bi = 0
        oi = 0
        for tb in range(8):
            xb = tb % 2
            dma("sp", xin[xb][:, :, :], src_d[tb * 512:(tb + 1) * 512, :].rearrange("(j p) d -> p j d", p=128),
                [srckey], [("xin", xb)])
            for j in range(4):
                for g in range(2):
                    bk = bi % 8
                    bi += 1

                    def fn(e, xb=xb, j=j, g=g, bk=bk):
                        ins = None
                        for i in range(4):
                            kc = g * 4 + i
                            ins = e.transpose(banks[bk][:, i * 128:(i + 1) * 128],
                                              xin[xb][:, j, kc * 128:(kc + 1) * 128], ident)
                        return ins
                    sc.add("pe", fn, reads=[("xin", xb)] + CST, writes=[bkey[bk]])
                    o = xT[xb][:, g * 4:g * 4 + 4, j * 128:(j + 1) * 128]
                    i_ = banks[bk].rearrange("p (a b) -> p a b", b=128)
                    if (j + g) % 2 == 0:
                        sc.add("dve", lambda e, o=o, i_=i_: e.tensor_copy(out=o, in_=i_),
                               reads=[bkey[bk]], writes=[("xT", xb, j, g)])
                    else:
                        act(o, i_, AF.Copy, [bkey[bk]], [("xT", xb, j, g)])
            XT = [("xT", xb, j, g) for j in range(4) for g in range(2)]
            for c in range(NFCH):
                col, kind = FCH[c]
                bk = bi % 8
                bi += 1
                if col >= 0:
                    pairs = [(wbf[:, kc, col:col + 128], xT[xb][:, kc, :]) for kc in range(8)]
                else:
                    pairs = [(wki2[:, kc, :], xT[xb][:, kc, :]) for kc in range(8)]
                mm_group(banks[bk], pairs, XT + WK, [bkey[bk]])
                os_ = oi % 4
                oi += 1
                if kind == "g":
                    act(ost[os_][:, :], banks[bk], AF.Silu, [bkey[bk]], [("ost", os_)])
                elif kind == "q":
                    sc.add("dve", lambda e, os_=os_, bk=bk: e.tensor_scalar(
                        out=ost[os_][:, :], in0=banks[bk], scalar1=0.125, scalar2=None, op0=ALU.mult),
                        reads=[bkey[bk]], writes=[("ost", os_)])
                else:
                    if c % 2 == 0:
                        sc.add("dve", lambda e, os_=os_, bk=bk: e.tensor_copy(out=ost[os_][:, :], in_=banks[bk]),
                               reads=[bkey[bk]], writes=[("ost", os_)])
                    else:
                        act(ost[os_][:, :], banks[bk], AF.Copy, [bkey[bk]], [("ost", os_)])
                dma("pool", featT_d[c * 128:(c + 1) * 128, tb * 512:(tb + 1) * 512], ost[os_][:, :],
                    [("ost", os_)], [("featT", c, tb)])
            for j in range(4):
                vs = (tb * 4 + j) % 2
                for hf in range(2):
                    bk = bi % 8
                    bi += 1
                    c0 = VA if hf == 0 else VB
                    pairs = [(xT[xb][:, kc, j * 128:(j + 1) * 128], wbf[:, kc, c0:c0 + 512]) for kc in range(8)]
                    mm_group(banks[bk], pairs, XT + WK, [bkey[bk]])
                    o = vst[vs][:, hf * 512:(hf + 1) * 512]
                    if hf == 0:
                        sc.add("dve", lambda e, o=o, bk=bk: e.tensor_copy(out=o, in_=banks[bk]),
                               reads=[bkey[bk]], writes=[("vst", vs, hf)])
                    else:
                        act(o, banks[bk], AF.Copy, [bkey[bk]], [("vst", vs, hf)])
                r0 = tb * 512 + j * 128
                dma("pool", V_d[r0:r0 + 128, :], vst[vs][:, :], [("vst", vs, 0), ("vst", vs, 1)], [("V", tb)])
                bk = bi % 8
                bi += 1
                pairs = [(xT[xb][:, kc, j * 128:(j + 1) * 128], wbf[:, kc, WI:WI + 8]) for kc in range(8)]
                mm_group(banks[bk][:, 0:8], pairs, XT + WK, [bkey[bk]])
                sc.add("dve", lambda e, vs=vs, bk=bk: e.tensor_copy(out=wist[vs][:, :], in_=banks[bk][:, 0:8]),
                       reads=[bkey[bk]], writes=[("wist", vs)])
                dma("pool", wI_d[r0:r0 + 128, :], wist[vs][:, :], [("wist", vs)], [("wI", tb)])

    def featkeys(c0, nch, tb_hi):
        return [("featT", c0 + i, t) for i in range(nch) for t in range(tb_hi + 1)]

    def phase_sb(L):
        ar.reset()
        qT = [ar.tile([128, 512], BF16) for _ in range(2)]
        sg = [ar.tile([128, 512], BF16) for _ in range(2)]
        kT = [ar.tile([128, S], BF16) for _ in range(2)]
        vP = [ar.tile([128, 32, 128], BF16) for _ in range(2)]
        ee = [ar.tile([128, 2, 512], F32) for _ in range(2)]
        sp = [ar.tile([128, 2, 512], BF16) for _ in range(2)]
        Sb = ar.tile([128, 2, 512], BF16)
        Aa = [ar.tile([128, 2, 512], BF16) for _ in range(2)]
        mst = [ar.tile([128, 512], BF16) for _ in range(2)]
        it = 0
        tcount = 0
        stages = []
        for q in qbs:
            for p in range(4):
                s = it % 2
                it += 1
                nk = 512 * (q + 1)
                sc.begin_defer()
                kb_hi = 4 * q + 3
                kb_lo = 0 if sb_win is None else max(0, 4 * q - sb_win)
                k0 = kb_lo * 128
                dma("sp", qT[s][:, :], featT_d[(F_QA + p) * 128:(F_QA + p + 1) * 128, q * 512:(q + 1) * 512],
                    [("featT", F_QA + p, q)], [("qT", s)])
                dma("sp", sg[s][:, :], featT_d[(F_SGA + p) * 128:(F_SGA + p + 1) * 128, q * 512:(q + 1) * 512],
                    [("featT", F_SGA + p, q)], [("sg", s)])
                dma("sp", kT[s][:, k0:nk], featT_d[(F_KA + p) * 128:(F_KA + p + 1) * 128, k0:nk],
                    featkeys(F_KA + p, 1, q), [("kT", s)])
                dma("sp", vP[s][:, kb_lo:kb_hi + 1, :],
                    V_d[k0:nk, p * 128:(p + 1) * 128].rearrange("(kb p) c -> p kb c", p=128),
                    [("V", t) for t in range(q + 1)], [("vP", s)])
                OB = PD[3]
                first = True
                for kb in range(kb_hi, kb_lo - 1, -1):
                    zi = tcount % 2
                    tcount += 1
                    Z = PD[zi]
                    T = PD[2]
                    diag = kb >= 4 * q
                    kk = kb - 4 * q
                    last = kb == kb_lo
                    if not first:
                        sc.begin_defer()

                    def fz(e, Z=Z, s=s, kb=kb):
                        ins = None
                        for hh in range(2):
                            pr = slice(64 * hh, 64 * hh + 64)
                            ins = e.matmul(Z[:, hh, :], kT[s][pr, kb * 128:(kb + 1) * 128], qT[s][pr, :],
                                           start=True, stop=True)
                        return ins
                    sc.add("pe", fz, reads=[("kT", s), ("qT", s)], writes=[("pd", zi)])
                    act(ee[zi][:, :, :], Z[:, :, :], AF.Exp, [("pd", zi)], [("ee", zi)])
                    act(sp[zi][:, :, :], ee[zi][:, :, :], AF.Ln, [("ee", zi)], [("sp", zi)], bias=1.0)
                    if diag:
                        msk = mw[:, 384 - 128 * kk:384 - 128 * kk + 512]
                        for hh in range(2):
                            sc.add("dve", lambda e, zi=zi, hh=hh, msk=msk: e.tensor_tensor(
                                out=sp[zi][:, hh, :], in0=sp[zi][:, hh, :], in1=msk, op=ALU.mult),
                                reads=[("sp", zi)] + CST, writes=[("sp", zi)])

                    st1 = sc.end_defer()
                    sc.begin_defer()

                    def ft(e, T=T, s=s, kb=kb, zi=zi, first=first):
                        ins = None
                        for hh in range(2):
                            pr = slice(64 * hh, 64 * hh + 64)
                            e.matmul(T[:, hh, :], nut, sp[zi][:, hh, :], start=True, stop=False)
                            if not first:
                                e.matmul(T[:, hh, :], neg1, Sb[:, hh, :], start=False, stop=False)
                            ins = e.matmul(T[:, hh, :], kT[s][pr, kb * 128:(kb + 1) * 128], qT[s][pr, :],
                                           start=False, stop=True)
                        return ins
                    sc.add("pe", ft, reads=[("sp", zi), ("Sb",), ("kT", s), ("qT", s)] + CST, writes=[("pd", 2)])
                    act(Aa[zi][:, :, :], T[:, :, :], AF.Exp, [("pd", 2)], [("Aa", zi)])
                    if diag:
                        for hh in range(2):
                            sc.add("dve", lambda e, zi=zi, hh=hh, msk=msk: e.tensor_tensor(
                                out=Aa[zi][:, hh, :], in0=Aa[zi][:, hh, :], in1=msk, op=ALU.mult),
                                reads=[("Aa", zi)] + CST, writes=[("Aa", zi)])

                    if not last:
                        if first:
                            sc.add("dve", lambda e, zi=zi: e.tensor_copy(out=Sb[:, :, :], in_=sp[zi][:, :, :]),
                                   reads=[("sp", zi)], writes=[("Sb",)])
                        else:
                            sc.add("dve", lambda e, zi=zi: e.tensor_tensor(
                                out=Sb[:, :, :], in0=Sb[:, :, :], in1=sp[zi][:, :, :], op=ALU.add),
                                reads=[("sp", zi), ("Sb",)], writes=[("Sb",)])
                    st2 = sc.end_defer()
                    sc.begin_defer()

                    def fo(e, s=s, kb=kb, zi=zi, first=first, last=last):
                        ins = None
                        for hh in range(2):
                            ins = e.matmul(OB[:, hh, :], vP[s][:, kb, :], Aa[zi][:, hh, :], start=first, stop=last)
                        return ins
                    sc.add("pe", fo, reads=[("vP", s), ("Aa", zi)], writes=[("pd", 3)])
                    first = False
                    if last:
                        for hh in range(2):
                            pr = slice(64 * hh, 64 * hh + 64)
                            sc.add("dve", lambda e, s=s, hh=hh, pr=pr: e.tensor_tensor(
                                out=mst[s][pr, :], in0=OB[pr, hh, :], in1=sg[s][pr, :], op=ALU.mult),
                                reads=[("pd", 3), ("sg", s)], writes=[("mst", s, hh)])
                        dma("pool", mixT_d[p * 128:(p + 1) * 128, q * 512:(q + 1) * 512], mst[s][:, :],
                            [("mst", s, 0), ("mst", s, 1)], [("mixT", p, q)])
                    st3 = sc.end_defer()
                    stages.append((st1, st2, st3))
        emit_pipelined(sc, stages)

    def phase_dsa(L):
        ar.reset()
        qIT = ar.tile([128, 4, 512], BF16)
        kI2 = ar.tile([128, S], BF16)
        wIt = ar.tile([128, 4, 8], F32)
        dg = ar.tile([128, 8, 128], BF16)
        Rr = [ar.tile([128, 2, 512], BF16) for _ in range(4)]
        score = ar.tile([128, S], F32)
        junk = ar.tile([128, S], U8)
        junk2 = ar.tile([128, 2304], BF16)
        maskT = ar.tile([128, 32, 512], BF16)
        small = ar.tile([128, 64], F32)
        wt = ar.tile([128, 32], F32)
        dthr = ar.tile([128, 512], F32)
        thrB = ar.tile([128, 512], F32)
        qT = [ar.tile([128, 512], BF16) for _ in range(2)]
        sg = [ar.tile([128, 512], BF16) for _ in range(2)]
        kT = [ar.tile([128, S], BF16) for _ in range(2)]
        vP = [ar.tile([128, 32, 128], BF16) for _ in range(2)]
        tzb = [ar.tile([128, 512], F32) for _ in range(4)]
        tmp = [ar.tile([128, 512], F32) for _ in range(2)]
        Ee = [ar.tile([128, 2, 512], BF16) for _ in range(3)]
        Pm = [ar.tile([128, 2, 512], BF16) for _ in range(3)]
        rd = ar.tile([128, 512], F32)
        mo = ar.tile([128, 512], F32)
        mst = [ar.tile([128, 512], BF16) for _ in range(2)]
        mx, mn, w0, lo0, mid, cnt, tt, thr, sgn, uu = [small[:, i:i + 1] for i in range(10)]
        it = 0
        tcount = 0
        tzc = 0
        for q in qbs:
            nkq = 512 * (q + 1)
            dma("sp", qIT[:, :, :],
                featT_d[F_QI * 128:(F_QI + 4) * 128, q * 512:(q + 1) * 512].rearrange("(c p) t -> p c t", p=128),
                [("featT", F_QI + i, q) for i in range(4)], [("qIT",)])
            dma("sp", kI2[:, 0:nkq], featT_d[F_KI * 128:(F_KI + 1) * 128, 0:nkq], featkeys(F_KI, 1, q), [("kI2",)])
            dma("sp", wIt[:, :, :], wI_d[q * 512:(q + 1) * 512, :].rearrange("(j p) h -> p j h", p=128),
                [("wI", q)], [("wIt",)])
            for j in range(3):
                sc.add("pool", lambda e, j=j, q=q: e.memset(maskT[:, 4 * q + j + 1:4 * q + 4, j * 128:(j + 1) * 128], 0.0),
                       reads=[], writes=[("maskT", j)])
            for j in range(4):
                n = 512 * q + 128 * (j + 1)
                for h in range(8):
                    sc.add("pool" if h % 2 else "dve", lambda e, j=j, h=h: e.tensor_scalar(
                        out=dg[:, h, :], in0=identb, scalar1=wIt[:, j, h:h + 1], scalar2=1.0, op0=ALU.mult, op1=ALU.mult),
                        reads=[("wIt",)] + CST, writes=[("dg", h)])
                nch = (n + 511) // 512
                stages = []
                for c in range(nch):
                    wc = min(512, n - 512 * c)
                    ACC = PD[2][:, 0, :]
                    for pp in range(4):
                        zi = tcount % 2
                        tcount += 1
                        X = PD[zi]
                        sc.begin_defer()

                        def fx(e, X=X, pp=pp, j=j, c=c, wc=wc):
                            ins = None
                            for hh in range(2):
                                pr = slice(64 * hh, 64 * hh + 64)
                                ins = e.matmul(X[:, hh, 0:wc], qIT[pr, pp, j * 128:(j + 1) * 128],
                                               kI2[pr, c * 512:c * 512 + wc], start=True, stop=True)
                            return ins
                        sc.add("pe", fx, reads=[("qIT",), ("kI2",)], writes=[("pd", zi)])
                        ri = tcount % 4
                        if pp % 2 == 0:
                            act(Rr[ri][:, :, 0:wc], X[:, :, 0:wc], AF.Relu, [("pd", zi)], [("Rr", ri)])
                        else:
                            sc.add("dve", lambda e, ri=ri, X=X, wc=wc: e.tensor_scalar(
                                out=Rr[ri][:, :, 0:wc], in0=X[:, :, 0:wc], scalar1=0.0, scalar2=None, op0=ALU.max),
                                reads=[("pd", zi)], writes=[("Rr", ri)])
                        st1 = sc.end_defer()
                        sc.begin_defer()

                        def fa(e, pp=pp, ri=ri, wc=wc, ACC=ACC):
                            ins = None
                            for hh in range(2):
                                ins = e.matmul(ACC[:, 0:wc], dg[:, 2 * pp + hh, :], Rr[ri][:, hh, 0:wc],
                                               start=(pp == 0 and hh == 0), stop=(pp == 3 and hh == 1))
                            return ins
                        sc.add("pe", fa, reads=[("Rr", ri), ("dg", 2 * pp), ("dg", 2 * pp + 1)], writes=[("pd", 2)])
                        if pp == 3:
                            act(score[:, c * 512:c * 512 + wc], ACC[:, 0:wc], AF.Copy, [("pd", 2)], [("score", c)])
                        st2 = sc.end_defer()
                        stages.append((st1, st2))
                emit_pipelined(sc, stages, skew=2)
                SCK = [("score", c) for c in range(nch)]
                sc.add("dve", lambda e, n=n: e.tensor_tensor(out=score[:, n - 128:n], in0=score[:, n - 128:n],
                                                             in1=negcc, op=ALU.add),
                       reads=SCK + CST, writes=[("score", nch - 1)])
                if n <= 256:
                    sc.add("dve", lambda e: e.memset(thr, -1.0e29), reads=[], writes=[("thr",)])
                else:
                    n1 = (n * 7 // 16) // 128 * 128
                    nA = n - n1
                    sc.add("dve", lambda e, n=n: e.tensor_reduce(out=mx, in_=score[:, 0:n], axis=AX.X, op=ALU.max),
                           reads=SCK, writes=[("mx",)])
                    sc.add("dve", lambda e, n=n: e.tensor_reduce(out=mn, in_=score[:, 0:n - 64], axis=AX.X, op=ALU.min),
                           reads=SCK, writes=[("mn",)])
                    sc.add("dve", lambda e: e.scalar_tensor_tensor(out=w0, in0=mx, scalar=1.0, in1=mn,
                                                                   op0=ALU.add, op1=ALU.subtract),
                           reads=[("mx",), ("mn",)], writes=[("w0",)])
                    sc.add("dve", lambda e: e.tensor_scalar(out=wt[:, 0:nit + 1], in0=pow2[:, 0:nit + 1],
                                                            scalar1=w0, scalar2=None, op0=ALU.mult),
                           reads=[("w0",)] + CST, writes=[("wt",)])
                    sc.add("dve", lambda e: e.scalar_tensor_tensor(out=mid, in0=mn, scalar=-1.0, in1=wt[:, 0:1],
                                                                   op0=ALU.add, op1=ALU.add),
                           reads=[("mn",), ("wt",)], writes=[("mid",)])
                    for k in range(nit):
                        act(junk2[:, 0:nA], score[:, n1:n], AF.Sign, SCK + [("mid",)], [("sgn",), ("junk2",)],
                            bias=mid, scale=-1.0, accum_out=sgn)
                        sc.add("dve", lambda e, n1=n1: e.tensor_scalar(
                            out=junk[:, 0:n1], in0=score[:, 0:n1], scalar1=mid, scalar2=None,
                            op0=ALU.is_gt, op1=ALU.add, accum_out=cnt),
                            reads=SCK + [("mid",)], writes=[("cnt",), ("junk",)])
                        sc.add("dve", lambda e: e.scalar_tensor_tensor(out=uu, in0=sgn, scalar=-0.5, in1=cnt,
                                                                       op0=ALU.mult, op1=ALU.add),
                               reads=[("sgn",), ("cnt",)], writes=[("uu",)])
                        sc.add("dve", lambda e, nA=nA: e.tensor_scalar(
                            out=tt, in0=uu, scalar1=float(TOPK) - 0.5 - 0.5 * nA, scalar2=0.5,
                            op0=ALU.is_ge, op1=ALU.subtract),
                            reads=[("uu",)], writes=[("tt",)])
                        if k < nit - 1:
                            sc.add("dve", lambda e, k=k: e.scalar_tensor_tensor(
                                out=mid, in0=tt, scalar=wt[:, k:k + 1], in1=mid, op0=ALU.mult, op1=ALU.add),
                                reads=[("tt",), ("wt",), ("mid",)], writes=[("mid",)])
                        else:
                            sc.add("dve", lambda e: e.tensor_scalar(out=tt, in0=tt, scalar1=-0.5, scalar2=None, op0=ALU.add),
                                   reads=[("tt",)], writes=[("tt",)])
                            sc.add("dve", lambda e, k=k: e.scalar_tensor_tensor(
                                out=thr, in0=tt, scalar=wt[:, k:k + 1], in1=mid, op0=ALU.mult, op1=ALU.add),
                                reads=[("tt",), ("wt",), ("mid",)], writes=[("thr",)])
                sc.add("dve", lambda e: e.tensor_scalar(out=dthr[:, :], in0=ident4, scalar1=thr, scalar2=None, op0=ALU.mult),
                       reads=[("thr",)] + CST, writes=[("dthr",)])
                mm_group(PD[3][:, 0, :], [(ones32, dthr[:, :])], [("dthr",)] + CST, [("pd", 3)])
                act(thrB[:, :], PD[3][:, 0, :], AF.Copy, [("pd", 3)], [("thrB",)])
                nkb = n // 128
                for g0 in range(0, nkb, 4):
                    gn = min(4, nkb - g0)
                    TR = PD[3][:, 1, :]

                    def ftr(e, g0=g0, gn=gn, TR=TR):
                        ins = None
                        for i in range(gn):
                            ins = e.transpose(TR[:, i * 128:(i + 1) * 128], score[:, (g0 + i) * 128:(g0 + i + 1) * 128], ident)
                        return ins
                    sc.add("pe", ftr, reads=SCK + CST, writes=[("pd", 3, 1)])
                    sc.add("dve", lambda e, g0=g0, gn=gn, TR=TR, j=j: e.tensor_tensor(
                        out=maskT[:, g0:g0 + gn, j * 128:(j + 1) * 128],
                        in0=TR[:, 0:gn * 128].rearrange("p (a b) -> p a b", b=128),
                        in1=thrB[:, 0:gn * 128].rearrange("p (a b) -> p a b", b=128), op=ALU.is_gt),
                        reads=[("pd", 3, 1), ("thrB",)], writes=[("maskT", j)])
            MK = [("maskT", j) for j in range(4)]
            stages = []
            for p in range(4):
                s = it % 2
                it += 1
                kb_hi = 4 * q + 3
                sc.begin_defer()
                dma("sp", qT[s][:, :], featT_d[(F_QB + p) * 128:(F_QB + p + 1) * 128, q * 512:(q + 1) * 512],
                    [("featT", F_QB + p, q)], [("qT", s)])
                dma("sp", sg[s][:, :], featT_d[(F_SGB + p) * 128:(F_SGB + p + 1) * 128, q * 512:(q + 1) * 512],
                    [("featT", F_SGB + p, q)], [("sg", s)])
                dma("sp", kT[s][:, 0:nkq], featT_d[(F_KB + p) * 128:(F_KB + p + 1) * 128, 0:nkq],
                    featkeys(F_KB + p, 1, q), [("kT", s)])
                dma("sp", vP[s][:, 0:kb_hi + 1, :],
                    V_d[0:nkq, 512 + p * 128:512 + (p + 1) * 128].rearrange("(kb p) c -> p kb c", p=128),
                    [("V", t) for t in range(q + 1)], [("vP", s)])
                OB = PD[2]
                DN = PD[3]
                for kb in range(0, kb_hi + 1):
                    zi = tcount % 2
                    e3 = tcount % 3
                    tcount += 1
                    Z = PD[zi]
                    first = kb == 0
                    last = kb == kb_hi
                    near = kb >= 4 * q - 1
                    di = kb - (4 * q - 1)
                    if not first:
                        sc.begin_defer()

                    def fz(e, Z=Z, s=s, kb=kb):
                        ins = None
                        for hh in range(2):
                            pr = slice(64 * hh, 64 * hh + 64)
                            ins = e.matmul(Z[:, hh, :], kT[s][pr, kb * 128:(kb + 1) * 128], qT[s][pr, :],
                                           start=True, stop=True)
                        return ins
                    sc.add("pe", fz, reads=[("kT", s), ("qT", s)], writes=[("pd", zi)])
                    for hh in range(2):
                        h = 2 * p + hh
                        if near:
                            tzi = tzc % 4
                            tzc += 1
                            dma("sp", tzb[tzi][:, :], tz_d[h, di, :, :], [], [("tzb", tzi)])
                            sc.add("dve", lambda e, Z=Z, hh=hh, tzi=tzi: e.tensor_tensor(
                                out=tmp[hh][:, :], in0=Z[:, hh, :], in1=tzb[tzi][:, :], op=ALU.add),
                                reads=[("pd", zi), ("tzb", tzi)], writes=[("tmp", hh)])
                            act(Ee[e3][:, hh, :], tmp[hh][:, :], AF.Exp, [("tmp", hh)], [("Ee", e3, hh)])
                        else:
                            act(Ee[e3][:, hh, :], Z[:, hh, :], AF.Exp, [("pd", zi)] + CST, [("Ee", e3, hh)],
                                bias=rb15[:, h:h + 1])
                        sc.add("dve", lambda e, e3=e3, hh=hh, kb=kb: e.tensor_tensor(
                            out=Pm[e3][:, hh, :], in0=Ee[e3][:, hh, :], in1=maskT[:, kb, :], op=ALU.mult),
                            reads=[("Ee", e3, hh)] + MK, writes=[("Pm", e3, hh)])
                    st1 = sc.end_defer()
                    sc.begin_defer()

                    def fo(e, s=s, kb=kb, e3=e3, first=first, last=last):
                        ins = None
                        for hh in range(2):
                            e.matmul(OB[:, hh, :], vP[s][:, kb, :], Pm[e3][:, hh, :], start=first, stop=last)
                            ins = e.matmul(DN[:, hh, :], onesb, Pm[e3][:, hh, :], start=first, stop=last)
                        return ins
                    sc.add("pe", fo, reads=[("vP", s), ("Pm", e3, 0), ("Pm", e3, 1)] + CST, writes=[("pd", 2), ("pd", 3)])
                    if last:
                        for hh in range(2):
                            pr = slice(64 * hh, 64 * hh + 64)
                            sc.add("dve", lambda e, hh=hh, pr=pr: e.reciprocal(out=rd[pr, :], in_=DN[pr, hh, :]),
                                   reads=[("pd", 3)], writes=[("rd", hh)])
                            sc.add("dve", lambda e, hh=hh, pr=pr: e.tensor_tensor(out=mo[pr, :], in0=OB[pr, hh, :],
                                                                                  in1=rd[pr, :], op=ALU.mult),
                                   reads=[("pd", 2), ("rd", hh)], writes=[("mo", hh)])
                            sc.add("dve", lambda e, hh=hh, pr=pr, s=s: e.tensor_tensor(out=mst[s][pr, :], in0=mo[pr, :],
                                                                                       in1=sg[s][pr, :], op=ALU.mult),
                                   reads=[("mo", hh), ("sg", s)], writes=[("mst", s, hh)])
                        dma("pool", mixT_d[512 + p * 128:512 + (p + 1) * 128, q * 512:(q + 1) * 512], mst[s][:, :],
                            [("mst", s, 0), ("mst", s, 1)], [("mixT", 4 + p, q)])
                    st2 = sc.end_defer()
                    stages.append((st1, st2))
            emit_pipelined(sc, stages, skew=2)

    def phase_out(L, src_d, srckey, dst_d, dstkey):
        ar.reset()
        wo = ar.tile([128, 8, 1024], BF16)
        wst = [ar.tile([128, 1024], F32) for _ in range(2)]
        mxg = [ar.tile([128, 8, 512], BF16) for _ in range(2)]
        xr = [ar.tile([128, 4, 1024], F32) for _ in range(2)]
        rr = [ar.tile([128, 1024], F32) for _ in range(2)]
        jk = ar.tile([128, 1024], F32)
        jk2 = ar.tile([128, 1024], F32)
        xn = [ar.tile([128, 1024], F32) for _ in range(2)]
        oo = [ar.tile([128, 1024], F32) for _ in range(2)]
        st = [ar.tile([128, 16], F32) for _ in range(2)]
        dma("sp", gam[:, :], lng_d[L:L + 1, :].to_broadcast([128, D]), [], [("gam",)])
        dma("sp", bet[:, :], lnb_d[L:L + 1, :].to_broadcast([128, D]), [], [("bet",)])
        for kc in range(8):
            b = kc % 2
            dma("sp", wst[b][:, :], wout_d[L, kc * 128:(kc + 1) * 128, :], [], [("wst", b)])
            sc.add("dve", lambda e, b=b, kc=kc: e.tensor_copy(out=wo[:, kc, :], in_=wst[b][:, :]),
                   reads=[("wst", b)], writes=[("wo", kc)])
        WK = [("wo", kc) for kc in range(8)]
        bi = 0
        ti = 0
        for tb in qbs:
            g = tb % 2
            dma("sp", mxg[g][:, :, :], mixT_d[:, tb * 512:(tb + 1) * 512].rearrange("(c p) t -> p c t", p=128),
                [("mixT", c, tb) for c in range(8)], [("mxg", g)])
            dma("sp", xr[g][:, :, :], src_d[tb * 512:(tb + 1) * 512, :].rearrange("(j p) d -> p j d", p=128),
                [srckey], [("xr", g)])
            for j in range(4):
                r = ti % 2
                ti += 1
                for nh in range(2):
                    bk = bi % 8
                    bi += 1
                    pairs = [(mxg[g][:, mc, j * 128:(j + 1) * 128], wo[:, mc, nh * 512:(nh + 1) * 512]) for mc in range(8)]
                    mm_group(banks[bk], pairs, [("mxg", g)] + WK, [bkey[bk]])
                    sc.add("dve", lambda e, r=r, g=g, j=j, nh=nh, bk=bk: e.scalar_tensor_tensor(
                        out=rr[r][:, nh * 512:(nh + 1) * 512], in0=xr[g][:, j, nh * 512:(nh + 1) * 512],
                        scalar=float(ALPHA), in1=banks[bk], op0=ALU.mult, op1=ALU.add),
                        reads=[("xr", g), bkey[bk]], writes=[("rr", r, nh)])
                RK = [("rr", r, 0), ("rr", r, 1)]
                sm, ssq, mean, msq, var, sd, rstd, nmr = [st[r][:, i:i + 1] for i in range(8)]
                sc.add("dve", lambda e, r=r, sm=sm: e.tensor_scalar(out=jk[:, :], in0=rr[r][:, :], scalar1=1.0, scalar2=None,
                                                                    op0=ALU.mult, op1=ALU.add, accum_out=sm),
                       reads=RK, writes=[("st", r, 0), ("jk",)])
                act(jk2[:, :], rr[r][:, :], AF.Square, RK, [("st", r, 1), ("jk2",)], accum_out=ssq)
                sc.add("dve", lambda e, sm=sm, mean=mean: e.tensor_scalar(out=mean, in0=sm, scalar1=1.0 / D, scalar2=None,
                                                                          op0=ALU.mult),
                       reads=[("st", r, 0)], writes=[("st", r, 2)])
                sc.add("dve", lambda e, mean=mean, msq=msq: e.tensor_tensor(out=msq, in0=mean, in1=mean, op=ALU.mult),
                       reads=[("st", r, 2)], writes=[("st", r, 3)])
                sc.add("dve", lambda e, ssq=ssq, msq=msq, var=var: e.scalar_tensor_tensor(
                    out=var, in0=ssq, scalar=1.0 / D, in1=msq, op0=ALU.mult, op1=ALU.subtract),
                    reads=[("st", r, 1), ("st", r, 3)], writes=[("st", r, 4)])
                sc.add("dve", lambda e, var=var: e.tensor_scalar(out=var, in0=var, scalar1=LN_EPS, scalar2=None, op0=ALU.add),
                       reads=[("st", r, 4)], writes=[("st", r, 4)])
                act(sd, var, AF.Sqrt, [("st", r, 4)], [("st", r, 5)])
                sc.add("dve", lambda e, sd=sd, rstd=rstd: e.reciprocal(out=rstd, in_=sd),
                       reads=[("st", r, 5)], writes=[("st", r, 6)])
                sc.add("dve", lambda e, mean=mean, rstd=rstd, nmr=nmr: e.scalar_tensor_tensor(
                    out=nmr, in0=mean, scalar=-1.0, in1=rstd, op0=ALU.mult, op1=ALU.mult),
                    reads=[("st", r, 2), ("st", r, 6)], writes=[("st", r, 7)])
                act(xn[r][:, :], rr[r][:, :], AF.Identity, RK + [("st", r, 6), ("st", r, 7)], [("xn", r)],
                    bias=nmr, scale=rstd)
                sc.add("pool", lambda e, r=r: e.tensor_tensor(out=oo[r][:, :], in0=xn[r][:, :], in1=gam[:, :], op=ALU.mult),
                       reads=[("xn", r), ("gam",)], writes=[("oo", r)])
                sc.add("dve", lambda e, r=r: e.tensor_tensor(out=oo[r][:, :], in0=oo[r][:, :], in1=bet[:, :], op=ALU.add),
                       reads=[("oo", r), ("bet",)], writes=[("oo", r)])
                r0 = tb * 512 + j * 128
                dma("pool", dst_d[r0:r0 + 128, :], oo[r][:, :], [("oo", r)], [dstkey])

    for L in range(nlayers):
        src_d, srckey = (x_d, ("xsrc",)) if L == 0 else (h1_d, ("h1",))
        dst_d, dstkey = (out_d, ("outd",)) if L == nlayers - 1 else (h1_d, ("h1",))
        if "proj" in phases:
            phase_proj(L, src_d, srckey)
            sc.barrier()
        if "sb" in phases:
            phase_sb(L)
            sc.barrier()
        if "dsa" in phases:
            phase_dsa(L)
            sc.barrier()
        if "out" in phases:
            phase_out(L, src_d, srckey, dst_d, dstkey)
            sc.barrier()
    sc.emit(nc)
    return nc


def t5_bucket_np(rel):
    rel = np.asarray(rel, np.int64)
    half, me = 16, 8
    ret = np.where(rel > 0, half, 0)
    n = np.abs(rel)
    nf = np.maximum(n, 1).astype(np.float32)
    large = me + (np.log(nf / np.float32(me)) / np.float32(math.log(128 / 8)) * np.float32(half - me)).astype(np.int32)
    large = np.minimum(large, half - 1)
    return ret + np.where(n < me, n, large)


def make_consts(rel_bias):
    c32 = np.zeros((128, NC32), np.float32)
    eye = np.eye(128, dtype=np.float32)
    for i in range(4):
        c32[:, C_ID4 + 128 * i:C_ID4 + 128 * (i + 1)] = eye
    ncc = np.zeros((128, 128), np.float32)
    ncc[:64, 64:] = -1.0e30
    c32[:, C_NEGCC:C_NEGCC + 128] = ncc
    c32[:, C_POW2:C_POW2 + 32] = (0.5 ** np.arange(1, 33, dtype=np.float64)).astype(np.float32)[None, :]
    c32[:, C_ONES:C_ONES + 128] = 1.0
    c32[:, C_RB15:C_RB15 + 8] = rel_bias[15, :][None, :]
    cb = np.zeros((128, NCBF), np.float32)
    jj = np.arange(128)[:, None]
    ss = np.arange(128)[None, :]
    cb[:, B_NUT:B_NUT + 128] = np.where(jj >= ss, -1.0, 0.0)
    cb[:, B_NEG1:B_NEG1 + 128] = -1.0
    cb[:, B_ONES:B_ONES + 128] = 1.0
    cb[:, B_IDB:B_IDB + 128] = eye
    cc = np.arange(896)[None, :]
    cb[:, B_MW:B_MW + 896] = np.where((cc - 384) > jj, 1.0, 0.0)
    cbf = cb.astype(ml_dtypes.bfloat16)
    sl = np.arange(128)[:, None]
    tl = np.arange(512)[None, :]
    tz = np.zeros((8, 5, 128, 512), np.float32)
    for di in range(5):
        bidx = t5_bucket_np((di - 1) * 128 + sl - tl)
        tz[:, di] = np.transpose(rel_bias[bidx], (2, 0, 1))
    return c32, cbf, tz


_CACHE = {}


def kernel(x, w_in, w_out, ln_g, ln_b, rel_bias):
    x = np.asarray(x, np.float32)
    w_in = np.ascontiguousarray(np.asarray(w_in, np.float32))
    w_out = np.ascontiguousarray(np.asarray(w_out, np.float32))
    ln_g = np.ascontiguousarray(np.asarray(ln_g, np.float32))
    ln_b = np.ascontiguousarray(np.asarray(ln_b, np.float32))
    rel_bias = np.asarray(rel_bias, np.float32)
    c32, cbf, tz = make_consts(rel_bias)
    if "nc" not in _CACHE:
        _CACHE["nc"] = build()
    nc = _CACHE["nc"]
    in_maps = []
    for c in range(8):
        b = c % 4
        in_maps.append({"x": np.ascontiguousarray(x[b]), "w_in": w_in, "w_out": w_out, "ln_g": ln_g, "ln_b": ln_b,
                        "tz": tz, "c32": c32, "cbf": cbf})
    res = run_bass_kernel_spmd(nc, in_maps, core_ids=list(range(8)))
    out = np.stack([np.asarray(res.results[b]["out"], np.float32) for b in range(4)], axis=0)
    return out
```

```python
import math
import numpy as np
import ml_dtypes
import concourse.bass as bass
import concourse.mybir as mybir
from concourse.bass_utils import run_bass_kernel_spmd

F32 = mybir.dt.float32
BF16 = mybir.dt.bfloat16
U8 = mybir.dt.uint8
AF = mybir.ActivationFunctionType
ALU = mybir.AluOpType
AX = mybir.AxisListType

S = 4096
D = 1024
NIN = 4680
NL = 2
QA, KA, VA, GA, QB, KB, VB, GB, QI, KI, WI = 0, 512, 1024, 1536, 2048, 2560, 3072, 3584, 4096, 4608, 4672
ALPHA = (2.0 * NL) ** 0.25
LN_EPS = 1e-5
TOPK = 256
NIT = 16
F_QA, F_KA, F_SGA, F_QB, F_KB, F_SGB, F_QI, F_KI = 0, 4, 8, 12, 16, 20, 24, 28
NFCH = 29
FCH = ([(QA + 128 * i, "q") for i in range(4)] + [(KA + 128 * i, "c") for i in range(4)]
       + [(GA + 128 * i, "g") for i in range(4)] + [(QB + 128 * i, "q") for i in range(4)]
       + [(KB + 128 * i, "c") for i in range(4)] + [(GB + 128 * i, "g") for i in range(4)]
       + [(QI + 128 * i, "c") for i in range(4)] + [(-1, "c")])
C_ID4, C_NEGCC, C_POW2, C_ONES, C_RB15 = 0, 512, 640, 672, 800
NC32 = 808
B_NUT, B_NEG1, B_ONES, B_IDB, B_MW = 0, 128, 256, 384, 512
NCBF = 512 + 896


class Sched:
    def __init__(self):
        self.ops = []
        self.last_w = {}
        self.readers = {}
        self.bar_idx = None
        self.defer = None

    def begin_defer(self):
        self.defer = []

    def end_defer(self):
        d = self.defer
        self.defer = None
        return d

    def flush(self, lst):
        for a in lst:
            self.add(*a)

    def add(self, eng, fn, reads=(), writes=(), dma=False):
        if self.defer is not None:
            self.defer.append((eng, fn, tuple(reads), tuple(writes), dma))
            return None
        idx = len(self.ops)
        deps = set()
        if self.bar_idx is not None:
            deps.add(self.bar_idx)
        for k in reads:
            if k in self.last_w:
                deps.add(self.last_w[k])
        for k in writes:
            if k in self.last_w:
                deps.add(self.last_w[k])
            rd = self.readers.get(k)
            if rd:
                deps.update(rd[0].values())
                deps.update(rd[1])
        self.ops.append(dict(eng=eng, fn=fn, deps=deps, dma=dma, sig=dma))
        for k in reads:
            rd = self.readers.setdefault(k, ({}, []))
            if dma:
                rd[1].append(idx)
            else:
                rd[0][eng] = idx
        for k in writes:
            self.last_w[k] = idx
            self.readers[k] = ({}, [])
        return idx

    def barrier(self):
        keys = list(set(self.last_w.keys()) | set(self.readers.keys()))
        bt = self.bar_tile
        self.bar_idx = self.add("dve", lambda e: e.memset(bt, 0.0), reads=(), writes=keys + ["__bar__"])
        self.last_w = {}
        self.readers = {}

    def emit(self, nc, ndma=8):
        ops = self.ops
        NDMA = {"sp": ndma, "pool": 4, "act": 4}
        for i, o in enumerate(ops):
            nd = set()
            for d in o["deps"]:
                if ops[d]["eng"] == "pe" and o["eng"] == "pe" and not ops[d]["dma"] and not o["dma"]:
                    continue
                nd.add(d)
            o["deps"] = nd
        slot_last = {}
        dcount = {"sp": 0, "pool": 0, "act": 0}
        for i, o in enumerate(ops):
            if o["dma"]:
                q = o["eng"]
                slot = dcount[q] % NDMA[q]
                dcount[q] += 1
                o["slot"] = slot
                prev = slot_last.get((q, slot))
                if prev is not None:
                    o["deps"].add(prev)
                slot_last[(q, slot)] = i
        for o in ops:
            for d in o["deps"]:
                ops[d]["sig"] = True
        csem = {e: nc.alloc_semaphore("sem_" + e) for e in ("pe", "act", "dve", "pool")}
        dsem = {q: [nc.alloc_semaphore("dsem_%s%d" % (q, i)) for i in range(NDMA[q])] for q in NDMA}
        ccount = {e: 0 for e in csem}
        duse = {(q, s): 0 for q in NDMA for s in range(NDMA[q])}
        for o in ops:
            if o["dma"]:
                key = (o["eng"], o["slot"])
                duse[key] += 1
                o["tick"] = (dsem[o["eng"]][o["slot"]], 16 * duse[key])
            elif o["sig"]:
                ccount[o["eng"]] += 1
                o["tick"] = (csem[o["eng"]], ccount[o["eng"]])
        streams = {e: [] for e in ("pe", "act", "dve", "pool", "sp")}
        for i, o in enumerate(ops):
            streams[o["eng"]].append(i)

        def run(engname, eng):
            waited = {}
            for i in streams[engname]:
                o = ops[i]
                for d in sorted(o["deps"]):
                    sem, val = ops[d]["tick"]
                    if waited.get(sem.num, 0) < val:
                        eng.wait_ge(sem, val)
                        waited[sem.num] = val
                ins = o["fn"](eng)
                if o["sig"]:
                    sem, val = o["tick"]
                    ins.then_inc(sem, 16 if o["dma"] else 1)
            return waited

        with nc.Block() as block:
            @block.tensor
            def _(e):
                run("pe", e)

            @block.scalar
            def _(e):
                run("act", e)

            @block.vector
            def _(e):
                run("dve", e)

            @block.gpsimd
            def _(e):
                run("pool", e)

            @block.sync
            def _(e):
                run("sp", e)
                for q in NDMA:
                    for s in range(NDMA[q]):
                        if duse[(q, s)]:
                            e.wait_ge(dsem[q][s], 16 * duse[(q, s)])


def emit_pipelined(sc, stages, skew=1, collect=False):
    n = len(stages)
    ns = len(stages[0]) if n else 0
    out = []
    for step in range(n + (ns - 1) * skew):
        for k in range(ns):
            i = step - k * skew
            if 0 <= i < n:
                if collect:
                    out.extend(stages[i][k])
                else:
                    sc.flush(stages[i][k])
    return out


def interleave(sc, B, A):
    nb, na = len(B), len(A)
    if nb == 0:
        sc.flush(A)
        return
    for i in range(nb):
        sc.flush([B[i]])
        sc.flush(A[i * na // nb:(i + 1) * na // nb])


class Arena:
    def __init__(self, nc, nbytes):
        self.t = nc.alloc_sbuf_tensor("arena", [128, nbytes], U8)
        self.n = nbytes
        self.off = 0

    def reset(self):
        self.off = 0

    def tile(self, shape, dtype):
        esz = 4 if dtype == F32 else (1 if dtype == U8 else 2)
        free = 1
        for s in shape[1:]:
            free *= s
        nb = (free * esz + 63) // 64 * 64
        assert self.off + nb <= self.n, ("arena overflow", self.off, nb)
        v = self.t[:, self.off:self.off + free * esz]
        if dtype != U8:
            v = v.bitcast(dtype)
        self.off += nb
        if len(shape) == 3:
            v = v.rearrange("p (a b) -> p a b", b=shape[2])
        return v


def build(nlayers=NL, phases=("proj", "sb", "dsa", "out"), dbg=False, sb_win=None, nit=NIT, qbs=range(8)):
    nc = bass.Bass("TRN2", target_bir_lowering=False)

    def dram(name, shape, dtype, kind="Internal"):
        return nc.dram_tensor(name, shape, dtype, kind=kind).ap()

    x_d = dram("x", [S, D], F32, "ExternalInput")
    win_d = dram("w_in", [NL, D, NIN], F32, "ExternalInput")
    wout_d = dram("w_out", [NL, D, D], F32, "ExternalInput")
    lng_d = dram("ln_g", [NL, D], F32, "ExternalInput")
    lnb_d = dram("ln_b", [NL, D], F32, "ExternalInput")
    tz_d = dram("tz", [8, 5, 128, 512], F32, "ExternalInput")
    c32_d = dram("c32", [128, NC32], F32, "ExternalInput")
    cbf_d = dram("cbf", [128, NCBF], BF16, "ExternalInput")
    out_d = dram("out", [S, D], F32, "ExternalOutput")
    sk = "ExternalOutput" if dbg else "Internal"
    featT_d = dram("featT", [NFCH * 128, S], BF16, sk)
    V_d = dram("Vs", [S, 1024], BF16, sk)
    wI_d = dram("wIs", [S, 8], F32, sk)
    mixT_d = dram("mixT", [1024, S], BF16, sk)
    h1_d = dram("h1", [S, D], F32, "Internal")

    sc = Sched()
    ar = Arena(nc, 166 * 1024)
    sc.bar_tile = nc.alloc_sbuf_tensor("bart", [128, 8], F32)[:, 0:1]
    c32 = nc.alloc_sbuf_tensor("c32s", [128, NC32], F32)
    cbf = nc.alloc_sbuf_tensor("cbfs", [128, NCBF], BF16)
    gam = nc.alloc_sbuf_tensor("gam", [128, D], F32)
    bet = nc.alloc_sbuf_tensor("bet", [128, D], F32)
    PD = [nc.alloc_psum_tensor("pd%d" % i, [128, 2, 512], F32) for i in range(4)]
    banks = [PD[i][:, j, :] for i in range(4) for j in range(2)]
    bkey = [("ps", i) for i in range(8)]

    ident = c32[:, C_ID4:C_ID4 + 128]
    ident4 = c32[:, C_ID4:C_ID4 + 512]
    negcc = c32[:, C_NEGCC:C_NEGCC + 128]
    pow2 = c32[:, C_POW2:C_POW2 + 32]
    ones32 = c32[:, C_ONES:C_ONES + 128]
    rb15 = c32[:, C_RB15:C_RB15 + 8]
    nut = cbf[:, B_NUT:B_NUT + 128]
    neg1 = cbf[:, B_NEG1:B_NEG1 + 128]
    onesb = cbf[:, B_ONES:B_ONES + 128]
    identb = cbf[:, B_IDB:B_IDB + 128]
    mw = cbf[:, B_MW:B_MW + 896]

    def dma(q, out, in_, reads, writes):
        sc.add(q, lambda e, out=out, in_=in_: e.dma_start(out=out, in_=in_), reads=reads, writes=writes, dma=True)

    dma("sp", c32[:, :], c32_d[:, :], [], ["c32"])
    dma("sp", cbf[:, :], cbf_d[:, :], [], ["cbf"])
    CST = ["c32", "cbf"]

    def act(out, in_, func, reads, writes, bias=None, scale=None, accum_out=None):
        kw = {}
        if bias is not None:
            kw["bias"] = bias
        if scale is not None:
            kw["scale"] = scale
        if accum_out is not None:
            kw["accum_out"] = accum_out
        sc.add("act", lambda e: e.activation(out=out, in_=in_, func=func, **kw), reads=reads, writes=writes)

    def mm_group(out, pairs, reads, writes):
        def fn(e):
            ins = None
            n = len(pairs)
            for i, (l, r) in enumerate(pairs):
                ins = e.matmul(out, l, r, start=(i == 0), stop=(i == n - 1))
            return ins
        sc.add("pe", fn, reads=reads, writes=writes)

    def phase_proj(L, src_d, srckey):
        ar.reset()
        wbf = ar.tile([128, 8, NIN], BF16)
        wki2 = ar.tile([128, 8, 128], BF16)
        wst = [ar.tile([128, 2340], F32) for _ in range(2)]
        xin = [ar.tile([128, 4, 1024], F32) for _ in range(2)]
        xT = [ar.tile([128, 8, 512], BF16) for _ in range(2)]
        ost = [ar.tile([128, 512], BF16) for _ in range(4)]
        vst = [ar.tile([128, 1024], BF16) for _ in range(2)]
        wist = [ar.tile([128, 8], F32) for _ in range(2)]
        h = 2340
        for kc in range(8):
            for b in range(2):
                dma("sp", wst[b][:, :], win_d[L, kc * 128:(kc + 1) * 128, b * h:(b + 1) * h], [], [("wst", b)])
                if b == 0:
                    sc.add("dve", lambda e, kc=kc: e.tensor_copy(out=wbf[:, kc, 0:h], in_=wst[0][:, :]),
                           reads=[("wst", 0)], writes=[("wbf", kc, 0)])
                else:
                    sc.add("pool", lambda e, kc=kc: e.tensor_copy(out=wbf[:, kc, h:NIN], in_=wst[1][:, :]),
                           reads=[("wst", 1)], writes=[("wbf", kc, 1)])
                    sc.add("pool", lambda e, kc=kc: e.tensor_copy(out=wki2[:, kc, 0:64], in_=wst[1][:, KI - h:KI - h + 64]),
                           reads=[("wst", 1)], writes=[("wki2", kc, 0)])
                    sc.add("pool", lambda e, kc=kc: e.tensor_copy(out=wki2[:, kc, 64:128], in_=wst[1][:, KI - h:KI - h + 64]),
                           reads=[("wst", 1)], writes=[("wki2", kc, 1)])
        WK = [("wbf", kc, i) for kc in range(8) for i in range(2)] + [("wki2", kc, i) for kc in range(8) for i in range(2)]
        bi = 0
        oi = 0
        for tb in range(8):
            xb = tb % 2
            dma("sp", xin[xb][:, :, :], src_d[tb * 512:(tb + 1) * 512, :].rearrange("(j p) d -> p j d", p=128),
                [srckey], [("xin", xb)])
            for j in range(4):
                for g in range(2):
                    bk = bi % 8
                    bi += 1

                    def fn(e, xb=xb, j=j, g=g, bk=bk):
                        ins = None
                        for i in range(4):
                            kc = g * 4 + i
                            ins = e.transpose(banks[bk][:, i * 128:(i + 1) * 128],
                                              xin[xb][:, j, kc * 128:(kc + 1) * 128], ident)
                        return ins
                    sc.add("pe", fn, reads=[("xin", xb)] + CST, writes=[bkey[bk]])
                    o = xT[xb][:, g * 4:g * 4 + 4, j * 128:(j + 1) * 128]
                    i_ = banks[bk].rearrange("p (a b) -> p a b", b=128)
                    if (j + g) % 2 == 0:
                        sc.add("dve", lambda e, o=o, i_=i_: e.tensor_copy(out=o, in_=i_),
                               reads=[bkey[bk]], writes=[("xT", xb, j, g)])
                    else:
                        act(o, i_, AF.Copy, [bkey[bk]], [("xT", xb, j, g)])
            XT = [("xT", xb, j, g) for j in range(4) for g in range(2)]
            for c in range(NFCH):
                col, kind = FCH[c]
                bk = bi % 8
                bi += 1
                if col >= 0:
                    pairs = [(wbf[:, kc, col:col + 128], xT[xb][:, kc, :]) for kc in range(8)]
                else:
                    pairs = [(wki2[:, kc, :], xT[xb][:, kc, :]) for kc in range(8)]
                mm_group(banks[bk], pairs, XT + WK, [bkey[bk]])
                os_ = oi % 4
                oi += 1
                if kind == "g":
                    act(ost[os_][:, :], banks[bk], AF.Silu, [bkey[bk]], [("ost", os_)])
                elif kind == "q":
                    sc.add("dve", lambda e, os_=os_, bk=bk: e.tensor_scalar(
                        out=ost[os_][:, :], in0=banks[bk], scalar1=0.125, scalar2=None, op0=ALU.mult),
                        reads=[bkey[bk]], writes=[("ost", os_)])
                else:
                    if c % 2 == 0:
                        sc.add("dve", lambda e, os_=os_, bk=bk: e.tensor_copy(out=ost[os_][:, :], in_=banks[bk]),
                               reads=[bkey[bk]], writes=[("ost", os_)])
                    else:
                        act(ost[os_][:, :], banks[bk], AF.Copy, [bkey[bk]], [("ost", os_)])
                dma("pool", featT_d[c * 128:(c + 1) * 128, tb * 512:(tb + 1) * 512], ost[os_][:, :],
                    [("ost", os_)], [("featT", c, tb)])
            for j in range(4):
                vs = (tb * 4 + j) % 2
                for hf in range(2):
                    bk = bi % 8
                    bi += 1
                    c0 = VA if hf == 0 else VB
                    pairs = [(xT[xb][:, kc, j * 128:(j + 1) * 128], wbf[:, kc, c0:c0 + 512]) for kc in range(8)]
                    mm_group(banks[bk], pairs, XT + WK, [bkey[bk]])
                    o = vst[vs][:, hf * 512:(hf + 1) * 512]
                    if hf == 0:
                        sc.add("dve", lambda e, o=o, bk=bk: e.tensor_copy(out=o, in_=banks[bk]),
                               reads=[bkey[bk]], writes=[("vst", vs, hf)])
                    else:
                        act(o, banks[bk], AF.Copy, [bkey[bk]], [("vst", vs, hf)])
                r0 = tb * 512 + j * 128
                dma("pool", V_d[r0:r0 + 128, :], vst[vs][:, :], [("vst", vs, 0), ("vst", vs, 1)], [("V", tb)])
                bk = bi % 8
                bi += 1
                pairs = [(xT[xb][:, kc, j * 128:(j + 1) * 128], wbf[:, kc, WI:WI + 8]) for kc in range(8)]
                mm_group(banks[bk][:, 0:8], pairs, XT + WK, [bkey[bk]])
                sc.add("dve", lambda e, vs=vs, bk=bk: e.tensor_copy(out=wist[vs][:, :], in_=banks[bk][:, 0:8]),
                       reads=[bkey[bk]], writes=[("wist", vs)])
                dma("pool", wI_d[r0:r0 + 128, :], wist[vs][:, :], [("wist", vs)], [("wI", tb)])

    def featkeys(c0, nch, tb_hi):
        return [("featT", c0 + i, t) for i in range(nch) for t in range(tb_hi + 1)]

    def phase_sb(L):
        ar.reset()
        qT = [ar.tile([128, 512], BF16) for _ in range(2)]
        sg = [ar.tile([128, 512], BF16) for _ in range(2)]
        kT = [ar.tile([128, S], BF16) for _ in range(2)]
        vP = [ar.tile([128, 32, 128], BF16) for _ in range(2)]
        ee = [ar.tile([128, 2, 512], F32) for _ in range(2)]
        sp = [ar.tile([128, 2, 512], BF16) for _ in range(2)]
        Sb = ar.tile([128, 2, 512], BF16)
        Aa = [ar.tile([128, 2, 512], BF16) for _ in range(2)]
        mst = [ar.tile([128, 512], BF16) for _ in range(2)]
        it = 0
        tcount = 0
        stages = []
        for q in qbs:
            for p in range(4):
                s = it % 2
                it += 1
                nk = 512 * (q + 1)
                sc.begin_defer()
                kb_hi = 4 * q + 3
                kb_lo = 0 if sb_win is None else max(0, 4 * q - sb_win)
                k0 = kb_lo * 128
                dma("sp", qT[s][:, :], featT_d[(F_QA + p) * 128:(F_QA + p + 1) * 128, q * 512:(q + 1) * 512],
                    [("featT", F_QA + p, q)], [("qT", s)])
                dma("sp", sg[s][:, :], featT_d[(F_SGA + p) * 128:(F_SGA + p + 1) * 128, q * 512:(q + 1) * 512],
                    [("featT", F_SGA + p, q)], [("sg", s)])
                dma("sp", kT[s][:, k0:nk], featT_d[(F_KA + p) * 128:(F_KA + p + 1) * 128, k0:nk],
                    featkeys(F_KA + p, 1, q), [("kT", s)])
                dma("sp", vP[s][:, kb_lo:kb_hi + 1, :],
                    V_d[k0:nk, p * 128:(p + 1) * 128].rearrange("(kb p) c -> p kb c", p=128),
                    [("V", t) for t in range(q + 1)], [("vP", s)])
                OB = PD[3]
                first = True
                for kb in range(kb_hi, kb_lo - 1, -1):
                    zi = tcount % 2
                    tcount += 1
                    Z = PD[zi]
                    T = PD[2]
                    diag = kb >= 4 * q
                    kk = kb - 4 * q
                    last = kb == kb_lo
                    if not first:
                        sc.begin_defer()

                    def fz(e, Z=Z, s=s, kb=kb):
                        ins = None
                        for hh in range(2):
                            pr = slice(64 * hh, 64 * hh + 64)
                            ins = e.matmul(Z[:, hh, :], kT[s][pr, kb * 128:(kb + 1) * 128], qT[s][pr, :],
                                           start=True, stop=True)
                        return ins
                    sc.add("pe", fz, reads=[("kT", s), ("qT", s)], writes=[("pd", zi)])
                    act(ee[zi][:, :, :], Z[:, :, :], AF.Exp, [("pd", zi)], [("ee", zi)])
                    act(sp[zi][:, :, :], ee[zi][:, :, :], AF.Ln, [("ee", zi)], [("sp", zi)], bias=1.0)
                    if diag:
                        msk = mw[:, 384 - 128 * kk:384 - 128 * kk + 512]
                        for hh in range(2):
                            sc.add("dve", lambda e, zi=zi, hh=hh, msk=msk: e.tensor_tensor(
                                out=sp[zi][:, hh, :], in0=sp[zi][:, hh, :], in1=msk, op=ALU.mult),
                                reads=[("sp", zi)] + CST, writes=[("sp", zi)])

                    st1 = sc.end_defer()
                    sc.begin_defer()

                    def ft(e, T=T, s=s, kb=kb, zi=zi, first=first):
                        ins = None
                        for hh in range(2):
                            pr = slice(64 * hh, 64 * hh + 64)
                            e.matmul(T[:, hh, :], nut, sp[zi][:, hh, :], start=True, stop=False)
                            if not first:
                                e.matmul(T[:, hh, :], neg1, Sb[:, hh, :], start=False, stop=False)
                            ins = e.matmul(T[:, hh, :], kT[s][pr, kb * 128:(kb + 1) * 128], qT[s][pr, :],
                                           start=False, stop=True)
                        return ins
                    sc.add("pe", ft, reads=[("sp", zi), ("Sb",), ("kT", s), ("qT", s)] + CST, writes=[("pd", 2)])
                    act(Aa[zi][:, :, :], T[:, :, :], AF.Exp, [("pd", 2)], [("Aa", zi)])
                    if diag:
                        for hh in range(2):
                            sc.add("dve", lambda e, zi=zi, hh=hh, msk=msk: e.tensor_tensor(
                                out=Aa[zi][:, hh, :], in0=Aa[zi][:, hh, :], in1=msk, op=ALU.mult),
                                reads=[("Aa", zi)] + CST, writes=[("Aa", zi)])

                    if not last:
                        if first:
                            sc.add("dve", lambda e, zi=zi: e.tensor_copy(out=Sb[:, :, :], in_=sp[zi][:, :, :]),
                                   reads=[("sp", zi)], writes=[("Sb",)])
                        else:
                            sc.add("dve", lambda e, zi=zi: e.tensor_tensor(
                                out=Sb[:, :, :], in0=Sb[:, :, :], in1=sp[zi][:, :, :], op=ALU.add),
                                reads=[("sp", zi), ("Sb",)], writes=[("Sb",)])
                    st2 = sc.end_defer()
                    sc.begin_defer()

                    def fo(e, s=s, kb=kb, zi=zi, first=first, last=last):
                        ins = None
                        for hh in range(2):
                            ins = e.matmul(OB[:, hh, :], vP[s][:, kb, :], Aa[zi][:, hh, :], start=first, stop=last)
                        return ins
                    sc.add("pe", fo, reads=[("vP", s), ("Aa", zi)], writes=[("pd", 3)])
                    first = False
                    if last:
                        for hh in range(2):
                            pr = slice(64 * hh, 64 * hh + 64)
                            sc.add("dve", lambda e, s=s, hh=hh, pr=pr: e.tensor_tensor(
                                out=mst[s][pr, :], in0=OB[pr, hh, :], in1=sg[s][pr, :], op=ALU.mult),
                                reads=[("pd", 3), ("sg", s)], writes=[("mst", s, hh)])
                        dma("pool", mixT_d[p * 128:(p + 1) * 128, q * 512:(q + 1) * 512], mst[s][:, :],
                            [("mst", s, 0), ("mst", s, 1)], [("mixT", p, q)])
                    st3 = sc.end_defer()
                    stages.append((st1, st2, st3))
        emit_pipelined(sc, stages)

    def phase_dsa(L):
        ar.reset()
        qIT = ar.tile([128, 4, 512], BF16)
        kI2 = ar.tile([128, S], BF16)
        wIt = ar.tile([128, 4, 8], F32)
        dg = ar.tile([128, 8, 128], BF16)
        Rr = [ar.tile([128, 2, 512], BF16) for _ in range(4)]
        scoreb = [ar.tile([128, S], F32) for _ in range(2)]
        junk = ar.tile([128, S], U8)
        junk2 = ar.tile([128, 2304], BF16)
        maskT = ar.tile([128, 32, 512], BF16)
        small = ar.tile([128, 64], F32)
        wt = ar.tile([128, 32], F32)
        dthr = ar.tile([128, 512], F32)
        thrB = ar.tile([128, 512], F32)
        qT = [ar.tile([128, 512], BF16) for _ in range(2)]
        sg = [ar.tile([128, 512], BF16) for _ in range(2)]
        kT = [ar.tile([128, S], BF16) for _ in range(2)]
        vP = [ar.tile([128, 32, 128], BF16) for _ in range(2)]
        tzb = [ar.tile([128, 512], F32) for _ in range(4)]
        tmp = [ar.tile([128, 512], F32) for _ in range(2)]
        Ee = [ar.tile([128, 2, 512], BF16) for _ in range(3)]
        Pm = [ar.tile([128, 2, 512], BF16) for _ in range(3)]
        rd = ar.tile([128, 512], F32)
        mo = ar.tile([128, 512], F32)
        mst = [ar.tile([128, 512], BF16) for _ in range(2)]
        mx, mn, w0, lo0, mid, cnt, tt, thr, sgn, uu = [small[:, i:i + 1] for i in range(10)]
        it = 0
        tcount = 0
        tzc = 0
        for q in qbs:
            nkq = 512 * (q + 1)
            dma("sp", qIT[:, :, :],
                featT_d[F_QI * 128:(F_QI + 4) * 128, q * 512:(q + 1) * 512].rearrange("(c p) t -> p c t", p=128),
                [("featT", F_QI + i, q) for i in range(4)], [("qIT",)])
            dma("sp", kI2[:, 0:nkq], featT_d[F_KI * 128:(F_KI + 1) * 128, 0:nkq], featkeys(F_KI, 1, q), [("kI2",)])
            dma("sp", wIt[:, :, :], wI_d[q * 512:(q + 1) * 512, :].rearrange("(j p) h -> p j h", p=128),
                [("wI", q)], [("wIt",)])
            for j in range(3):
                sc.add("pool", lambda e, j=j, q=q: e.memset(maskT[:, 4 * q + j + 1:4 * q + 4, j * 128:(j + 1) * 128], 0.0),
                       reads=[], writes=[("maskT", j)])
            Aops = []
            Bops = []
            for j in range(4):
                n = 512 * q + 128 * (j + 1)
                score = scoreb[j % 2]
                sck = ("score%d" % (j % 2))
                sc.begin_defer()
                for h in range(8):
                    sc.add("pool" if h % 2 else "dve", lambda e, j=j, h=h: e.tensor_scalar(
                        out=dg[:, h, :], in0=identb, scalar1=wIt[:, j, h:h + 1], scalar2=1.0, op0=ALU.mult, op1=ALU.mult),
                        reads=[("wIt",)] + CST, writes=[("dg", h)])
                pre = sc.end_defer()
                nch = (n + 511) // 512
                stages = []
                for c in range(nch):
                    wc = min(512, n - 512 * c)
                    ACC = PD[2][:, 0, :]
                    for pp in range(4):
                        zi = tcount % 2
                        tcount += 1
                        X = PD[zi]
                        sc.begin_defer()

                        def fx(e, X=X, pp=pp, j=j, c=c, wc=wc):
                            ins = None
                            for hh in range(2):
                                pr = slice(64 * hh, 64 * hh + 64)
                                ins = e.matmul(X[:, hh, 0:wc], qIT[pr, pp, j * 128:(j + 1) * 128],
                                               kI2[pr, c * 512:c * 512 + wc], start=True, stop=True)
                            return ins
                        sc.add("pe", fx, reads=[("qIT",), ("kI2",)], writes=[("pd", zi)])
                        ri = tcount % 4
                        if pp % 2 == 0:
                            act(Rr[ri][:, :, 0:wc], X[:, :, 0:wc], AF.Relu, [("pd", zi)], [("Rr", ri)])
                        else:
                            sc.add("dve", lambda e, ri=ri, X=X, wc=wc: e.tensor_scalar(
                                out=Rr[ri][:, :, 0:wc], in0=X[:, :, 0:wc], scalar1=0.0, scalar2=None, op0=ALU.max),
                                reads=[("pd", zi)], writes=[("Rr", ri)])
                        st1 = sc.end_defer()
                        sc.begin_defer()

                        def fa(e, pp=pp, ri=ri, wc=wc, ACC=ACC):
                            ins = None
                            for hh in range(2):
                                ins = e.matmul(ACC[:, 0:wc], dg[:, 2 * pp + hh, :], Rr[ri][:, hh, 0:wc],
                                               start=(pp == 0 and hh == 0), stop=(pp == 3 and hh == 1))
                            return ins
                        sc.add("pe", fa, reads=[("Rr", ri), ("dg", 2 * pp), ("dg", 2 * pp + 1)], writes=[("pd", 2)])
                        if pp == 3:
                            act(score[:, c * 512:c * 512 + wc], ACC[:, 0:wc], AF.Copy, [("pd", 2)], [(sck, c)])
                        st2 = sc.end_defer()
                        stages.append((st1, st2))
                Aops.append(pre + emit_pipelined(sc, stages, skew=2, collect=True))
                sc.begin_defer()
                SCK = [(sck, c) for c in range(nch)]
                sc.add("dve", lambda e, n=n, score=score: e.tensor_tensor(out=score[:, n - 128:n], in0=score[:, n - 128:n],
                                                                          in1=negcc, op=ALU.add),
                       reads=SCK + CST, writes=[(sck, nch - 1)])
                if n <= 256:
                    sc.add("dve", lambda e: e.memset(thr, -1.0e29), reads=[], writes=[("thr",)])
                else:
                    n1 = max(128, (n * 8 // 16) // 128 * 128)
                    nA = n - n1
                    sc.add("dve", lambda e, n=n, score=score: e.tensor_reduce(out=mx, in_=score[:, 0:n], axis=AX.X, op=ALU.max),
                           reads=SCK, writes=[("mx",)])
                    sc.add("dve", lambda e, n=n, score=score: e.tensor_reduce(out=mn, in_=score[:, 0:n - 64], axis=AX.X, op=ALU.min),
                           reads=SCK, writes=[("mn",)])
                    sc.add("dve", lambda e: e.scalar_tensor_tensor(out=w0, in0=mx, scalar=1.0, in1=mn,
                                                                   op0=ALU.add, op1=ALU.subtract),
                           reads=[("mx",), ("mn",)], writes=[("w0",)])
                    sc.add("dve", lambda e: e.tensor_scalar(out=wt[:, 0:nit + 1], in0=pow2[:, 0:nit + 1],
                                                            scalar1=w0, scalar2=None, op0=ALU.mult),
                           reads=[("w0",)] + CST, writes=[("wt",)])
                    sc.add("dve", lambda e: e.scalar_tensor_tensor(out=mid, in0=mn, scalar=-1.0, in1=wt[:, 0:1],
                                                                   op0=ALU.add, op1=ALU.add),
                           reads=[("mn",), ("wt",)], writes=[("mid",)])
                    for k in range(nit):
                        act(junk2[:, 0:nA], score[:, n1:n], AF.Sign, SCK + [("mid",)], [("sgn",), ("junk2",)],
                            bias=mid, scale=-1.0, accum_out=sgn)
                        sc.add("dve", lambda e, n1=n1, score=score: e.tensor_scalar(
                            out=junk[:, 0:n1], in0=score[:, 0:n1], scalar1=mid, scalar2=None,
                            op0=ALU.is_gt, op1=ALU.add, accum_out=cnt),
                            reads=SCK + [("mid",)], writes=[("cnt",), ("junk",)])
                        sc.add("dve", lambda e: e.scalar_tensor_tensor(out=uu, in0=sgn, scalar=-0.5, in1=cnt,
                                                                       op0=ALU.mult, op1=ALU.add),
                               reads=[("sgn",), ("cnt",)], writes=[("uu",)])
                        sc.add("dve", lambda e, nA=nA: e.tensor_scalar(
                            out=tt, in0=uu, scalar1=float(TOPK) - 0.5 - 0.5 * nA, scalar2=0.5,
                            op0=ALU.is_ge, op1=ALU.subtract),
                            reads=[("uu",)], writes=[("tt",)])
                        if k < nit - 1:
                            sc.add("dve", lambda e, k=k: e.scalar_tensor_tensor(
                                out=mid, in0=tt, scalar=wt[:, k:k + 1], in1=mid, op0=ALU.mult, op1=ALU.add),
                                reads=[("tt",), ("wt",), ("mid",)], writes=[("mid",)])
                        else:
                            sc.add("dve", lambda e: e.tensor_scalar(out=tt, in0=tt, scalar1=-0.5, scalar2=None, op0=ALU.add),
                                   reads=[("tt",)], writes=[("tt",)])
                            sc.add("dve", lambda e, k=k: e.scalar_tensor_tensor(
                                out=thr, in0=tt, scalar=wt[:, k:k + 1], in1=mid, op0=ALU.mult, op1=ALU.add),
                                reads=[("tt",), ("wt",), ("mid",)], writes=[("thr",)])
                sc.add("dve", lambda e: e.tensor_scalar(out=dthr[:, :], in0=ident4, scalar1=thr, scalar2=None, op0=ALU.mult),
                       reads=[("thr",)] + CST, writes=[("dthr",)])
                mm_group(PD[3][:, 0, :], [(ones32, dthr[:, :])], [("dthr",)] + CST, [("pd", 3)])
                act(thrB[:, :], PD[3][:, 0, :], AF.Copy, [("pd", 3)], [("thrB",)])
                nkb = n // 128
                for g0 in range(0, nkb, 4):
                    gn = min(4, nkb - g0)
                    TR = PD[3][:, 1, :]

                    def ftr(e, g0=g0, gn=gn, TR=TR, score=score):
                        ins = None
                        for i in range(gn):
                            ins = e.transpose(TR[:, i * 128:(i + 1) * 128], score[:, (g0 + i) * 128:(g0 + i + 1) * 128], ident)
                        return ins
                    sc.add("pe", ftr, reads=SCK + CST, writes=[("pd", 3, 1)])
                    sc.add("dve", lambda e, g0=g0, gn=gn, TR=TR, j=j: e.tensor_tensor(
                        out=maskT[:, g0:g0 + gn, j * 128:(j + 1) * 128],
                        in0=TR[:, 0:gn * 128].rearrange("p (a b) -> p a b", b=128),
                        in1=thrB[:, 0:gn * 128].rearrange("p (a b) -> p a b", b=128), op=ALU.is_gt),
                        reads=[("pd", 3, 1), ("thrB",)], writes=[("maskT", j)])
                Bops.append(sc.end_defer())
            sc.flush(Aops[0])
            for j in range(4):
                interleave(sc, Bops[j], Aops[j + 1] if j < 3 else [])
            MK = [("maskT", j) for j in range(4)]
            stages = []
            for p in range(4):
                s = it % 2
                it += 1
                kb_hi = 4 * q + 3
                sc.begin_defer()
                dma("sp", qT[s][:, :], featT_d[(F_QB + p) * 128:(F_QB + p + 1) * 128, q * 512:(q + 1) * 512],
                    [("featT", F_QB + p, q)], [("qT", s)])
                dma("sp", sg[s][:, :], featT_d[(F_SGB + p) * 128:(F_SGB + p + 1) * 128, q * 512:(q + 1) * 512],
                    [("featT", F_SGB + p, q)], [("sg", s)])
                dma("sp", kT[s][:, 0:nkq], featT_d[(F_KB + p) * 128:(F_KB + p + 1) * 128, 0:nkq],
                    featkeys(F_KB + p, 1, q), [("kT", s)])
                dma("sp", vP[s][:, 0:kb_hi + 1, :],
                    V_d[0:nkq, 512 + p * 128:512 + (p + 1) * 128].rearrange("(kb p) c -> p kb c", p=128),
                    [("V", t) for t in range(q + 1)], [("vP", s)])
                OB = PD[2]
                DN = PD[3]
                for kb in range(0, kb_hi + 1):
                    zi = tcount % 2
                    e3 = tcount % 3
                    tcount += 1
                    Z = PD[zi]
                    first = kb == 0
                    last = kb == kb_hi
                    near = kb >= 4 * q - 1
                    di = kb - (4 * q - 1)
                    if not first:
                        sc.begin_defer()

                    def fz(e, Z=Z, s=s, kb=kb):
                        ins = None
                        for hh in range(2):
                            pr = slice(64 * hh, 64 * hh + 64)
                            ins = e.matmul(Z[:, hh, :], kT[s][pr, kb * 128:(kb + 1) * 128], qT[s][pr, :],
                                           start=True, stop=True)
                        return ins
                    sc.add("pe", fz, reads=[("kT", s), ("qT", s)], writes=[("pd", zi)])
                    for hh in range(2):
                        h = 2 * p + hh
                        if near:
                            tzi = tzc % 4
                            tzc += 1
                            dma("sp", tzb[tzi][:, :], tz_d[h, di, :, :], [], [("tzb", tzi)])
                            sc.add("dve", lambda e, Z=Z, hh=hh, tzi=tzi: e.tensor_tensor(
                                out=tmp[hh][:, :], in0=Z[:, hh, :], in1=tzb[tzi][:, :], op=ALU.add),
                                reads=[("pd", zi), ("tzb", tzi)], writes=[("tmp", hh)])
                            act(Ee[e3][:, hh, :], tmp[hh][:, :], AF.Exp, [("tmp", hh)], [("Ee", e3, hh)])
                        else:
                            act(Ee[e3][:, hh, :], Z[:, hh, :], AF.Exp, [("pd", zi)] + CST, [("Ee", e3, hh)],
                                bias=rb15[:, h:h + 1])
                        sc.add("dve", lambda e, e3=e3, hh=hh, kb=kb: e.tensor_tensor(
                            out=Pm[e3][:, hh, :], in0=Ee[e3][:, hh, :], in1=maskT[:, kb, :], op=ALU.mult),
                            reads=[("Ee", e3, hh)] + MK, writes=[("Pm", e3, hh)])
                    st1 = sc.end_defer()
                    sc.begin_defer()

                    def fo(e, s=s, kb=kb, e3=e3, first=first, last=last):
                        ins = None
                        for hh in range(2):
                            e.matmul(OB[:, hh, :], vP[s][:, kb, :], Pm[e3][:, hh, :], start=first, stop=last)
                            ins = e.matmul(DN[:, hh, :], onesb, Pm[e3][:, hh, :], start=first, stop=last)
                        return ins
                    sc.add("pe", fo, reads=[("vP", s), ("Pm", e3, 0), ("Pm", e3, 1)] + CST, writes=[("pd", 2), ("pd", 3)])
                    if last:
                        for hh in range(2):
                            pr = slice(64 * hh, 64 * hh + 64)
                            sc.add("dve", lambda e, hh=hh, pr=pr: e.reciprocal(out=rd[pr, :], in_=DN[pr, hh, :]),
                                   reads=[("pd", 3)], writes=[("rd", hh)])
                            sc.add("dve", lambda e, hh=hh, pr=pr: e.tensor_tensor(out=mo[pr, :], in0=OB[pr, hh, :],
                                                                                  in1=rd[pr, :], op=ALU.mult),
                                   reads=[("pd", 2), ("rd", hh)], writes=[("mo", hh)])
                            sc.add("dve", lambda e, hh=hh, pr=pr, s=s: e.tensor_tensor(out=mst[s][pr, :], in0=mo[pr, :],
                                                                                       in1=sg[s][pr, :], op=ALU.mult),
                                   reads=[("mo", hh), ("sg", s)], writes=[("mst", s, hh)])
                        dma("pool", mixT_d[512 + p * 128:512 + (p + 1) * 128, q * 512:(q + 1) * 512], mst[s][:, :],
                            [("mst", s, 0), ("mst", s, 1)], [("mixT", 4 + p, q)])
                    st2 = sc.end_defer()
                    stages.append((st1, st2))
            emit_pipelined(sc, stages, skew=2)

    def phase_out(L, src_d, srckey, dst_d, dstkey):
        ar.reset()
        wo = ar.tile([128, 8, 1024], BF16)
        wst = [ar.tile([128, 1024], F32) for _ in range(2)]
        mxg = [ar.tile([128, 8, 512], BF16) for _ in range(2)]
        xr = [ar.tile([128, 4, 1024], F32) for _ in range(2)]
        rr = [ar.tile([128, 1024], F32) for _ in range(2)]
        jk = ar.tile([128, 1024], F32)
        jk2 = ar.tile([128, 1024], F32)
        xn = [ar.tile([128, 1024], F32) for _ in range(2)]
        oo = [ar.tile([128, 1024], F32) for _ in range(2)]
        st = [ar.tile([128, 16], F32) for _ in range(2)]
        dma("sp", gam[:, :], lng_d[L:L + 1, :].to_broadcast([128, D]), [], [("gam",)])
        dma("sp", bet[:, :], lnb_d[L:L + 1, :].to_broadcast([128, D]), [], [("bet",)])
        for kc in range(8):
            b = kc % 2
            dma("sp", wst[b][:, :], wout_d[L, kc * 128:(kc + 1) * 128, :], [], [("wst", b)])
            sc.add("dve", lambda e, b=b, kc=kc: e.tensor_copy(out=wo[:, kc, :], in_=wst[b][:, :]),
                   reads=[("wst", b)], writes=[("wo", kc)])
        WK = [("wo", kc) for kc in range(8)]
        bi = 0
        ti = 0
        for tb in qbs:
            g = tb % 2
            dma("sp", mxg[g][:, :, :], mixT_d[:, tb * 512:(tb + 1) * 512].rearrange("(c p) t -> p c t", p=128),
                [("mixT", c, tb) for c in range(8)], [("mxg", g)])
            dma("sp", xr[g][:, :, :], src_d[tb * 512:(tb + 1) * 512, :].rearrange("(j p) d -> p j d", p=128),
                [srckey], [("xr", g)])
            for j in range(4):
                r = ti % 2
                ti += 1
                for nh in range(2):
                    bk = bi % 8
                    bi += 1
                    pairs = [(mxg[g][:, mc, j * 128:(j + 1) * 128], wo[:, mc, nh * 512:(nh + 1) * 512]) for mc in range(8)]
                    mm_group(banks[bk], pairs, [("mxg", g)] + WK, [bkey[bk]])
                    sc.add("dve", lambda e, r=r, g=g, j=j, nh=nh, bk=bk: e.scalar_tensor_tensor(
                        out=rr[r][:, nh * 512:(nh + 1) * 512], in0=xr[g][:, j, nh * 512:(nh + 1) * 512],
                        scalar=float(ALPHA), in1=banks[bk], op0=ALU.mult, op1=ALU.add),
                        reads=[("xr", g), bkey[bk]], writes=[("rr", r, nh)])
                RK = [("rr", r, 0), ("rr", r, 1)]
                sm, ssq, mean, msq, var, sd, rstd, nmr = [st[r][:, i:i + 1] for i in range(8)]
                sc.add("dve", lambda e, r=r, sm=sm: e.tensor_scalar(out=jk[:, :], in0=rr[r][:, :], scalar1=1.0, scalar2=None,
                                                                    op0=ALU.mult, op1=ALU.add, accum_out=sm),
                       reads=RK, writes=[("st", r, 0), ("jk",)])
                act(jk2[:, :], rr[r][:, :], AF.Square, RK, [("st", r, 1), ("jk2",)], accum_out=ssq)
                sc.add("dve", lambda e, sm=sm, mean=mean: e.tensor_scalar(out=mean, in0=sm, scalar1=1.0 / D, scalar2=None,
                                                                          op0=ALU.mult),
                       reads=[("st", r, 0)], writes=[("st", r, 2)])
                sc.add("dve", lambda e, mean=mean, msq=msq: e.tensor_tensor(out=msq, in0=mean, in1=mean, op=ALU.mult),
                       reads=[("st", r, 2)], writes=[("st", r, 3)])
                sc.add("dve", lambda e, ssq=ssq, msq=msq, var=var: e.scalar_tensor_tensor(
                    out=var, in0=ssq, scalar=1.0 / D, in1=msq, op0=ALU.mult, op1=ALU.subtract),
                    reads=[("st", r, 1), ("st", r, 3)], writes=[("st", r, 4)])
                sc.add("dve", lambda e, var=var: e.tensor_scalar(out=var, in0=var, scalar1=LN_EPS, scalar2=None, op0=ALU.add),
                       reads=[("st", r, 4)], writes=[("st", r, 4)])
                act(sd, var, AF.Sqrt, [("st", r, 4)], [("st", r, 5)])
                sc.add("dve", lambda e, sd=sd, rstd=rstd: e.reciprocal(out=rstd, in_=sd),
                       reads=[("st", r, 5)], writes=[("st", r, 6)])
                sc.add("dve", lambda e, mean=mean, rstd=rstd, nmr=nmr: e.scalar_tensor_tensor(
                    out=nmr, in0=mean, scalar=-1.0, in1=rstd, op0=ALU.mult, op1=ALU.mult),
                    reads=[("st", r, 2), ("st", r, 6)], writes=[("st", r, 7)])
                act(xn[r][:, :], rr[r][:, :], AF.Identity, RK + [("st", r, 6), ("st", r, 7)], [("xn", r)],
                    bias=nmr, scale=rstd)
                sc.add("pool", lambda e, r=r: e.tensor_tensor(out=oo[r][:, :], in0=xn[r][:, :], in1=gam[:, :], op=ALU.mult),
                       reads=[("xn", r), ("gam",)], writes=[("oo", r)])
                sc.add("dve", lambda e, r=r: e.tensor_tensor(out=oo[r][:, :], in0=oo[r][:, :], in1=bet[:, :], op=ALU.add),
                       reads=[("oo", r), ("bet",)], writes=[("oo", r)])
                r0 = tb * 512 + j * 128
                dma("pool", dst_d[r0:r0 + 128, :], oo[r][:, :], [("oo", r)], [dstkey])

    for L in range(nlayers):
        src_d, srckey = (x_d, ("xsrc",)) if L == 0 else (h1_d, ("h1",))
        dst_d, dstkey = (out_d, ("outd",)) if L == nlayers - 1 else (h1_d, ("h1",))
        if "proj" in phases:
            phase_proj(L, src_d, srckey)
            sc.barrier()
        if "sb" in phases:
            phase_sb(L)
            sc.barrier()
        if "dsa" in phases:
            phase_dsa(L)
            sc.barrier()
        if "out" in phases:
            phase_out(L, src_d, srckey, dst_d, dstkey)
            sc.barrier()
    sc.emit(nc)
    return nc


def t5_bucket_np(rel):
    rel = np.asarray(rel, np.int64)
    half, me = 16, 8
    ret = np.where(rel > 0, half, 0)
    n = np.abs(rel)
    nf = np.maximum(n, 1).astype(np.float32)
    large = me + (np.log(nf / np.float32(me)) / np.float32(math.log(128 / 8)) * np.float32(half - me)).astype(np.int32)
    large = np.minimum(large, half - 1)
    return ret + np.where(n < me, n, large)


def make_consts(rel_bias):
    c32 = np.zeros((128, NC32), np.float32)
    eye = np.eye(128, dtype=np.float32)
    for i in range(4):
        c32[:, C_ID4 + 128 * i:C_ID4 + 128 * (i + 1)] = eye
    ncc = np.zeros((128, 128), np.float32)
    ncc[:64, 64:] = -1.0e30
    c32[:, C_NEGCC:C_NEGCC + 128] = ncc
    c32[:, C_POW2:C_POW2 + 32] = (0.5 ** np.arange(1, 33, dtype=np.float64)).astype(np.float32)[None, :]
    c32[:, C_ONES:C_ONES + 128] = 1.0
    c32[:, C_RB15:C_RB15 + 8] = rel_bias[15, :][None, :]
    cb = np.zeros((128, NCBF), np.float32)
    jj = np.arange(128)[:, None]
    ss = np.arange(128)[None, :]
    cb[:, B_NUT:B_NUT + 128] = np.where(jj >= ss, -1.0, 0.0)
    cb[:, B_NEG1:B_NEG1 + 128] = -1.0
    cb[:, B_ONES:B_ONES + 128] = 1.0
    cb[:, B_IDB:B_IDB + 128] = eye
    cc = np.arange(896)[None, :]
    cb[:, B_MW:B_MW + 896] = np.where((cc - 384) > jj, 1.0, 0.0)
    cbf = cb.astype(ml_dtypes.bfloat16)
    sl = np.arange(128)[:, None]
    tl = np.arange(512)[None, :]
    tz = np.zeros((8, 5, 128, 512), np.float32)
    for di in range(5):
        bidx = t5_bucket_np((di - 1) * 128 + sl - tl)
        tz[:, di] = np.transpose(rel_bias[bidx], (2, 0, 1))
    return c32, cbf, tz


_CACHE = {}


def kernel(x, w_in, w_out, ln_g, ln_b, rel_bias):
    x = np.asarray(x, np.float32)
    w_in = np.ascontiguousarray(np.asarray(w_in, np.float32))
    w_out = np.ascontiguousarray(np.asarray(w_out, np.float32))
    ln_g = np.ascontiguousarray(np.asarray(ln_g, np.float32))
    ln_b = np.ascontiguousarray(np.asarray(ln_b, np.float32))
    rel_bias = np.asarray(rel_bias, np.float32)
    c32, cbf, tz = make_consts(rel_bias)
    if "nc" not in _CACHE:
        _CACHE["nc"] = build()
    nc = _CACHE["nc"]
    in_maps = []
    for c in range(8):
        b = c % 4
        in_maps.append({"x": np.ascontiguousarray(x[b]), "w_in": w_in, "w_out": w_out, "ln_g": ln_g, "ln_b": ln_b,
                        "tz": tz, "c32": c32, "cbf": cbf})
    res = run_bass_kernel_spmd(nc, in_maps, core_ids=list(range(8)))
    out = np.stack([np.asarray(res.results[b]["out"], np.float32) for b in range(4)], axis=0)
    return out
```

```python
import math
import numpy as np
import ml_dtypes
import concourse.bass as bass
import concourse.mybir as mybir
from concourse.bass_utils import run_bass_kernel_spmd

F32 = mybir.dt.float32
BF16 = mybir.dt.bfloat16
U8 = mybir.dt.uint8
AF = mybir.ActivationFunctionType
ALU = mybir.AluOpType
AX = mybir.AxisListType

S = 4096
D = 1024
NIN = 4680
NL = 2
QA, KA, VA, GA, QB, KB, VB, GB, QI, KI, WI = 0, 512, 1024, 1536, 2048, 2560, 3072, 3584, 4096, 4608, 4672
ALPHA = (2.0 * NL) ** 0.25
LN_EPS = 1e-5
TOPK = 256
NIT = 16
SB_WIN = 1
F_QA, F_KA, F_SGA, F_QB, F_KB, F_SGB, F_QI, F_KI = 0, 4, 8, 12, 16, 20, 24, 28
NFCH = 29
FCH = ([(QA + 128 * i, "q") for i in range(4)] + [(KA + 128 * i, "c") for i in range(4)]
       + [(GA + 128 * i, "g") for i in range(4)] + [(QB + 128 * i, "q") for i in range(4)]
       + [(KB + 128 * i, "c") for i in range(4)] + [(GB + 128 * i, "g") for i in range(4)]
       + [(QI + 128 * i, "c") for i in range(4)] + [(-1, "c")])
C_ID4, C_NEGCC, C_POW2, C_ONES, C_RB15 = 0, 512, 640, 672, 800
NC32 = 808
B_NUT, B_NEG1, B_ONES, B_IDB, B_MW = 0, 128, 256, 384, 512
NCBF = 512 + 896


class Sched:
    def __init__(self):
        self.ops = []
        self.last_w = {}
        self.readers = {}
        self.bar_idx = None
        self.defer = None

    def begin_defer(self):
        self.defer = []

    def end_defer(self):
        d = self.defer
        self.defer = None
        return d

    def flush(self, lst):
        for a in lst:
            self.add(*a)

    def add(self, eng, fn, reads=(), writes=(), dma=False):
        if self.defer is not None:
            self.defer.append((eng, fn, tuple(reads), tuple(writes), dma))
            return None
        idx = len(self.ops)
        deps = set()
        if self.bar_idx is not None:
            deps.add(self.bar_idx)
        for k in reads:
            if k in self.last_w:
                deps.add(self.last_w[k])
        for k in writes:
            if k in self.last_w:
                deps.add(self.last_w[k])
            rd = self.readers.get(k)
            if rd:
                deps.update(rd[0].values())
                deps.update(rd[1])
        self.ops.append(dict(eng=eng, fn=fn, deps=deps, dma=dma, sig=dma))
        for k in reads:
            rd = self.readers.setdefault(k, ({}, []))
            if dma:
                rd[1].append(idx)
            else:
                rd[0][eng] = idx
        for k in writes:
            self.last_w[k] = idx
            self.readers[k] = ({}, [])
        return idx

    def barrier(self):
        keys = list(set(self.last_w.keys()) | set(self.readers.keys()))
        bt = self.bar_tile
        self.bar_idx = self.add("dve", lambda e: e.memset(bt, 0.0), reads=(), writes=keys + ["__bar__"])
        self.last_w = {}
        self.readers = {}

    def emit(self, nc, ndma=8):
        ops = self.ops
        NDMA = {"sp": ndma, "pool": 4, "act": 4}
        for i, o in enumerate(ops):
            nd = set()
            for d in o["deps"]:
                if ops[d]["eng"] == "pe" and o["eng"] == "pe" and not ops[d]["dma"] and not o["dma"]:
                    continue
                nd.add(d)
            o["deps"] = nd
        slot_last = {}
        dcount = {"sp": 0, "pool": 0, "act": 0}
        for i, o in enumerate(ops):
            if o["dma"]:
                q = o["eng"]
                slot = dcount[q] % NDMA[q]
                dcount[q] += 1
                o["slot"] = slot
                prev = slot_last.get((q, slot))
                if prev is not None:
                    o["deps"].add(prev)
                slot_last[(q, slot)] = i
        for o in ops:
            for d in o["deps"]:
                ops[d]["sig"] = True
        csem = {e: nc.alloc_semaphore("sem_" + e) for e in ("pe", "act", "dve", "pool")}
        dsem = {q: [nc.alloc_semaphore("dsem_%s%d" % (q, i)) for i in range(NDMA[q])] for q in NDMA}
        ccount = {e: 0 for e in csem}
        duse = {(q, s): 0 for q in NDMA for s in range(NDMA[q])}
        for o in ops:
            if o["dma"]:
                key = (o["eng"], o["slot"])
                duse[key] += 1
                o["tick"] = (dsem[o["eng"]][o["slot"]], 16 * duse[key])
            elif o["sig"]:
                ccount[o["eng"]] += 1
                o["tick"] = (csem[o["eng"]], ccount[o["eng"]])
        streams = {e: [] for e in ("pe", "act", "dve", "pool", "sp")}
        for i, o in enumerate(ops):
            streams[o["eng"]].append(i)

        def run(engname, eng):
            waited = {}
            for i in streams[engname]:
                o = ops[i]
                for d in sorted(o["deps"]):
                    sem, val = ops[d]["tick"]
                    if waited.get(sem.num, 0) < val:
                        eng.wait_ge(sem, val)
                        waited[sem.num] = val
                ins = o["fn"](eng)
                if o["sig"]:
                    sem, val = o["tick"]
                    ins.then_inc(sem, 16 if o["dma"] else 1)
            return waited

        with nc.Block() as block:
            @block.tensor
            def _(e):
                run("pe", e)

            @block.scalar
            def _(e):
                run("act", e)

            @block.vector
            def _(e):
                run("dve", e)

            @block.gpsimd
            def _(e):
                run("pool", e)

            @block.sync
            def _(e):
                run("sp", e)
                for q in NDMA:
                    for s in range(NDMA[q]):
                        if duse[(q, s)]:
                            e.wait_ge(dsem[q][s], 16 * duse[(q, s)])


def emit_pipelined(sc, stages, skew=1, collect=False):
    n = len(stages)
    ns = len(stages[0]) if n else 0
    out = []
    for step in range(n + (ns - 1) * skew):
        for k in range(ns):
            i = step - k * skew
            if 0 <= i < n:
                if collect:
                    out.extend(stages[i][k])
                else:
                    sc.flush(stages[i][k])
    return out


def interleave(sc, B, A):
    nb, na = len(B), len(A)
    if nb == 0:
        sc.flush(A)
        return
    for i in range(nb):
        sc.flush([B[i]])
        sc.flush(A[i * na // nb:(i + 1) * na // nb])


class Arena:
    def __init__(self, nc, nbytes):
        self.t = nc.alloc_sbuf_tensor("arena", [128, nbytes], U8)
        self.n = nbytes
        self.off = 0

    def reset(self):
        self.off = 0

    def tile(self, shape, dtype):
        esz = 4 if dtype == F32 else (1 if dtype == U8 else 2)
        free = 1
        for s in shape[1:]:
            free *= s
        nb = (free * esz + 63) // 64 * 64
        assert self.off + nb <= self.n, ("arena overflow", self.off, nb)
        v = self.t[:, self.off:self.off + free * esz]
        if dtype != U8:
            v = v.bitcast(dtype)
        self.off += nb
        if len(shape) == 3:
            v = v.rearrange("p (a b) -> p a b", b=shape[2])
        return v


def build(nlayers=NL, phases=("proj", "sb", "dsa", "out"), dbg=False, sb_win=SB_WIN, nit=NIT, qbs=range(8)):
    nc = bass.Bass("TRN2", target_bir_lowering=False)

    def dram(name, shape, dtype, kind="Internal"):
        return nc.dram_tensor(name, shape, dtype, kind=kind).ap()

    x_d = dram("x", [S, D], F32, "ExternalInput")
    win_d = dram("w_in", [NL, D, NIN], F32, "ExternalInput")
    wout_d = dram("w_out", [NL, D, D], F32, "ExternalInput")
    lng_d = dram("ln_g", [NL, D], F32, "ExternalInput")
    lnb_d = dram("ln_b", [NL, D], F32, "ExternalInput")
    tz_d = dram("tz", [8, 5, 128, 512], F32, "ExternalInput")
    c32_d = dram("c32", [128, NC32], F32, "ExternalInput")
    cbf_d = dram("cbf", [128, NCBF], BF16, "ExternalInput")
    out_d = dram("out", [S, D], F32, "ExternalOutput")
    sk = "ExternalOutput" if dbg else "Internal"
    featT_d = dram("featT", [NFCH * 128, S], BF16, sk)
    V_d = dram("Vs", [S, 1024], BF16, sk)
    wI_d = dram("wIs", [S, 8], F32, sk)
    mixT_d = dram("mixT", [1024, S], BF16, sk)
    h1_d = dram("h1", [S, D], F32, "Internal")

    sc = Sched()
    ar = Arena(nc, 166 * 1024)
    sc.bar_tile = nc.alloc_sbuf_tensor("bart", [128, 8], F32)[:, 0:1]
    c32 = nc.alloc_sbuf_tensor("c32s", [128, NC32], F32)
    cbf = nc.alloc_sbuf_tensor("cbfs", [128, NCBF], BF16)
    gam = nc.alloc_sbuf_tensor("gam", [128, D], F32)
    bet = nc.alloc_sbuf_tensor("bet", [128, D], F32)
    PD = [nc.alloc_psum_tensor("pd%d" % i, [128, 2, 512], F32) for i in range(4)]
    banks = [PD[i][:, j, :] for i in range(4) for j in range(2)]
    bkey = [("ps", i) for i in range(8)]

    ident = c32[:, C_ID4:C_ID4 + 128]
    ident4 = c32[:, C_ID4:C_ID4 + 512]
    negcc = c32[:, C_NEGCC:C_NEGCC + 128]
    pow2 = c32[:, C_POW2:C_POW2 + 32]
    ones32 = c32[:, C_ONES:C_ONES + 128]
    rb15 = c32[:, C_RB15:C_RB15 + 8]
    nut = cbf[:, B_NUT:B_NUT + 128]
    neg1 = cbf[:, B_NEG1:B_NEG1 + 128]
    onesb = cbf[:, B_ONES:B_ONES + 128]
    identb = cbf[:, B_IDB:B_IDB + 128]
    mw = cbf[:, B_MW:B_MW + 896]

    def dma(q, out, in_, reads, writes):
        sc.add(q, lambda e, out=out, in_=in_: e.dma_start(out=out, in_=in_), reads=reads, writes=writes, dma=True)

    dma("sp", c32[:, :], c32_d[:, :], [], ["c32"])
    dma("sp", cbf[:, :], cbf_d[:, :], [], ["cbf"])
    CST = ["c32", "cbf"]

    def act(out, in_, func, reads, writes, bias=None, scale=None, accum_out=None):
        kw = {}
        if bias is not None:
            kw["bias"] = bias
        if scale is not None:
            kw["scale"] = scale
        if accum_out is not None:
            kw["accum_out"] = accum_out
        sc.add("act", lambda e: e.activation(out=out, in_=in_, func=func, **kw), reads=reads, writes=writes)

    def mm_group(out, pairs, reads, writes):
        def fn(e):
            ins = None
            n = len(pairs)
            for i, (l, r) in enumerate(pairs):
                ins = e.matmul(out, l, r, start=(i == 0), stop=(i == n - 1))
            return ins
        sc.add("pe", fn, reads=reads, writes=writes)

    def phase_proj(L, src_d, srckey):
        ar.reset()
        wbf = ar.tile([128, 8, NIN], BF16)
        wki2 = ar.tile([128, 8, 128], BF16)
        wst = [ar.tile([128, 2340], F32) for _ in range(2)]
        xin = [ar.tile([128, 4, 1024], F32) for _ in range(2)]
        xT = [ar.tile([128, 8, 512], BF16) for _ in range(2)]
        ost = [ar.tile([128, 512], BF16) for _ in range(4)]
        vst = [ar.tile([128, 1024], BF16) for _ in range(2)]
        wist = [ar.tile([128, 8], F32) for _ in range(2)]
        h = 2340
        for kc in range(8):
            for b in range(2):
                dma("sp", wst[b][:, :], win_d[L, kc * 128:(kc + 1) * 128, b * h:(b + 1) * h], [], [("wst", b)])
                if b == 0:
                    sc.add("dve", lambda e, kc=kc: e.tensor_copy(out=wbf[:, kc, 0:h], in_=wst[0][:, :]),
                           reads=[("wst", 0)], writes=[("wbf", kc, 0)])
                else:
                    sc.add("pool", lambda e, kc=kc: e.tensor_copy(out=wbf[:, kc, h:NIN], in_=wst[1][:, :]),
                           reads=[("wst", 1)], writes=[("wbf", kc, 1)])
                    sc.add("pool", lambda e, kc=kc: e.tensor_copy(out=wki2[:, kc, 0:64], in_=wst[1][:, KI - h:KI - h + 64]),
                           reads=[("wst", 1)], writes=[("wki2", kc, 0)])
                    sc.add("pool", lambda e, kc=kc: e.tensor_copy(out=wki2[:, kc, 64:128], in_=wst[1][:, KI - h:KI - h + 64]),
                           reads=[("wst", 1)], writes=[("wki2", kc, 1)])
        WK = [("wbf", kc, i) for kc in range(8) for i in range(2)] + [("wki2", kc, i) for kc in range(8) for i in range(2)]
        bi = 0
        oi = 0
        for tb in range(8):
            xb = tb % 2
            dma("sp", xin[xb][:, :, :], src_d[tb * 512:(tb + 1) * 512, :].rearrange("(j p) d -> p j d", p=128),
                [srckey], [("xin", xb)])
            for j in range(4):
                for g in range(2):
                    bk = bi % 8
                    bi += 1

                    def fn(e, xb=xb, j=j, g=g, bk=bk):
                        ins = None
                        for i in range(4):
                            kc = g * 4 + i
                            ins = e.transpose(banks[bk][:, i * 128:(i + 1) * 128],
                                              xin[xb][:, j, kc * 128:(kc + 1) * 128], ident)
                        return ins
                    sc.add("pe", fn, reads=[("xin", xb)] + CST, writes=[bkey[bk]])
                    o = xT[xb][:, g * 4:g * 4 + 4, j * 128:(j + 1) * 128]
                    i_ = banks[bk].rearrange("p (a b) -> p a b", b=128)
                    if (j + g) % 2 == 0:
                        sc.add("dve", lambda e, o=o, i_=i_: e.tensor_copy(out=o, in_=i_),
                               reads=[bkey[bk]], writes=[("xT", xb, j, g)])
                    else:
                        act(o, i_, AF.Copy, [bkey[bk]], [("xT", xb, j, g)])
            XT = [("xT", xb, j, g) for j in range(4) for g in range(2)]
            for c in range(NFCH):
                col, kind = FCH[c]
                bk = bi % 8
                bi += 1
                if col >= 0:
                    pairs = [(wbf[:, kc, col:col + 128], xT[xb][:, kc, :]) for kc in range(8)]
                else:
                    pairs = [(wki2[:, kc, :], xT[xb][:, kc, :]) for kc in range(8)]
                mm_group(banks[bk], pairs, XT + WK, [bkey[bk]])
                os_ = oi % 4
                oi += 1
                if kind == "g":
                    act(ost[os_][:, :], banks[bk], AF.Silu, [bkey[bk]], [("ost", os_)])
                elif kind == "q":
                    sc.add("dve", lambda e, os_=os_, bk=bk: e.tensor_scalar(
                        out=ost[os_][:, :], in0=banks[bk], scalar1=0.125, scalar2=None, op0=ALU.mult),
                        reads=[bkey[bk]], writes=[("ost", os_)])
                else:
                    if c % 2 == 0:
                        sc.add("dve", lambda e, os_=os_, bk=bk: e.tensor_copy(out=ost[os_][:, :], in_=banks[bk]),
                               reads=[bkey[bk]], writes=[("ost", os_)])
                    else:
                        act(ost[os_][:, :], banks[bk], AF.Copy, [bkey[bk]], [("ost", os_)])
                dma("pool", featT_d[c * 128:(c + 1) * 128, tb * 512:(tb + 1) * 512], ost[os_][:, :],
                    [("ost", os_)], [("featT", c, tb)])
            for j in range(4):
                vs = (tb * 4 + j) % 2
                for hf in range(2):
                    bk = bi % 8
                    bi += 1
                    c0 = VA if hf == 0 else VB
                    pairs = [(xT[xb][:, kc, j * 128:(j + 1) * 128], wbf[:, kc, c0:c0 + 512]) for kc in range(8)]
                    mm_group(banks[bk], pairs, XT + WK, [bkey[bk]])
                    o = vst[vs][:, hf * 512:(hf + 1) * 512]
                    if hf == 0:
                        sc.add("dve", lambda e, o=o, bk=bk: e.tensor_copy(out=o, in_=banks[bk]),
                               reads=[bkey[bk]], writes=[("vst", vs, hf)])
                    else:
                        act(o, banks[bk], AF.Copy, [bkey[bk]], [("vst", vs, hf)])
                r0 = tb * 512 + j * 128
                dma("pool", V_d[r0:r0 + 128, :], vst[vs][:, :], [("vst", vs, 0), ("vst", vs, 1)], [("V", tb)])
                bk = bi % 8
                bi += 1
                pairs = [(xT[xb][:, kc, j * 128:(j + 1) * 128], wbf[:, kc, WI:WI + 8]) for kc in range(8)]
                mm_group(banks[bk][:, 0:8], pairs, XT + WK, [bkey[bk]])
                sc.add("dve", lambda e, vs=vs, bk=bk: e.tensor_copy(out=wist[vs][:, :], in_=banks[bk][:, 0:8]),
                       reads=[bkey[bk]], writes=[("wist", vs)])
                dma("pool", wI_d[r0:r0 + 128, :], wist[vs][:, :], [("wist", vs)], [("wI", tb)])

    def featkeys(c0, nch, tb_hi):
        return [("featT", c0 + i, t) for i in range(nch) for t in range(tb_hi + 1)]

    def phase_sb(L):
        ar.reset()
        qT = [ar.tile([128, 512], BF16) for _ in range(2)]
        sg = [ar.tile([128, 512], BF16) for _ in range(2)]
        kT = [ar.tile([128, S], BF16) for _ in range(2)]
        vP = [ar.tile([128, 32, 128], BF16) for _ in range(2)]
        ee = [ar.tile([128, 2, 512], F32) for _ in range(2)]
        sp = [ar.tile([128, 2, 512], BF16) for _ in range(2)]
        Sb = ar.tile([128, 2, 512], BF16)
        Aa = [ar.tile([128, 2, 512], BF16) for _ in range(2)]
        mst = [ar.tile([128, 512], BF16) for _ in range(2)]
        it = 0
        tcount = 0
        stages = []
        for q in qbs:
            for p in range(4):
                s = it % 2
                it += 1
                nk = 512 * (q + 1)
                sc.begin_defer()
                kb_hi = 4 * q + 3
                kb_lo = 0 if sb_win is None else max(0, 4 * q - sb_win)
                k0 = kb_lo * 128
                dma("sp", qT[s][:, :], featT_d[(F_QA + p) * 128:(F_QA + p + 1) * 128, q * 512:(q + 1) * 512],
                    [("featT", F_QA + p, q)], [("qT", s)])
                dma("sp", sg[s][:, :], featT_d[(F_SGA + p) * 128:(F_SGA + p + 1) * 128, q * 512:(q + 1) * 512],
                    [("featT", F_SGA + p, q)], [("sg", s)])
                dma("sp", kT[s][:, k0:nk], featT_d[(F_KA + p) * 128:(F_KA + p + 1) * 128, k0:nk],
                    featkeys(F_KA + p, 1, q), [("kT", s)])
                dma("sp", vP[s][:, kb_lo:kb_hi + 1, :],
                    V_d[k0:nk, p * 128:(p + 1) * 128].rearrange("(kb p) c -> p kb c", p=128),
                    [("V", t) for t in range(q + 1)], [("vP", s)])
                OB = PD[3]
                first = True
                for kb in range(kb_hi, kb_lo - 1, -1):
                    zi = tcount % 2
                    tcount += 1
                    Z = PD[zi]
                    T = PD[2]
                    diag = kb >= 4 * q
                    kk = kb - 4 * q
                    last = kb == kb_lo
                    if not first:
                        sc.begin_defer()

                    def fz(e, Z=Z, s=s, kb=kb):
                        ins = None
                        for hh in range(2):
                            pr = slice(64 * hh, 64 * hh + 64)
                            ins = e.matmul(Z[:, hh, :], kT[s][pr, kb * 128:(kb + 1) * 128], qT[s][pr, :],
                                           start=True, stop=True)
                        return ins
                    sc.add("pe", fz, reads=[("kT", s), ("qT", s)], writes=[("pd", zi)])
                    act(ee[zi][:, :, :], Z[:, :, :], AF.Exp, [("pd", zi)], [("ee", zi)])
                    act(sp[zi][:, :, :], ee[zi][:, :, :], AF.Ln, [("ee", zi)], [("sp", zi)], bias=1.0)
                    if diag:
                        msk = mw[:, 384 - 128 * kk:384 - 128 * kk + 512]
                        for hh in range(2):
                            sc.add("dve", lambda e, zi=zi, hh=hh, msk=msk: e.tensor_tensor(
                                out=sp[zi][:, hh, :], in0=sp[zi][:, hh, :], in1=msk, op=ALU.mult),
                                reads=[("sp", zi)] + CST, writes=[("sp", zi)])

                    st1 = sc.end_defer()
                    sc.begin_defer()

                    def ft(e, T=T, s=s, kb=kb, zi=zi, first=first):
                        ins = None
                        for hh in range(2):
                            pr = slice(64 * hh, 64 * hh + 64)
                            e.matmul(T[:, hh, :], nut, sp[zi][:, hh, :], start=True, stop=False)
                            if not first:
                                e.matmul(T[:, hh, :], neg1, Sb[:, hh, :], start=False, stop=False)
                            ins = e.matmul(T[:, hh, :], kT[s][pr, kb * 128:(kb + 1) * 128], qT[s][pr, :],
                                           start=False, stop=True)
                        return ins
                    sc.add("pe", ft, reads=[("sp", zi), ("Sb",), ("kT", s), ("qT", s)] + CST, writes=[("pd", 2)])
                    act(Aa[zi][:, :, :], T[:, :, :], AF.Exp, [("pd", 2)], [("Aa", zi)])
                    if diag:
                        for hh in range(2):
                            sc.add("dve", lambda e, zi=zi, hh=hh, msk=msk: e.tensor_tensor(
                                out=Aa[zi][:, hh, :], in0=Aa[zi][:, hh, :], in1=msk, op=ALU.mult),
                                reads=[("Aa", zi)] + CST, writes=[("Aa", zi)])

                    if not last:
                        if first:
                            sc.add("dve", lambda e, zi=zi: e.tensor_copy(out=Sb[:, :, :], in_=sp[zi][:, :, :]),
                                   reads=[("sp", zi)], writes=[("Sb",)])
                        else:
                            sc.add("dve", lambda e, zi=zi: e.tensor_tensor(
                                out=Sb[:, :, :], in0=Sb[:, :, :], in1=sp[zi][:, :, :], op=ALU.add),
                                reads=[("sp", zi), ("Sb",)], writes=[("Sb",)])
                    st2 = sc.end_defer()
                    sc.begin_defer()

                    def fo(e, s=s, kb=kb, zi=zi, first=first, last=last):
                        ins = None
                        for hh in range(2):
                            ins = e.matmul(OB[:, hh, :], vP[s][:, kb, :], Aa[zi][:, hh, :], start=first, stop=last)
                        return ins
                    sc.add("pe", fo, reads=[("vP", s), ("Aa", zi)], writes=[("pd", 3)])
                    first = False
                    if last:
                        for hh in range(2):
                            pr = slice(64 * hh, 64 * hh + 64)
                            sc.add("dve", lambda e, s=s, hh=hh, pr=pr: e.tensor_tensor(
                                out=mst[s][pr, :], in0=OB[pr, hh, :], in1=sg[s][pr, :], op=ALU.mult),
                                reads=[("pd", 3), ("sg", s)], writes=[("mst", s, hh)])
                        dma("pool", mixT_d[p * 128:(p + 1) * 128, q * 512:(q + 1) * 512], mst[s][:, :],
                            [("mst", s, 0), ("mst", s, 1)], [("mixT", p, q)])
                    st3 = sc.end_defer()
                    stages.append((st1, st2, st3))
        emit_pipelined(sc, stages)

    def phase_dsa(L):
        ar.reset()
        qIT = ar.tile([128, 4, 512], BF16)
        kI2 = ar.tile([128, S], BF16)
        wIt = ar.tile([128, 4, 8], F32)
        dg = ar.tile([128, 8, 128], BF16)
        Rr = [ar.tile([128, 2, 512], BF16) for _ in range(4)]
        scoreb = [ar.tile([128, S], F32) for _ in range(2)]
        junk = ar.tile([128, S], U8)
        junk2 = ar.tile([128, 2304], BF16)
        maskT = ar.tile([128, 32, 512], BF16)
        small = ar.tile([128, 64], F32)
        wt = ar.tile([128, 32], F32)
        dthr = ar.tile([128, 512], F32)
        thrB = ar.tile([128, 512], F32)
        qT = [ar.tile([128, 512], BF16) for _ in range(2)]
        sg = [ar.tile([128, 512], BF16) for _ in range(2)]
        kT = [ar.tile([128, S], BF16) for _ in range(2)]
        vP = [ar.tile([128, 32, 128], BF16) for _ in range(2)]
        tzb = [ar.tile([128, 512], F32) for _ in range(4)]
        tmp = [ar.tile([128, 512], F32) for _ in range(2)]
        Ee = [ar.tile([128, 2, 512], BF16) for _ in range(3)]
        Pm = [ar.tile([128, 2, 512], BF16) for _ in range(3)]
        rd = ar.tile([128, 512], F32)
        mo = ar.tile([128, 512], F32)
        mst = [ar.tile([128, 512], BF16) for _ in range(2)]
        mx, mn, w0, lo0, mid, cnt, tt, thr, sgn, uu = [small[:, i:i + 1] for i in range(10)]
        it = 0
        tcount = 0
        tzc = 0
        for q in qbs:
            nkq = 512 * (q + 1)
            dma("sp", qIT[:, :, :],
                featT_d[F_QI * 128:(F_QI + 4) * 128, q * 512:(q + 1) * 512].rearrange("(c p) t -> p c t", p=128),
                [("featT", F_QI + i, q) for i in range(4)], [("qIT",)])
            dma("sp", kI2[:, 0:nkq], featT_d[F_KI * 128:(F_KI + 1) * 128, 0:nkq], featkeys(F_KI, 1, q), [("kI2",)])
            dma("sp", wIt[:, :, :], wI_d[q * 512:(q + 1) * 512, :].rearrange("(j p) h -> p j h", p=128),
                [("wI", q)], [("wIt",)])
            for j in range(3):
                sc.add("pool", lambda e, j=j, q=q: e.memset(maskT[:, 4 * q + j + 1:4 * q + 4, j * 128:(j + 1) * 128], 0.0),
                       reads=[], writes=[("maskT", j)])
            Aops = []
            Bops = []
            for j in range(4):
                n = 512 * q + 128 * (j + 1)
                score = scoreb[j % 2]
                sck = ("score%d" % (j % 2))
                sc.begin_defer()
                for h in range(8):
                    sc.add("pool" if h % 2 else "dve", lambda e, j=j, h=h: e.tensor_scalar(
                        out=dg[:, h, :], in0=identb, scalar1=wIt[:, j, h:h + 1], scalar2=1.0, op0=ALU.mult, op1=ALU.mult),
                        reads=[("wIt",)] + CST, writes=[("dg", h)])
                pre = sc.end_defer()
                nch = (n + 511) // 512
                stages = []
                for c in range(nch):
                    wc = min(512, n - 512 * c)
                    ACC = PD[2][:, 0, :]
                    for pp in range(4):
                        zi = tcount % 2
                        tcount += 1
                        X = PD[zi]
                        sc.begin_defer()

                        def fx(e, X=X, pp=pp, j=j, c=c, wc=wc):
                            ins = None
                            for hh in range(2):
                                pr = slice(64 * hh, 64 * hh + 64)
                                ins = e.matmul(X[:, hh, 0:wc], qIT[pr, pp, j * 128:(j + 1) * 128],
                                               kI2[pr, c * 512:c * 512 + wc], start=True, stop=True)
                            return ins
                        sc.add("pe", fx, reads=[("qIT",), ("kI2",)], writes=[("pd", zi)])
                        ri = tcount % 4
                        if pp % 2 == 0:
                            act(Rr[ri][:, :, 0:wc], X[:, :, 0:wc], AF.Relu, [("pd", zi)], [("Rr", ri)])
                        else:
                            sc.add("dve", lambda e, ri=ri, X=X, wc=wc: e.tensor_scalar(
                                out=Rr[ri][:, :, 0:wc], in0=X[:, :, 0:wc], scalar1=0.0, scalar2=None, op0=ALU.max),
                                reads=[("pd", zi)], writes=[("Rr", ri)])
                        st1 = sc.end_defer()
                        sc.begin_defer()

                        def fa(e, pp=pp, ri=ri, wc=wc, ACC=ACC):
                            ins = None
                            for hh in range(2):
                                ins = e.matmul(ACC[:, 0:wc], dg[:, 2 * pp + hh, :], Rr[ri][:, hh, 0:wc],
                                               start=(pp == 0 and hh == 0), stop=(pp == 3 and hh == 1))
                            return ins
                        sc.add("pe", fa, reads=[("Rr", ri), ("dg", 2 * pp), ("dg", 2 * pp + 1)], writes=[("pd", 2)])
                        if pp == 3:
                            act(score[:, c * 512:c * 512 + wc], ACC[:, 0:wc], AF.Copy, [("pd", 2)], [(sck, c)])
                        st2 = sc.end_defer()
                        stages.append((st1, st2))
                Aops.append(pre + emit_pipelined(sc, stages, skew=2, collect=True))
                sc.begin_defer()
                SCK = [(sck, c) for c in range(nch)]
                sc.add("dve", lambda e, n=n, score=score: e.tensor_tensor(out=score[:, n - 128:n], in0=score[:, n - 128:n],
                                                                          in1=negcc, op=ALU.add),
                       reads=SCK + CST, writes=[(sck, nch - 1)])
                if n <= 256:
                    sc.add("dve", lambda e: e.memset(thr, -1.0e29), reads=[], writes=[("thr",)])
                else:
                    n1 = max(128, (n * 8 // 16) // 128 * 128)
                    nA = n - n1
                    sc.add("dve", lambda e, n=n, score=score: e.tensor_reduce(out=mx, in_=score[:, 0:n], axis=AX.X, op=ALU.max),
                           reads=SCK, writes=[("mx",)])
                    sc.add("dve", lambda e, n=n, score=score: e.tensor_reduce(out=mn, in_=score[:, 0:n - 64], axis=AX.X, op=ALU.min),
                           reads=SCK, writes=[("mn",)])
                    sc.add("dve", lambda e: e.scalar_tensor_tensor(out=w0, in0=mx, scalar=1.0, in1=mn,
                                                                   op0=ALU.add, op1=ALU.subtract),
                           reads=[("mx",), ("mn",)], writes=[("w0",)])
                    sc.add("dve", lambda e: e.tensor_scalar(out=wt[:, 0:nit + 1], in0=pow2[:, 0:nit + 1],
                                                            scalar1=w0, scalar2=None, op0=ALU.mult),
                           reads=[("w0",)] + CST, writes=[("wt",)])
                    sc.add("dve", lambda e: e.scalar_tensor_tensor(out=mid, in0=mn, scalar=-1.0, in1=wt[:, 0:1],
                                                                   op0=ALU.add, op1=ALU.add),
                           reads=[("mn",), ("wt",)], writes=[("mid",)])
                    for k in range(nit):
                        act(junk2[:, 0:nA], score[:, n1:n], AF.Sign, SCK + [("mid",)], [("sgn",), ("junk2",)],
                            bias=mid, scale=-1.0, accum_out=sgn)
                        sc.add("dve", lambda e, n1=n1, score=score: e.tensor_scalar(
                            out=junk[:, 0:n1], in0=score[:, 0:n1], scalar1=mid, scalar2=None,
                            op0=ALU.is_gt, op1=ALU.add, accum_out=cnt),
                            reads=SCK + [("mid",)], writes=[("cnt",), ("junk",)])
                        sc.add("dve", lambda e: e.scalar_tensor_tensor(out=uu, in0=sgn, scalar=-0.5, in1=cnt,
                                                                       op0=ALU.mult, op1=ALU.add),
                               reads=[("sgn",), ("cnt",)], writes=[("uu",)])
                        sc.add("dve", lambda e, nA=nA: e.tensor_scalar(
                            out=tt, in0=uu, scalar1=float(TOPK) - 0.5 - 0.5 * nA, scalar2=0.5,
                            op0=ALU.is_ge, op1=ALU.subtract),
                            reads=[("uu",)], writes=[("tt",)])
                        if k < nit - 1:
                            sc.add("dve", lambda e, k=k: e.scalar_tensor_tensor(
                                out=mid, in0=tt, scalar=wt[:, k:k + 1], in1=mid, op0=ALU.mult, op1=ALU.add),
                                reads=[("tt",), ("wt",), ("mid",)], writes=[("mid",)])
                        else:
                            sc.add("dve", lambda e: e.tensor_scalar(out=tt, in0=tt, scalar1=-0.5, scalar2=None, op0=ALU.add),
                                   reads=[("tt",)], writes=[("tt",)])
                            sc.add("dve", lambda e, k=k: e.scalar_tensor_tensor(
                                out=thr, in0=tt, scalar=wt[:, k:k + 1], in1=mid, op0=ALU.mult, op1=ALU.add),
                                reads=[("tt",), ("wt",), ("mid",)], writes=[("thr",)])
                sc.add("dve", lambda e: e.tensor_scalar(out=dthr[:, :], in0=ident4, scalar1=thr, scalar2=None, op0=ALU.mult),
                       reads=[("thr",)] + CST, writes=[("dthr",)])
                mm_group(PD[3][:, 0, :], [(ones32, dthr[:, :])], [("dthr",)] + CST, [("pd", 3)])
                act(thrB[:, :], PD[3][:, 0, :], AF.Copy, [("pd", 3)], [("thrB",)])
                nkb = n // 128
                for g0 in range(0, nkb, 4):
                    gn = min(4, nkb - g0)
                    TR = PD[3][:, 1, :]

                    def ftr(e, g0=g0, gn=gn, TR=TR, score=score):
                        ins = None
                        for i in range(gn):
                            ins = e.transpose(TR[:, i * 128:(i + 1) * 128], score[:, (g0 + i) * 128:(g0 + i + 1) * 128], ident)
                        return ins
                    sc.add("pe", ftr, reads=SCK + CST, writes=[("pd", 3, 1)])
                    sc.add("dve", lambda e, g0=g0, gn=gn, TR=TR, j=j: e.tensor_tensor(
                        out=maskT[:, g0:g0 + gn, j * 128:(j + 1) * 128],
                        in0=TR[:, 0:gn * 128].rearrange("p (a b) -> p a b", b=128),
                        in1=thrB[:, 0:gn * 128].rearrange("p (a b) -> p a b", b=128), op=ALU.is_gt),
                        reads=[("pd", 3, 1), ("thrB",)], writes=[("maskT", j)])
                Bops.append(sc.end_defer())
            sc.flush(Aops[0])
            for j in range(4):
                interleave(sc, Bops[j], Aops[j + 1] if j < 3 else [])
            MK = [("maskT", j) for j in range(4)]
            stages = []
            for p in range(4):
                s = it % 2
                it += 1
                kb_hi = 4 * q + 3
                sc.begin_defer()
                dma("sp", qT[s][:, :], featT_d[(F_QB + p) * 128:(F_QB + p + 1) * 128, q * 512:(q + 1) * 512],
                    [("featT", F_QB + p, q)], [("qT", s)])
                dma("sp", sg[s][:, :], featT_d[(F_SGB + p) * 128:(F_SGB + p + 1) * 128, q * 512:(q + 1) * 512],
                    [("featT", F_SGB + p, q)], [("sg", s)])
                dma("sp", kT[s][:, 0:nkq], featT_d[(F_KB + p) * 128:(F_KB + p + 1) * 128, 0:nkq],
                    featkeys(F_KB + p, 1, q), [("kT", s)])
                dma("sp", vP[s][:, 0:kb_hi + 1, :],
                    V_d[0:nkq, 512 + p * 128:512 + (p + 1) * 128].rearrange("(kb p) c -> p kb c", p=128),
                    [("V", t) for t in range(q + 1)], [("vP", s)])
                OB = PD[2]
                DN = PD[3]
                for kb in range(0, kb_hi + 1):
                    zi = tcount % 2
                    e3 = tcount % 3
                    tcount += 1
                    Z = PD[zi]
                    first = kb == 0
                    last = kb == kb_hi
                    near = kb >= 4 * q - 1
                    di = kb - (4 * q - 1)
                    if not first:
                        sc.begin_defer()

                    def fz(e, Z=Z, s=s, kb=kb):
                        ins = None
                        for hh in range(2):
                            pr = slice(64 * hh, 64 * hh + 64)
                            ins = e.matmul(Z[:, hh, :], kT[s][pr, kb * 128:(kb + 1) * 128], qT[s][pr, :],
                                           start=True, stop=True)
                        return ins
                    sc.add("pe", fz, reads=[("kT", s), ("qT", s)], writes=[("pd", zi)])
                    for hh in range(2):
                        h = 2 * p + hh
                        if near:
                            tzi = tzc % 4
                            tzc += 1
                            dma("sp", tzb[tzi][:, :], tz_d[h, di, :, :], [], [("tzb", tzi)])
                            sc.add("dve", lambda e, Z=Z, hh=hh, tzi=tzi: e.tensor_tensor(
                                out=tmp[hh][:, :], in0=Z[:, hh, :], in1=tzb[tzi][:, :], op=ALU.add),
                                reads=[("pd", zi), ("tzb", tzi)], writes=[("tmp", hh)])
                            act(Ee[e3][:, hh, :], tmp[hh][:, :], AF.Exp, [("tmp", hh)], [("Ee", e3, hh)])
                        else:
                            act(Ee[e3][:, hh, :], Z[:, hh, :], AF.Exp, [("pd", zi)] + CST, [("Ee", e3, hh)],
                                bias=rb15[:, h:h + 1])
                        sc.add("dve", lambda e, e3=e3, hh=hh, kb=kb: e.tensor_tensor(
                            out=Pm[e3][:, hh, :], in0=Ee[e3][:, hh, :], in1=maskT[:, kb, :], op=ALU.mult),
                            reads=[("Ee", e3, hh)] + MK, writes=[("Pm", e3, hh)])
                    st1 = sc.end_defer()
                    sc.begin_defer()

                    def fo(e, s=s, kb=kb, e3=e3, first=first, last=last):
                        ins = None
                        for hh in range(2):
                            e.matmul(OB[:, hh, :], vP[s][:, kb, :], Pm[e3][:, hh, :], start=first, stop=last)
                            ins = e.matmul(DN[:, hh, :], onesb, Pm[e3][:, hh, :], start=first, stop=last)
                        return ins
                    sc.add("pe", fo, reads=[("vP", s), ("Pm", e3, 0), ("Pm", e3, 1)] + CST, writes=[("pd", 2), ("pd", 3)])
                    if last:
                        for hh in range(2):
                            pr = slice(64 * hh, 64 * hh + 64)
                            sc.add("dve", lambda e, hh=hh, pr=pr: e.reciprocal(out=rd[pr, :], in_=DN[pr, hh, :]),
                                   reads=[("pd", 3)], writes=[("rd", hh)])
                            sc.add("dve", lambda e, hh=hh, pr=pr: e.tensor_tensor(out=mo[pr, :], in0=OB[pr, hh, :],
                                                                                  in1=rd[pr, :], op=ALU.mult),
                                   reads=[("pd", 2), ("rd", hh)], writes=[("mo", hh)])
                            sc.add("dve", lambda e, hh=hh, pr=pr, s=s: e.tensor_tensor(out=mst[s][pr, :], in0=mo[pr, :],
                                                                                       in1=sg[s][pr, :], op=ALU.mult),
                                   reads=[("mo", hh), ("sg", s)], writes=[("mst", s, hh)])
                        dma("pool", mixT_d[512 + p * 128:512 + (p + 1) * 128, q * 512:(q + 1) * 512], mst[s][:, :],
                            [("mst", s, 0), ("mst", s, 1)], [("mixT", 4 + p, q)])
                    st2 = sc.end_defer()
                    stages.append((st1, st2))
            emit_pipelined(sc, stages, skew=2)

    def phase_out(L, src_d, srckey, dst_d, dstkey):
        ar.reset()
        wo = ar.tile([128, 8, 1024], BF16)
        wst = [ar.tile([128, 1024], F32) for _ in range(2)]
        mxg = [ar.tile([128, 8, 512], BF16) for _ in range(2)]
        xr = [ar.tile([128, 4, 1024], F32) for _ in range(2)]
        rr = [ar.tile([128, 1024], F32) for _ in range(2)]
        jk = ar.tile([128, 1024], F32)
        jk2 = ar.tile([128, 1024], F32)
        xn = [ar.tile([128, 1024], F32) for _ in range(2)]
        oo = [ar.tile([128, 1024], F32) for _ in range(2)]
        st = [ar.tile([128, 16], F32) for _ in range(2)]
        dma("sp", gam[:, :], lng_d[L:L + 1, :].to_broadcast([128, D]), [], [("gam",)])
        dma("sp", bet[:, :], lnb_d[L:L + 1, :].to_broadcast([128, D]), [], [("bet",)])
        for kc in range(8):
            b = kc % 2
            dma("sp", wst[b][:, :], wout_d[L, kc * 128:(kc + 1) * 128, :], [], [("wst", b)])
            sc.add("dve", lambda e, b=b, kc=kc: e.tensor_copy(out=wo[:, kc, :], in_=wst[b][:, :]),
                   reads=[("wst", b)], writes=[("wo", kc)])
        WK = [("wo", kc) for kc in range(8)]
        bi = 0
        ti = 0
        for tb in qbs:
            g = tb % 2
            dma("sp", mxg[g][:, :, :], mixT_d[:, tb * 512:(tb + 1) * 512].rearrange("(c p) t -> p c t", p=128),
                [("mixT", c, tb) for c in range(8)], [("mxg", g)])
            dma("sp", xr[g][:, :, :], src_d[tb * 512:(tb + 1) * 512, :].rearrange("(j p) d -> p j d", p=128),
                [srckey], [("xr", g)])
            for j in range(4):
                r = ti % 2
                ti += 1
                for nh in range(2):
                    bk = bi % 8
                    bi += 1
                    pairs = [(mxg[g][:, mc, j * 128:(j + 1) * 128], wo[:, mc, nh * 512:(nh + 1) * 512]) for mc in range(8)]
                    mm_group(banks[bk], pairs, [("mxg", g)] + WK, [bkey[bk]])
                    sc.add("dve", lambda e, r=r, g=g, j=j, nh=nh, bk=bk: e.scalar_tensor_tensor(
                        out=rr[r][:, nh * 512:(nh + 1) * 512], in0=xr[g][:, j, nh * 512:(nh + 1) * 512],
                        scalar=float(ALPHA), in1=banks[bk], op0=ALU.mult, op1=ALU.add),
                        reads=[("xr", g), bkey[bk]], writes=[("rr", r, nh)])
                RK = [("rr", r, 0), ("rr", r, 1)]
                sm, ssq, mean, msq, var, sd, rstd, nmr = [st[r][:, i:i + 1] for i in range(8)]
                sc.add("dve", lambda e, r=r, sm=sm: e.tensor_scalar(out=jk[:, :], in0=rr[r][:, :], scalar1=1.0, scalar2=None,
                                                                    op0=ALU.mult, op1=ALU.add, accum_out=sm),
                       reads=RK, writes=[("st", r, 0), ("jk",)])
                act(jk2[:, :], rr[r][:, :], AF.Square, RK, [("st", r, 1), ("jk2",)], accum_out=ssq)
                sc.add("dve", lambda e, sm=sm, mean=mean: e.tensor_scalar(out=mean, in0=sm, scalar1=1.0 / D, scalar2=None,
                                                                          op0=ALU.mult),
                       reads=[("st", r, 0)], writes=[("st", r, 2)])
                sc.add("dve", lambda e, mean=mean, msq=msq: e.tensor_tensor(out=msq, in0=mean, in1=mean, op=ALU.mult),
                       reads=[("st", r, 2)], writes=[("st", r, 3)])
                sc.add("dve", lambda e, ssq=ssq, msq=msq, var=var: e.scalar_tensor_tensor(
                    out=var, in0=ssq, scalar=1.0 / D, in1=msq, op0=ALU.mult, op1=ALU.subtract),
                    reads=[("st", r, 1), ("st", r, 3)], writes=[("st", r, 4)])
                sc.add("dve", lambda e, var=var: e.tensor_scalar(out=var, in0=var, scalar1=LN_EPS, scalar2=None, op0=ALU.add),
                       reads=[("st", r, 4)], writes=[("st", r, 4)])
                act(sd, var, AF.Sqrt, [("st", r, 4)], [("st", r, 5)])
                sc.add("dve", lambda e, sd=sd, rstd=rstd: e.reciprocal(out=rstd, in_=sd),
                       reads=[("st", r, 5)], writes=[("st", r, 6)])
                sc.add("dve", lambda e, mean=mean, rstd=rstd, nmr=nmr: e.scalar_tensor_tensor(
                    out=nmr, in0=mean, scalar=-1.0, in1=rstd, op0=ALU.mult, op1=ALU.mult),
                    reads=[("st", r, 2), ("st", r, 6)], writes=[("st", r, 7)])
                act(xn[r][:, :], rr[r][:, :], AF.Identity, RK + [("st", r, 6), ("st", r, 7)], [("xn", r)],
                    bias=nmr, scale=rstd)
                sc.add("pool", lambda e, r=r: e.tensor_tensor(out=oo[r][:, :], in0=xn[r][:, :], in1=gam[:, :], op=ALU.mult),
                       reads=[("xn", r), ("gam",)], writes=[("oo", r)])
                sc.add("dve", lambda e, r=r: e.tensor_tensor(out=oo[r][:, :], in0=oo[r][:, :], in1=bet[:, :], op=ALU.add),
                       reads=[("oo", r), ("bet",)], writes=[("oo", r)])
                r0 = tb * 512 + j * 128
                dma("pool", dst_d[r0:r0 + 128, :], oo[r][:, :], [("oo", r)], [dstkey])

    for L in range(nlayers):
        src_d, srckey = (x_d, ("xsrc",)) if L == 0 else (h1_d, ("h1",))
        dst_d, dstkey = (out_d, ("outd",)) if L == nlayers - 1 else (h1_d, ("h1",))
        if "proj" in phases:
            phase_proj(L, src_d, srckey)
            sc.barrier()
        if "sb" in phases:
            phase_sb(L)
            sc.barrier()
        if "dsa" in phases:
            phase_dsa(L)
            sc.barrier()
        if "out" in phases:
            phase_out(L, src_d, srckey, dst_d, dstkey)
            sc.barrier()
    sc.emit(nc)
    return nc


def t5_bucket_np(rel):
    rel = np.asarray(rel, np.int64)
    half, me = 16, 8
    ret = np.where(rel > 0, half, 0)
    n = np.abs(rel)
    nf = np.maximum(n, 1).astype(np.float32)
    large = me + (np.log(nf / np.float32(me)) / np.float32(math.log(128 / 8)) * np.float32(half - me)).astype(np.int32)
    large = np.minimum(large, half - 1)
    return ret + np.where(n < me, n, large)


def make_consts(rel_bias):
    c32 = np.zeros((128, NC32), np.float32)
    eye = np.eye(128, dtype=np.float32)
    for i in range(4):
        c32[:, C_ID4 + 128 * i:C_ID4 + 128 * (i + 1)] = eye
    ncc = np.zeros((128, 128), np.float32)
    ncc[:64, 64:] = -1.0e30
    c32[:, C_NEGCC:C_NEGCC + 128] = ncc
    c32[:, C_POW2:C_POW2 + 32] = (0.5 ** np.arange(1, 33, dtype=np.float64)).astype(np.float32)[None, :]
    c32[:, C_ONES:C_ONES + 128] = 1.0
    c32[:, C_RB15:C_RB15 + 8] = rel_bias[15, :][None, :]
    cb = np.zeros((128, NCBF), np.float32)
    jj = np.arange(128)[:, None]
    ss = np.arange(128)[None, :]
    cb[:, B_NUT:B_NUT + 128] = np.where(jj >= ss, -1.0, 0.0)
    cb[:, B_NEG1:B_NEG1 + 128] = -1.0
    cb[:, B_ONES:B_ONES + 128] = 1.0
    cb[:, B_IDB:B_IDB + 128] = eye
    cc = np.arange(896)[None, :]
    cb[:, B_MW:B_MW + 896] = np.where((cc - 384) > jj, 1.0, 0.0)
    cbf = cb.astype(ml_dtypes.bfloat16)
    sl = np.arange(128)[:, None]
    tl = np.arange(512)[None, :]
    tz = np.zeros((8, 5, 128, 512), np.float32)
    for di in range(5):
        bidx = t5_bucket_np((di - 1) * 128 + sl - tl)
        tz[:, di] = np.transpose(rel_bias[bidx], (2, 0, 1))
    return c32, cbf, tz


_CACHE = {}


def kernel(x, w_in, w_out, ln_g, ln_b, rel_bias):
    x = np.asarray(x, np.float32)
    w_in = np.ascontiguousarray(np.asarray(w_in, np.float32))
    w_out = np.ascontiguousarray(np.asarray(w_out, np.float32))
    ln_g = np.ascontiguousarray(np.asarray(ln_g, np.float32))
    ln_b = np.ascontiguousarray(np.asarray(ln_b, np.float32))
    rel_bias = np.asarray(rel_bias, np.float32)
    c32, cbf, tz = make_consts(rel_bias)
    if "nc" not in _CACHE:
        _CACHE["nc"] = build()
    nc = _CACHE["nc"]
    in_maps = []
    for c in range(8):
        b = c % 4
        in_maps.append({"x": np.ascontiguousarray(x[b]), "w_in": w_in, "w_out": w_out, "ln_g": ln_g, "ln_b": ln_b,
                        "tz": tz, "c32": c32, "cbf": cbf})
    res = run_bass_kernel_spmd(nc, in_maps, core_ids=list(range(8)))
    out = np.stack([np.asarray(res.results[b]["out"], np.float32) for b in range(4)], axis=0)
    return out
```
